# Optimizing a Trainium2 kernel written in Bass

```python
import math
import jax
import jax.numpy as jnp
from jax import lax
import numpy as np

D_MODEL = 1024
BATCH = 1
SEQ = 16384
DEPTH = 1

GRID_W = 64
CTX_LEN = 256
NA_HEADS = 8
NA_DH = 64
NA_W = NA_HEADS * NA_DH
NA_KH = 8
NA_KW = 16
GDN_HEADS = 4
GDN_DK = 128
GDN_DV = 128
GDN_W = GDN_HEADS * GDN_DV
GDN_QKV_W = 2 * GDN_HEADS * GDN_DK + GDN_W
GDN_CHUNK = 64
CONV_K = 5
ROPE_AXIS_DIM = GDN_DK // 2
ROPE_BASE = 10000.0
EPS = 1e-6
MIX_W = NA_W + GDN_W
IN_WIDTHS = (NA_W, NA_W, NA_W, NA_W, GDN_QKV_W, GDN_W, 2 * GDN_HEADS, 2 * GDN_HEADS)
IN_DIM = sum(IN_WIDTHS)
IN_SPLITS = tuple(int(s) for s in np.cumsum(IN_WIDTHS)[:-1])

kernel_name = 'hybrid_natten_gdn_diffusion_block'


def rmsnorm(x, w):
    xf = x.astype(jnp.float32)
    xf = xf * lax.rsqrt(jnp.mean(xf * xf, axis=-1, keepdims=True) + EPS)
    return xf.astype(x.dtype) * w


def l2norm(x):
    xf = x.astype(jnp.float32)
    return xf * lax.rsqrt(jnp.sum(xf * xf, axis=-1, keepdims=True) + EPS)


def rotate_half_axis(x, cos, sin):
    x1, x2 = jnp.split(x, 2, axis=-1)
    c, s = cos[:, None, :], sin[:, None, :]
    return jnp.concatenate([x1 * c - x2 * s, x2 * c + x1 * s], axis=-1)


def axial_rope(x, rope):
    cos_r, sin_r, cos_c, sin_c = rope
    xr, xc = x[..., :ROPE_AXIS_DIM], x[..., ROPE_AXIS_DIM:]
    return jnp.concatenate([rotate_half_axis(xr, cos_r, sin_r),
                            rotate_half_axis(xc, cos_c, sin_c)], axis=-1)


def short_conv(x, w):
    out = lax.conv_general_dilated(
        x, w[:, None, :], window_strides=(1,), padding=[(CONV_K // 2, CONV_K // 2)],
        dimension_numbers=('NWC', 'WIO', 'NWC'), feature_group_count=x.shape[-1])
    return jax.nn.silu(out)


def neighborhood_attention(q, k, v, k_ctx, v_ctx, rpb):
    B, L, H, dh = q.shape
    rows = L // GRID_W
    kh = min(NA_KH, rows)
    scale = dh ** -0.5
    qg = (q * scale).reshape(B, rows, GRID_W, H, dh)
    kg = k.reshape(B, rows, GRID_W, H, dh)
    vg = v.reshape(B, rows, GRID_W, H, dh)
    col = jnp.arange(GRID_W)
    col_start = jnp.clip(col - NA_KW // 2, 0, GRID_W - NA_KW)
    col_idx = col_start[:, None] + jnp.arange(NA_KW)[None, :]
    col_off = col_idx - col[:, None] + NA_KW - 1

    def row_block(r):
        rs = jnp.clip(r - kh // 2, 0, rows - kh)
        k_rows = lax.dynamic_slice_in_dim(kg, rs, kh, axis=1)
        v_rows = lax.dynamic_slice_in_dim(vg, rs, kh, axis=1)
        k_win = k_rows[:, :, col_idx]
        v_win = v_rows[:, :, col_idx]
        q_r = lax.dynamic_index_in_dim(qg, r, axis=1, keepdims=False)
        row_off = rs + jnp.arange(kh) - r + NA_KH - 1
        bias = rpb[:, row_off[:, None, None], col_off[None, :, :]]
        bias = bias.transpose(0, 2, 1, 3).astype(jnp.float32)
        s_loc = jnp.einsum('bchd,bicjhd->bhcij', q_r, k_win).astype(jnp.float32) + bias
        s_ctx = jnp.einsum('bchd,bnhd->bhcn', q_r, k_ctx).astype(jnp.float32)
        s = jnp.concatenate([s_loc.reshape(B, H, GRID_W, kh * NA_KW), s_ctx], axis=-1)
        p = jax.nn.softmax(s, axis=-1).astype(v.dtype)
        p_loc = p[..., :kh * NA_KW].reshape(B, H, GRID_W, kh, NA_KW)
        p_ctx = p[..., kh * NA_KW:]
        return (jnp.einsum('bhcij,bicjhd->bchd', p_loc, v_win)
                + jnp.einsum('bhcn,bnhd->bchd', p_ctx, v_ctx))

    out = lax.map(row_block, jnp.arange(rows))
    return out.transpose(1, 0, 2, 3, 4).reshape(B, L, H * dh)


def context_attention(q, k, v):
    B, Lc, H, dh = q.shape
    s = jnp.einsum('bqhd,bkhd->bhqk', q, k).astype(jnp.float32) * dh ** -0.5
    p = jax.nn.softmax(s, axis=-1).astype(v.dtype)
    return jnp.einsum('bhqk,bkhd->bqhd', p, v).reshape(B, Lc, H * dh)


def gated_delta_chunked(q, k, v, g, beta, state0):
    B, L, H, dk = q.shape
    dv = v.shape[-1]
    n = L // GDN_CHUNK

    def blocks(t):
        return t.reshape(B, n, GDN_CHUNK, H, t.shape[-1]).transpose(1, 0, 3, 2, 4)

    q = blocks(q * dk ** -0.5)
    k = blocks(k)
    v = blocks(v)
    g = g.reshape(B, n, GDN_CHUNK, H).transpose(1, 0, 3, 2)
    beta = beta.reshape(B, n, GDN_CHUNK, H).transpose(1, 0, 3, 2)
    gc = jnp.cumsum(g, axis=-1)
    pos = jnp.arange(GDN_CHUNK)
    incl = pos[:, None] >= pos[None, :]
    decay = jnp.exp(jnp.where(incl, gc[..., :, None] - gc[..., None, :], -jnp.inf))
    kb = k * beta[..., None]
    a = jnp.where(pos[:, None] > pos[None, :],
                  jnp.einsum('nbhid,nbhjd->nbhij', kb, k) * decay, 0.0)
    eye = jnp.eye(GDN_CHUNK, dtype=a.dtype)
    t_inv = lax.linalg.triangular_solve(eye + a, jnp.broadcast_to(eye, a.shape),
                                        left_side=True, lower=True, unit_diagonal=True)
    u = jnp.einsum('nbhij,nbhje->nbhie', t_inv, v * beta[..., None])
    w = jnp.einsum('nbhij,nbhjd->nbhid', t_inv, kb * jnp.exp(gc)[..., None])
    qk = jnp.einsum('nbhid,nbhjd->nbhij', q, k) * decay
    q_dec = q * jnp.exp(gc)[..., None]
    k_dec = k * jnp.exp(gc[..., -1:] - gc)[..., None]
    g_last = jnp.exp(gc[..., -1])

    def step(state, xs):
        u_n, w_n, qk_n, q_n, k_n, gl_n = xs
        v_new = u_n - jnp.einsum('bhcd,bhde->bhce', w_n, state)
        o_n = (jnp.einsum('bhcd,bhde->bhce', q_n, state)
               + jnp.einsum('bhij,bhje->bhie', qk_n, v_new))
        state = state * gl_n[..., None, None] + jnp.einsum('bhcd,bhce->bhde', k_n, v_new)
        return state, o_n

    state, o = lax.scan(step, state0, (u, w, qk, q_dec, k_dec, g_last))
    o = o.transpose(1, 0, 3, 2, 4).reshape(B, L, H, dv)
    return o, state


def gdn_qkv(qkv, conv_w):
    B, L, _ = qkv.shape
    qkv = short_conv(qkv, conv_w).astype(jnp.float32)
    q, k, v = jnp.split(qkv, (GDN_HEADS * GDN_DK, 2 * GDN_HEADS * GDN_DK), axis=-1)
    q = l2norm(q.reshape(B, L, GDN_HEADS, GDN_DK))
    k = l2norm(k.reshape(B, L, GDN_HEADS, GDN_DK))
    v = v.reshape(B, L, GDN_HEADS, GDN_DV)
    return q, k, v


def gdn_gates(b, a, A_log, dt_bias):
    B, L, _ = b.shape
    b = b.reshape(B, L, 2, GDN_HEADS).astype(jnp.float32)
    a = a.reshape(B, L, 2, GDN_HEADS).astype(jnp.float32)
    beta = jax.nn.sigmoid(b)
    g = -jnp.exp(A_log.astype(jnp.float32)) * jax.nn.softplus(a + dt_bias.astype(jnp.float32))
    return beta, g


def bidirectional_gdn(q, k, v, g, beta, qc, kc, vc, gc, betac):
    B = q.shape[0]
    zero = jnp.zeros((B, GDN_HEADS, GDN_DK, GDN_DV), jnp.float32)
    flip = lambda t: jnp.flip(t, axis=1)
    oc_f, s_f = gated_delta_chunked(qc, kc, vc, gc[:, :, 0], betac[:, :, 0], zero)
    oc_b, s_b = gated_delta_chunked(flip(qc), flip(kc), flip(vc),
                                    flip(gc[:, :, 1]), flip(betac[:, :, 1]), zero)
    o_f, _ = gated_delta_chunked(q, k, v, g[:, :, 0], beta[:, :, 0], s_f)
    o_b, _ = gated_delta_chunked(flip(q), flip(k), flip(v),
                                 flip(g[:, :, 1]), flip(beta[:, :, 1]), s_b)
    return o_f + flip(o_b), oc_f + flip(oc_b)


def hybrid_layer(x, xc, c, c_ctx, w_ada, b_ada, g_pre, g_post, w_in, conv_w, rpb,
                 A_log, dt_bias, gdn_norm_w, w_out, rope, update_ctx):
    B, L, _ = x.shape
    Lc = xc.shape[1]
    shift, scale, gate = jnp.split(jax.nn.silu(c) @ w_ada + b_ada, 3, axis=-1)
    shift_c, scale_c, gate_c = jnp.split(jax.nn.silu(c_ctx) @ w_ada + b_ada, 3, axis=-1)
    h = rmsnorm(x, g_pre) * (1.0 + scale[:, None]) + shift[:, None]
    hc = rmsnorm(xc, g_pre) * (1.0 + scale_c) + shift_c
    na_q, na_k, na_v, na_z, g_qkv, g_z, g_b, g_a = jnp.split(h @ w_in, IN_SPLITS, axis=-1)
    na_qc, na_kc, na_vc, na_zc, g_qkvc, g_zc, g_bc, g_ac = jnp.split(hc @ w_in, IN_SPLITS, axis=-1)
    heads = lambda t: t.reshape(t.shape[0], t.shape[1], NA_HEADS, NA_DH)

    o_na = neighborhood_attention(heads(na_q), heads(na_k), heads(na_v),
                                  heads(na_kc), heads(na_vc), rpb)

    q, k, v = gdn_qkv(g_qkv, conv_w)
    q, k = axial_rope(q, rope), axial_rope(k, rope)
    qc, kc, vc = gdn_qkv(g_qkvc, conv_w)
    beta, g = gdn_gates(g_b, g_a, A_log, dt_bias)
    betac, gc = gdn_gates(g_bc, g_ac, A_log, dt_bias)
    o_g, o_gc = bidirectional_gdn(q, k, v, g, beta, qc, kc, vc, gc, betac)
    o_g = rmsnorm(o_g, gdn_norm_w).reshape(B, L, GDN_W).astype(x.dtype)

    y = jnp.concatenate([o_na * jax.nn.silu(na_z), o_g * jax.nn.silu(g_z)], axis=-1) @ w_out
    x = x + gate[:, None] * rmsnorm(y, g_post)

    if update_ctx:
        o_nac = context_attention(heads(na_qc), heads(na_kc), heads(na_vc))
        o_gc = rmsnorm(o_gc, gdn_norm_w).reshape(B, Lc, GDN_W).astype(xc.dtype)
        yc = jnp.concatenate([o_nac * jax.nn.silu(na_zc), o_gc * jax.nn.silu(g_zc)], axis=-1) @ w_out
        xc = xc + gate_c * rmsnorm(yc, g_post)
    return x, xc


def setup_inputs(seed: int = 0) -> dict:
    key = jax.random.key(seed)
    ks = jax.random.split(key, 16)
    f32 = jnp.float32
    nrm = lambda kk, shape, s: jax.random.normal(kk, shape, f32) * s
    x = nrm(ks[0], (BATCH, SEQ, D_MODEL), 1.0)
    c = nrm(ks[1], (BATCH, D_MODEL), 1.0)
    ctx = nrm(ks[2], (BATCH, CTX_LEN, D_MODEL), 1.0)
    c_ctx = nrm(ks[3], (D_MODEL,), 1.0)
    w_ada = nrm(ks[4], (DEPTH, D_MODEL, 3 * D_MODEL), 0.5 * D_MODEL ** -0.5)
    b_ada = nrm(ks[5], (DEPTH, 3 * D_MODEL), 0.01)
    g_pre = 1.0 + nrm(ks[6], (DEPTH, D_MODEL), 0.02)
    g_post = 1.0 + nrm(ks[7], (DEPTH, D_MODEL), 0.02)
    w_in = nrm(ks[8], (DEPTH, D_MODEL, IN_DIM), D_MODEL ** -0.5)
    conv_w = nrm(ks[9], (DEPTH, CONV_K, GDN_QKV_W), CONV_K ** -0.5)
    rpb = nrm(ks[10], (DEPTH, NA_HEADS, 2 * NA_KH - 1, 2 * NA_KW - 1), 0.1)
    A_log = jnp.log(jax.random.uniform(ks[11], (DEPTH, 2, GDN_HEADS), f32, 1.0, 16.0))
    dt = jnp.exp(jax.random.uniform(ks[12], (DEPTH, 2, GDN_HEADS), f32,
                                    math.log(1e-3), math.log(1e-1)))
    dt_bias = dt + jnp.log(-jnp.expm1(-dt))
    gdn_norm_w = 1.0 + nrm(ks[13], (DEPTH, GDN_DV), 0.02)
    w_out = nrm(ks[14], (DEPTH, MIX_W, D_MODEL), MIX_W ** -0.5)
    return {'x': x, 'c': c, 'ctx': ctx, 'c_ctx': c_ctx, 'w_ada': w_ada, 'b_ada': b_ada,
            'g_pre': g_pre, 'g_post': g_post, 'w_in': w_in, 'conv_w': conv_w, 'rpb': rpb,
            'A_log': A_log, 'dt_bias': dt_bias, 'gdn_norm_w': gdn_norm_w, 'w_out': w_out}


def reference(x, c, ctx, c_ctx, w_ada, b_ada, g_pre, g_post, w_in, conv_w, rpb,
              A_log, dt_bias, gdn_norm_w, w_out):
    L = x.shape[1]
    t = jnp.arange(L)
    row = (t // GRID_W).astype(jnp.float32)
    col = (t % GRID_W).astype(jnp.float32)
    inv_freq = ROPE_BASE ** (-jnp.arange(0, ROPE_AXIS_DIM, 2, dtype=jnp.float32) / ROPE_AXIS_DIM)
    ang_r = row[:, None] * inv_freq[None, :]
    ang_c = col[:, None] * inv_freq[None, :]
    rope = (jnp.cos(ang_r), jnp.sin(ang_r), jnp.cos(ang_c), jnp.sin(ang_c))
    xc = ctx
    for layer in range(DEPTH):
        x, xc = hybrid_layer(x, xc, c, c_ctx, w_ada[layer], b_ada[layer], g_pre[layer],
                             g_post[layer], w_in[layer], conv_w[layer], rpb[layer],
                             A_log[layer], dt_bias[layer], gdn_norm_w[layer], w_out[layer],
                             rope, update_ctx=(layer < DEPTH - 1))
    return x
```

```python
import os
import numpy as np
from contextlib import ExitStack
import concourse.bass as bass
import concourse.mybir as mybir
from concourse.bass_utils import run_bass_kernel_spmd

F32 = mybir.dt.float32
F32R = mybir.dt.float32r
AF = mybir.ActivationFunctionType
ALU = mybir.AluOpType

NCORES = 8
D = 1024
KC = 8
SEQ = 16384
SEGT = 2048
WIN = 2560
CTXL = 256
SU = 1024
NSU = 14
IN_DIM = 4112
EPS = 1e-6
NEG = -30000.0
PCOLS = 6400
PRCOLS = 3328
TOTCOLS = 53000

ENGS = ("pe", "act", "dve", "pool", "sp")


class Buf:
    _n = 0

    def __init__(self, ap, name="", key=None, excl=False):
        self.ap = ap
        if key is None:
            Buf._n += 1
            key = Buf._n
        self.key = key
        self.name = name
        self.excl = excl

    def v(self, a, b):
        return self.ap[:, a:b]

    def t(self, i, n):
        return self.ap[:, i * n:(i + 1) * n]


class Sched:
    def __init__(self, nc, n_dma_sems=24, sems_per_eng=4, blk=4096):
        self.nc = nc
        self.ops = []
        self.n_dma_sems = n_dma_sems
        self.sems_per_eng = sems_per_eng
        self.blk = blk
        self.BAR = Buf(None, "BAR")

    def add(self, eng, fn, reads=(), writes=(), dma=False, uwrites=()):
        r = [b.key for b in reads if not b.excl]
        r.append(self.BAR.key)
        wr = list(dict.fromkeys([b.key for b in writes] + [b.key for b in reads if b.excl]))
        self.ops.append(dict(eng=eng, fn=fn, reads=r, writes=wr, dma=dma,
                             uwrites=[b.key for b in uwrites]))

    def barrier(self):
        self.ops.append(dict(eng="pool", fn=None, reads=[], writes=[self.BAR.key], dma=False, uwrites=[]))

    def emit(self, stack):
        nc = self.nc
        ops = self.ops
        n = len(ops)
        last_w = {}
        readers = {}
        deps = [set() for _ in range(n)]
        unord = {}
        for i, op in enumerate(ops):
            for k in op["reads"]:
                deps[i].update(last_w.get(k, ()))
            for k in op["writes"]:
                deps[i].update(last_w.get(k, ()))
                deps[i].update(readers.get(k, ()))
            for k in op["uwrites"]:
                deps[i].update(readers.get(k, ()))
                if not (unord.get(k, False) and not readers.get(k)):
                    deps[i].update(last_w.get(k, ()))
            deps[i].discard(i)
            for k in op["reads"]:
                readers.setdefault(k, []).append(i)
            for k in op["writes"]:
                last_w[k] = [i]
                readers[k] = []
                unord[k] = False
            for k in op["uwrites"]:
                if unord.get(k, False) and not readers.get(k):
                    last_w[k].append(i)
                else:
                    last_w[k] = [i]
                    readers[k] = []
                    unord[k] = True
        need_sig = [False] * n
        for i, op in enumerate(ops):
            if op["dma"]:
                need_sig[i] = True
            for d in deps[i]:
                od = ops[d]
                if od["dma"] or op["dma"] or od["eng"] != op["eng"]:
                    need_sig[d] = True
                elif od["eng"] != "pe":
                    need_sig[d] = True
        for i, op in enumerate(ops):
            if op["fn"] is None and not op["dma"]:
                pass
        eng_sems = {e: [stack.enter_context(nc.semaphore(f"s_{e}{j}")) for j in range(self.sems_per_eng)]
                    for e in ENGS if e != "sp"}
        dma_sems = [stack.enter_context(nc.semaphore(f"s_dma{j}")) for j in range(self.n_dma_sems)]
        sig = [None] * n
        eng_cnt = {e: 0 for e in ENGS}
        dma_use = [0] * self.n_dma_sems
        dma_idx = 0
        dma_prev = [None] * n
        for i, op in enumerate(ops):
            if not need_sig[i]:
                continue
            if op["dma"]:
                s = dma_idx % self.n_dma_sems
                dma_idx += 1
                if dma_use[s] > 0:
                    dma_prev[i] = (dma_sems[s], 16 * dma_use[s])
                dma_use[s] += 1
                sig[i] = (dma_sems[s], 16 * dma_use[s], 16)
            else:
                e = op["eng"]
                c = eng_cnt[e]
                eng_cnt[e] += 1
                blk_i = c // self.blk
                s = blk_i % self.sems_per_eng
                val = (blk_i // self.sems_per_eng) * self.blk + (c % self.blk) + 1
                sig[i] = (eng_sems[e][s], val, 1)
        known = {e: {} for e in ENGS}
        waits = [[] for _ in range(n)]
        for i, op in enumerate(ops):
            e = op["eng"]
            wl = []
            if dma_prev[i] is not None:
                wl.append(dma_prev[i])
            for d in deps[i]:
                od = ops[d]
                if sig[d] is None:
                    continue
                if not od["dma"] and not op["dma"] and od["eng"] == e and e == "pe":
                    continue
                wl.append((sig[d][0], sig[d][1]))
            best = {}
            for (s, v) in wl:
                key = id(s)
                if v > known[e].get(key, 0):
                    if key not in best or best[key][1] < v:
                        best[key] = (s, v)
            for key, (s, v) in best.items():
                known[e][key] = v
                waits[i].append((s, v))
        self.n_waits = sum(len(w) for w in waits)
        self.n_sigs = sum(1 for s in sig if s is not None)
        with nc.Block() as block:
            def run(ename):
                def body(eng):
                    for i, op in enumerate(ops):
                        if op["eng"] != ename:
                            continue
                        for (s, v) in waits[i]:
                            eng.wait_ge(s, v)
                        if op["fn"] is None:
                            if sig[i] is not None:
                                eng.engine_nop().then_inc(sig[i][0], sig[i][2])
                            continue
                        ins = op["fn"](eng)
                        if sig[i] is not None:
                            ins.then_inc(sig[i][0], sig[i][2])
                return body
            block.sync(run("sp"))
            block.scalar(run("act"))
            block.vector(run("dve"))
            block.gpsimd(run("pool"))
            block.tensor(run("pe"))


def _rope_tables():
    t = np.arange(SEQ)
    row = (t // 64).astype(np.float32)
    col = (t % 64).astype(np.float32)
    inv = (np.float32(10000.0) ** (-np.arange(0, 64, 2, dtype=np.float32) / np.float32(64))).astype(np.float32)
    ar = (row[:, None] * inv[None, :]).astype(np.float32)
    ac = (col[:, None] * inv[None, :]).astype(np.float32)
    cr, sr, cc, sc = np.cos(ar), np.sin(ar), np.cos(ac), np.sin(ac)
    COS = np.concatenate([cr, cr, cc, cc], 1).T.astype(np.float32)
    SIN = np.concatenate([-sr, sr, -sc, sc], 1).T.astype(np.float32)
    return np.ascontiguousarray(COS), np.ascontiguousarray(SIN)


def _consts():
    i = np.arange(128)
    Linc = (i[:, None] >= i[None, :]).astype(np.float32)
    Lst = (i[:, None] > i[None, :]).astype(np.float32)
    Uinc = (i[:, None] <= i[None, :]).astype(np.float32)
    Ust = (i[:, None] < i[None, :]).astype(np.float32)
    I = np.eye(128, dtype=np.float32)
    ones = np.ones((128, 128), np.float32)
    cm = np.concatenate([Linc, Lst, Uinc, Ust, I, ones], 1)
    perm = np.concatenate([np.arange(32, 64), np.arange(0, 32), np.arange(96, 128), np.arange(64, 96)])
    Pm = np.zeros((128, 128), np.float32)
    Pm[np.arange(128), perm] = 1.0
    return np.ascontiguousarray(cm), np.ascontiguousarray(Pm.T)


def _na_bias_tables(rpb, core):
    base = 32 * core - 4
    out = np.full((8, 128, 5, 6, 128), NEG, np.float32)
    p = np.arange(128)
    q = np.arange(128)
    kcol = p % 64
    qcol = q % 64
    cs = np.clip(qcol - 8, 0, 48)
    colok = (kcol[:, None] >= cs[None, :]) & (kcol[:, None] < cs[None, :] + 16)
    coloff = kcol[:, None] - qcol[None, :] + 15
    for ci, g in enumerate([0, 1, 2, 14, 15]):
        kt0 = min(g, 14)
        for t in range(6):
            wrow = 2 * (kt0 + t) + p // 64
            krow = base + wrow
            j = 2 * g + q // 64
            r = 32 * core + j
            rs = np.clip(r - 4, 0, 248)
            rowok = (krow[:, None] >= rs[None, :]) & (krow[:, None] < rs[None, :] + 8)
            rowoff = krow[:, None] - r[None, :] + 7
            ok = rowok & colok
            ro = np.clip(rowoff, 0, 14)
            co = np.clip(coloff, 0, 30)
            vals = rpb[:, ro, co]
            out[:, :, ci, t, :] = np.where(ok[None], vals, NEG)
    return np.ascontiguousarray(out.reshape(8, 128, 5 * 6 * 128))


def host_prep(inp):
    x = np.asarray(inp["x"], np.float32)[0]
    ctx = np.asarray(inp["ctx"], np.float32)[0]
    c = np.asarray(inp["c"], np.float32)[0]
    c_ctx = np.asarray(inp["c_ctx"], np.float32)
    w_ada = np.ascontiguousarray(np.asarray(inp["w_ada"], np.float32)[0])
    b_ada = np.asarray(inp["b_ada"], np.float32)[0]
    g_pre = np.asarray(inp["g_pre"], np.float32)[0]
    g_post = np.asarray(inp["g_post"], np.float32)[0]
    w_in = np.ascontiguousarray(np.asarray(inp["w_in"], np.float32)[0])
    conv_w = np.asarray(inp["conv_w"], np.float32)[0]
    rpb = np.asarray(inp["rpb"], np.float32)[0]
    A_log = np.asarray(inp["A_log"], np.float32)[0]
    dt_bias = np.asarray(inp["dt_bias"], np.float32)[0]
    gnw = np.asarray(inp["gdn_norm_w"], np.float32)[0]
    w_out = np.ascontiguousarray(np.asarray(inp["w_out"], np.float32)[0])

    COS, SIN = _rope_tables()
    cmask, ProtT = _consts()
    xpad = np.zeros((SEQ + 8, D), np.float32)
    xpad[4:4 + SEQ] = x

    def colT(v, n):
        return np.ascontiguousarray(v.reshape(n, 128).T)

    cvec = np.ascontiguousarray(np.stack([c.reshape(8, 128).T, c_ctx.reshape(8, 128).T], -1).reshape(128, 16))
    shared = dict(
        ctxT=np.ascontiguousarray(ctx.T.reshape(8, 128, CTXL)),
        cvec=cvec, w_ada=w_ada, b_adaT=colT(b_ada, 24),
        b_gate_rep=np.ascontiguousarray(np.broadcast_to(b_ada[2048:3072], (128, 1024))),
        g_preT=colT(g_pre, 8),
        g_post_rep=np.ascontiguousarray(np.broadcast_to(g_post, (128, 1024))),
        w_in=w_in, w_out=w_out, cmask=cmask, ProtT=ProtT,
        gnw=np.ascontiguousarray(gnw.reshape(128, 1)),
        AlO=np.ascontiguousarray(np.broadcast_to(np.tile(A_log.reshape(8), 16), (128, 128))),
        dtO=np.ascontiguousarray(np.broadcast_to(np.tile(dt_bias.reshape(8), 16), (128, 128))),
    )
    shared["convO"] = np.ascontiguousarray(conv_w.T.reshape(12, 128, 5).transpose(1, 0, 2).reshape(128, 60))
    conv_kv = conv_w[:, 512:1536]
    conv_f = conv_kv.T.reshape(8, 128, 5).transpose(1, 0, 2).reshape(128, 40)
    conv_b = conv_kv[::-1].T.reshape(8, 128, 5).transpose(1, 0, 2).reshape(128, 40)

    per_core = []
    for i in range(NCORES):
        T0 = SEGT * i
        d = dict(shared)
        win = np.zeros((WIN, D), np.float32)
        lo, hi = T0 - 256, T0 + 2304
        a, b = max(lo, 0), min(hi, SEQ)
        win[a - lo:b - lo] = x[a:b]
        d["xwT"] = np.ascontiguousarray(win.T.reshape(8, 128, WIN))
        d["xown"] = np.ascontiguousarray(x[T0:T0 + SEGT])
        nf = 2 * i
        toks = np.empty((NSU, SU + 4), np.int64)
        isf = np.zeros(NSU, bool)
        for u in range(NSU):
            pidx = np.arange(-2, SU + 2)
            if u < nf:
                toks[u] = SU * u + pidx
                isf[u] = True
            else:
                v = u - nf
                toks[u] = (SEQ - 1) - (SU * v + pidx)
        valid = (toks >= 0) & (toks < SEQ)
        xs = xpad[np.clip(toks, -4, SEQ + 3) + 4]
        xs = xs * valid[..., None]
        xsT = xs.transpose(0, 2, 1).reshape(NSU, 8, 128, SU + 4)
        d["xsT"] = np.ascontiguousarray(xsT[..., 2:SU + 2])
        d["xsH"] = np.ascontiguousarray(np.concatenate([xsT[..., 0:2], xsT[..., SU + 2:SU + 4]], -1))
        hm = np.concatenate([valid[:, 0:2], valid[:, SU + 2:SU + 4]], 1).astype(np.float32)
        d["hmS"] = np.ascontiguousarray(np.broadcast_to(hm.reshape(1, NSU * 4), (128, NSU * 4)))
        hmo = np.ones((2, 4), np.float32)
        if i == 0:
            hmo[0, 0:2] = 0
        if i == NCORES - 1:
            hmo[0, 2:4] = 0
        d["hmO"] = np.ascontiguousarray(np.broadcast_to(hmo.reshape(1, 8), (128, 8)))
        wg = np.empty((NSU, D, 8), np.float32)
        al = np.empty((NSU, 4), np.float32)
        dtb = np.empty((NSU, 4), np.float32)
        cv = np.empty((NSU, 128, 40), np.float32)
        for u in range(NSU):
            dr = 0 if isf[u] else 1
            wg[u, :, 0:4] = w_in[:, 4096 + 4 * dr:4096 + 4 * dr + 4]
            wg[u, :, 4:8] = w_in[:, 4104 + 4 * dr:4104 + 4 * dr + 4]
            al[u] = A_log[dr]
            dtb[u] = dt_bias[dr]
            cv[u] = conv_f if isf[u] else conv_b
        d["w_gs"] = wg
        d["AlS"] = np.ascontiguousarray(np.broadcast_to(np.tile(al[:, None, :], (1, 8, 1)).reshape(1, NSU * 32), (128, NSU * 32)))
        d["dtS"] = np.ascontiguousarray(np.broadcast_to(np.tile(dtb[:, None, :], (1, 8, 1)).reshape(1, NSU * 32), (128, NSU * 32)))
        d["convS"] = cv
        tk = np.clip(toks[:, 2:SU + 2], 0, SEQ - 1).reshape(-1)
        tk = np.concatenate([tk, np.arange(T0, T0 + SEGT)])
        d["ropeC"] = np.ascontiguousarray(COS[:, tk])
        d["ropeS"] = np.ascontiguousarray(SIN[:, tk])
        ms = np.zeros((8,), np.float32)
        ms[i] = 1.0
        d["mseg"] = np.ascontiguousarray(np.broadcast_to(ms, (128, 8)))
        d["nab"] = _na_bias_tables(rpb, i)
        per_core.append(d)
    return per_core


IN_SHAPES = dict(
    xwT=[8, 128, WIN], xown=[SEGT, D], xsT=[NSU, 8, 128, SU], xsH=[NSU, 8, 128, 4], ctxT=[8, 128, CTXL],
    cvec=[128, 16], w_ada=[D, 3072], b_adaT=[128, 24], b_gate_rep=[128, 1024], g_preT=[128, 8],
    g_post_rep=[128, 1024], w_in=[D, IN_DIM], w_out=[D, D], cmask=[128, 768], ProtT=[128, 128],
    gnw=[128, 1], AlO=[128, 128], dtO=[128, 128], convO=[128, 60], hmS=[128, NSU * 4], hmO=[128, 8],
    w_gs=[NSU, D, 8], AlS=[128, NSU * 32], dtS=[128, NSU * 32], convS=[NSU, 128, 40],
    ropeC=[128, SEQ], ropeS=[128, SEQ], mseg=[128, 8], nab=[8, 128, 3840],
)


class Arena:
    def __init__(self, ap, size):
        self.ap = ap
        self.size = size
        self.off = 0

    def alloc(self, cols, name=""):
        o = self.off
        self.off += cols
        assert self.off <= self.size, f"arena overflow at {name}: {self.off} > {self.size}"
        return Buf(self.ap[:, o:o + cols], name)

    def mark(self):
        return self.off

    def release(self, m):
        self.off = m


class Chain:
    pass


class LR(list):
    r = False
    x = None


class Builder:
    def __init__(self, stop="full", dbg=False):
        self.stop = stop
        self.nc = bass.Bass("TRN2", target_bir_lowering=False)
        nc = self.nc
        self.din = {k: nc.dram_tensor(k, shp, F32, kind="ExternalInput").ap() for k, shp in IN_SHAPES.items()}
        self.out = nc.dram_tensor("out", [SEGT, D], F32, kind="ExternalOutput").ap()
        self.scr = nc.dram_tensor("scr_gated", [8, 128, SEGT], F32).ap()
        self.dbg = nc.dram_tensor("dbg", [128, 8192], F32, kind="ExternalOutput").ap() if dbg else None
        self.dbg_off = 0
        self.dbg_map = {}
        self.outs = []
        self.dmaq = 0

    def dma(self, out, in_, r=(), w=(), uw=(), eng=None):
        if eng is None:
            eng = "sp"
        self.S.add(eng, lambda E: E.dma_start(out=out, in_=in_), r, w, dma=True, uwrites=uw)

    def act(self, out, in_, func, r, w, scale=None, bias=None):
        kw = dict(out=out, in_=in_, func=func)
        if scale is not None:
            kw["scale"] = scale
        if bias is not None:
            kw["bias"] = bias
        self.S.add("act", lambda E: E.activation(**kw), r, w)

    def ts(self, out, in0, s1, s2, op0, op1, r, w, eng="dve"):
        kw = dict(out=out, in0=in0, scalar1=s1, scalar2=s2, op0=op0)
        if op1 is not None:
            kw["op1"] = op1
        self.S.add(eng, lambda E: E.tensor_scalar(**kw), r, w)

    def tt(self, out, in0, in1, op, r, w, eng="dve"):
        self.S.add(eng, lambda E: E.tensor_tensor(out=out, in0=in0, in1=in1, op=op), r, w)

    def stt(self, out, in0, scalar, in1, op0, op1, r, w, eng="dve"):
        eng = "dve"
        self.S.add(eng, lambda E: E.scalar_tensor_tensor(out=out, in0=in0, scalar=scalar, in1=in1,
                                                         op0=op0, op1=op1), r, w)

    def cp(self, out, in_, r, w, eng="dve"):
        self.S.add(eng, lambda E: E.tensor_copy(out=out, in_=in_), r, w)

    def recip(self, out, in_, r, w):
        self.S.add("dve", lambda E: E.reciprocal(out=out, in_=in_), r, w)

    def memset(self, ap, val, w, eng="pool"):
        self.S.add(eng, lambda E: E.memset(ap, val), (), w)

    def mm(self, out, lhsT, rhs, start, stop, r, w):
        self.S.add("pe", lambda E: E.matmul(out, lhsT=lhsT, rhs=rhs, start=start, stop=stop), r, w)

    def tr(self, out, in_, r, w):
        ident = self.Id
        self.S.add("pe", lambda E: E.transpose(out=out, in_=in_, identity=ident), list(r) + [self.cm], w)

    def const(self, name, cols):
        b = self.parena.alloc(cols, name)
        self.dma(b.ap, self.din[name], w=[b])
        return b

    def tap(self, name, ap, r, cols):
        if self.dbg is None:
            return
        o = self.dbg_off
        self.dbg_off += cols
        assert self.dbg_off <= 8192
        self.dbg_map[name] = (o, cols)
        ob = Buf(None, "dbg_" + name)
        self.dma(self.dbg[:, o:o + cols], ap, r=r, w=[ob])
        self.outs.append(ob)

    def nps(self):
        self.psi = (self.psi + 1) % 2
        return self.PS[self.psi]

    def build(self):
        nc = self.nc
        with ExitStack() as st:
            self.S = Sched(nc)
            pt_ = st.enter_context(nc.sbuf_tensor("parena", [128, PCOLS], F32))
            self.parena = Arena(pt_[:, :], PCOLS)
            prt_ = st.enter_context(nc.sbuf_tensor("prarena", [128, PRCOLS], F32R))
            self.prarena = Arena(prt_[:, :], PRCOLS)
            self.arena = None
            self.rarena = None
            self.phase_stack = None
            self.phase_id = 0
            self.PS = []
            for j in range(4):
                pt = st.enter_context(nc.psum_tensor(f"ps{j}", [128, 1024], F32))
                self.PS.append(Buf(pt[:, 0:512], f"ps{j}a", excl=True))
                self.PS.append(Buf(pt[:, 512:1024], f"ps{j}b", excl=True))
                self.PSWt = getattr(self, "PSWt", []) + [pt]
            self.psi = 0
            self.cps = [[Buf(self.PS[4 + c].ap[:, 128 * s:128 * (s + 1)], f"cps{c}_{s}", key=self.PS[4 + c].key, excl=True)
                         for s in range(4)] for c in range(4)]
            self.program()
            fin = list(self.outs)
            self.S.add("sp", None, fin, [])
            self.S.emit(st)
        return nc

    def begin_phase(self, cols, rcols=0):
        assert cols + rcols + PCOLS + PRCOLS <= TOTCOLS, (cols, rcols)
        self.phase_id += 1
        self.phase_stack = ExitStack()
        t = self.phase_stack.enter_context(self.nc.sbuf_tensor(f"ph{self.phase_id}", [128, cols], F32))
        self.arena = Arena(t[:, :], cols)
        self.rarena = None
        self.stage = None
        if rcols:
            tr_ = self.phase_stack.enter_context(self.nc.sbuf_tensor(f"phr{self.phase_id}", [128, rcols], F32R))
            self.rarena = Arena(tr_[:, :], rcols)
            self.stage = [self.arena.alloc(1024, "stage0"), self.arena.alloc(1024, "stage1")]
            self.stage_i = 0

    def end_phase(self):
        self.phase_stack.close()
        self.arena = None
        self.rarena = None
        self.S.barrier()

    def program(self):
        self.phase0()
        if self.stop == "p0":
            return
        self.phase_ctx()
        if self.stop == "ctx":
            return
        self.phase_stream()
        if self.stop == "stream":
            return
        self.phase_own_gdn()
        if self.stop == "gdn":
            return
        self.phase_natten()
        if self.stop == "na":
            return
        self.phase_final()

    def phase0(self):
        A = self.parena
        din = self.din
        self.cm = self.const("cmask", 768)
        self.Linc, self.Lst, self.Uinc, self.Ust, self.Id, self.Ones = [self.cm.t(i, 128) for i in range(6)]
        self.prot = self.const("ProtT", 128)
        self.gnw = self.const("gnw", 1)
        self.AlO = self.const("AlO", 128)
        self.dtO = self.const("dtO", 128)
        self.convO = self.const("convO", 60)
        self.hmO = self.const("hmO", 8)
        self.hmS = self.const("hmS", NSU * 4)
        self.AlS = self.const("AlS", NSU * 32)
        self.dtS = self.const("dtS", NSU * 32)
        self.mseg = self.const("mseg", 8)
        gpre = self.const("g_preT", 8)
        badaT = self.const("b_adaT", 24)
        cv = self.const("cvec", 16)
        self.mod = A.alloc(48, "mod")
        self.A1 = A.alloc(16, "A1")
        self.G2 = A.alloc(1024, "G2")
        self.nAO = A.alloc(128, "nAO")
        self.nAS = A.alloc(NSU * 32, "nAS")
        self.act(self.nAO.ap, self.AlO.ap, AF.Exp, [self.AlO], [self.nAO])
        self.ts(self.nAO.ap, self.nAO.ap, -1.0, None, ALU.mult, None, [self.nAO], [self.nAO])
        self.act(self.nAS.ap, self.AlS.ap, AF.Exp, [self.AlS], [self.nAS])
        self.ts(self.nAS.ap, self.nAS.ap, -1.0, None, ALU.mult, None, [self.nAS], [self.nAS])
        self.Ones_r = self.prarena.alloc(128, "Ones_r")
        self.cp(self.Ones_r.ap, self.Ones, [self.cm], [self.Ones_r])
        self.begin_phase(30000)
        A = self.arena
        csil = A.alloc(16, "csil")
        self.act(csil.ap, cv.ap, AF.Silu, [cv], [csil])
        wada = A.alloc(8 * 3072, "wada")
        for kc in range(8):
            self.dma(wada.v(kc * 3072, (kc + 1) * 3072), din["w_ada"][kc * 128:(kc + 1) * 128, :], uw=[wada])
        ps = self.PS[0]
        for ct in range(24):
            for kc in range(8):
                self.mm(ps.ap[:, ct * 2:ct * 2 + 2], wada.ap[:, kc * 3072 + ct * 128: kc * 3072 + (ct + 1) * 128],
                        csil.ap[:, kc * 2:kc * 2 + 2], kc == 0, kc == 7, [wada, csil], [ps])
        mod3 = self.mod.ap.rearrange("p (c w) -> p c w", w=2)
        ps3 = ps.ap[:, 0:48].rearrange("p (c w) -> p c w", w=2)
        for w in range(2):
            self.tt(mod3[:, :, w], ps3[:, :, w], badaT.ap, ALU.add, [ps, badaT], [self.mod])
        a13 = self.A1.ap.rearrange("p (c w) -> p c w", w=2)
        for w in range(2):
            self.stt(a13[:, :, w], mod3[:, 8:16, w], 1.0, gpre.ap, ALU.add, ALU.mult, [self.mod, gpre], [self.A1])
        rep = A.alloc(8 * 128, "rep")
        for kc in range(8):
            self.ts(rep.t(kc, 128), self.Ones, csil.ap[:, kc * 2:kc * 2 + 1], None, ALU.mult, None,
                    [self.cm, csil], [rep])
        bg = A.alloc(1024, "bgate")
        gp = A.alloc(1024, "gpost")
        self.dma(bg.ap, din["b_gate_rep"], w=[bg])
        self.dma(gp.ap, din["g_post_rep"], w=[gp])
        for n in range(2):
            pg = self.PS[2 + n]
            for kc in range(8):
                self.mm(pg.ap, rep.t(kc, 128), wada.ap[:, kc * 3072 + 2048 + n * 512: kc * 3072 + 2048 + (n + 1) * 512],
                        kc == 0, kc == 7, [rep, wada], [pg])
            self.tt(self.G2.t(n, 512), pg.ap, bg.t(n, 512), ALU.add, [pg, bg], [self.G2])
        self.tt(self.G2.ap, self.G2.ap, gp.ap, ALU.mult, [self.G2, gp], [self.G2])
        self.tap("mod", self.mod.ap, [self.mod], 48)
        self.tap("G2", self.G2.ap[:, 0:64], [self.G2], 64)
        self.end_phase()

    def gen_h(self, srcs, N, w, xt, sq, rs):
        xs = xt.x
        ones = self.Ones_r.ap if sq.r else self.Ones
        ones_b = self.Ones_r if sq.r else self.cm
        for kc in range(8):
            if isinstance(srcs[kc], tuple):
                self.dma(xs[kc].ap[:, 0:2], srcs[kc][0], uw=[xs[kc]])
                self.dma(xs[kc].ap[:, 2:4], srcs[kc][1], uw=[xs[kc]])
            else:
                self.dma(xs[kc].ap[:, 0:N], srcs[kc], w=[xs[kc]])
        ps = self.PS[2]
        for kc in range(8):
            sqb = sq[kc % 2]
            self.act(sqb.ap[:, 0:N], xs[kc].ap[:, 0:N], AF.Square, [xs[kc]], [sqb])
            self.mm(ps.ap[:, 0:N], ones, sqb.ap[:, 0:N], kc == 0, kc == 7, [sqb, ones_b], [ps])
        self.act(rs.ap[:, 0:N], ps.ap[:, 0:N], AF.Sqrt, [ps], [rs], scale=1.0 / D, bias=EPS)
        self.recip(rs.ap[:, 0:N], rs.ap[:, 0:N], [rs], [rs])
        for kc in range(8):
            self.tt(xs[kc].ap[:, 0:N], xs[kc].ap[:, 0:N], rs.ap[:, 0:N], ALU.mult, [xs[kc], rs], [xs[kc]], eng="pool")
            self.ts(xt[kc].ap[:, 0:N], xs[kc].ap[:, 0:N], self.A1.ap[:, kc * 2 + w:kc * 2 + w + 1],
                    self.mod.ap[:, kc * 2 + w:kc * 2 + w + 1], ALU.mult, ALU.add,
                    [xs[kc], self.A1, self.mod], [xt[kc]])

    def alloc_xt(self, N, nbuf=2, r=False):
        sets = LR()
        x_shared = None
        for j in range(nbuf):
            if r and x_shared is not None:
                x = x_shared
            else:
                x = [self.arena.alloc(N, f"xt{j}_{kc}") for kc in range(8)]
                x_shared = x
            if r:
                h = LR(self.rarena.alloc(N, f"ht{j}_{kc}") for kc in range(8))
            else:
                h = LR(x)
            h.x = x
            h.r = r
            sets.append(h)
        if r and nbuf >= 2:
            hh = LR(self.rarena.alloc(4, f"hth_{kc}") for kc in range(8))
            hh.x = [self.arena.alloc(4, f"xth_{kc}") for kc in range(8)]
            hh.r = True
            sets.x = hh
        src = self.rarena if r else self.arena
        sq = LR([src.alloc(N, "sq0"), src.alloc(N, "sq1")])
        sq.r = r
        rs = self.arena.alloc(N, "rs")
        return sets, sq, rs

    def proj_fm(self, xt, N, wfn, M, wbufs, evac, c0=0):
        ps = self.nps()
        for kc in range(8):
            self.mm(ps.ap[0:M, 0:N], wfn(kc), xt[kc].ap[:, c0:c0 + N], kc == 0, kc == 7, [xt[kc]] + list(wbufs), [ps])
        evac(ps)

    def prep_unit(self, main_fn, halo_srcs, n, w, cts, wget, raw, conv, conv_base, hm_ap, hm_buf,
                  gate_wfn, gate_wbufs, ng, graw, rope_off, xsets, sq, rs, tmp, rct, rst, extra=None, pro=False):
        NT = min(512, n)
        ntile = n // NT
        nct = len(cts)
        prefetch = len(xsets) >= 2 and getattr(xsets, "x", None) is not None
        cur = None
        if halo_srcs is not None:
            xth = xsets.x if prefetch else xsets[0]
            if not pro:
                self.gen_h(halo_srcs, 4, w, xth, sq, rs)
            if prefetch:
                cur = xsets[0]
                if not pro:
                    self.gen_h(main_fn(0, NT), NT, w, cur, sq, rs)
            for ci in range(nct):
                wfn_c, wbufs = wget(ci)
                def ev(ps, ci=ci):
                    self.tt(raw[ci].ap[:, 0:2], ps.ap[:, 0:2], hm_ap[:, 0:2], ALU.mult, [ps, hm_buf], [raw[ci]])
                    self.tt(raw[ci].ap[:, n + 2:n + 4], ps.ap[:, 2:4], hm_ap[:, 2:4], ALU.mult, [ps, hm_buf], [raw[ci]])
                self.proj_fm(xth, 4, wfn_c, 128, wbufs, ev)
        else:
            for ci in range(nct):
                self.memset(raw[ci].ap[:, 0:2], 0.0, [raw[ci]])
                self.memset(raw[ci].ap[:, n + 2:n + 4], 0.0, [raw[ci]])
        for j in range(ntile):
            if prefetch:
                if cur is None:
                    cur = xsets[j % 2]
                    self.gen_h(main_fn(j, NT), NT, w, cur, sq, rs)
                xt = cur
                cur = None
                if j + 1 < ntile:
                    cur = xsets[(j + 1) % 2]
                    self.gen_h(main_fn(j + 1, NT), NT, w, cur, sq, rs)
            else:
                xt = xsets[(j + 1) % len(xsets)]
                self.gen_h(main_fn(j, NT), NT, w, xt, sq, rs)
            for ci in range(nct):
                wfn_c, wbufs = wget(ci)
                def ev(ps, ci=ci, j=j):
                    self.act(raw[ci].ap[:, 2 + NT * j:2 + NT * (j + 1)], ps.ap[:, 0:NT], AF.Copy, [ps], [raw[ci]])
                self.proj_fm(xt, NT, wfn_c, 128, wbufs, ev)
            if extra is not None:
                extra(j, xt, NT)
            ps = self.nps()
            for kc in range(8):
                self.mm(ps.ap[0:ng, 0:NT], gate_wfn(kc), xt[kc].ap[:, 0:NT], kc == 0, kc == 7,
                        [xt[kc]] + list(gate_wbufs), [ps])
            gT = tmp[0]
            self.act(gT.ap[0:ng, 0:NT], ps.ap[0:ng, 0:NT], AF.Copy, [ps], [gT])
            for c in range(NT // 128):
                ps2 = self.nps()
                idn = self.Id[0:ng, 0:ng]
                self.S.add("pe", lambda E, o=ps2.ap[:, 0:ng], i=gT.ap[0:ng, c * 128:(c + 1) * 128], idn=idn:
                           E.transpose(out=o, in_=i, identity=idn), [gT, self.cm], [ps2])
                cc = j * (NT // 128) + c
                self.cp(graw.ap[:, cc * ng:(cc + 1) * ng], ps2.ap[:, 0:ng], [ps2], [graw])
        for j in range(ntile):
            a = NT * j
            for ci in range(nct):
                eng = "dve"
                cb = conv_base + ci * 5
                self.ts(tmp[ci % 2].ap[:, 0:NT], raw[ci].ap[:, a:a + NT], conv.ap[:, cb:cb + 1], None, ALU.mult, None,
                        [raw[ci], conv], [tmp[ci % 2]], eng=eng)
                for tp in range(1, 5):
                    self.stt(tmp[ci % 2].ap[:, 0:NT], raw[ci].ap[:, a + tp:a + tp + NT], conv.ap[:, cb + tp:cb + tp + 1],
                             tmp[ci % 2].ap[:, 0:NT], ALU.mult, ALU.add, [raw[ci], conv, tmp[ci % 2]], [tmp[ci % 2]], eng=eng)
                self.act(raw[ci].ap[:, a:a + NT], tmp[ci % 2].ap[:, 0:NT], AF.Silu, [tmp[ci % 2]], [raw[ci]])
        for j in range(ntile):
            a = NT * j
            if rope_off is not None:
                self.dma(rct.ap[:, 0:NT], self.din["ropeC"][:, rope_off + a:rope_off + a + NT], w=[rct])
                self.dma(rst.ap[:, 0:NT], self.din["ropeS"][:, rope_off + a:rope_off + a + NT], w=[rst])
            for ci in range(nct):
                if cts[ci] == "v":
                    continue
                rr = raw[ci].ap[:, a:a + NT]
                sqb = sq[ci % 2]
                self.act(sqb.ap[:, 0:NT], rr, AF.Square, [raw[ci]], [sqb])
                ps = self.PS[2]
                self.mm(ps.ap[:, 0:NT], self.Ones_r.ap if sq.r else self.Ones, sqb.ap[:, 0:NT], True, True,
                        [sqb, self.Ones_r if sq.r else self.cm], [ps])
                self.act(rs.ap[:, 0:NT], ps.ap[:, 0:NT], AF.Sqrt, [ps], [rs], scale=1.0, bias=EPS)
                self.recip(rs.ap[:, 0:NT], rs.ap[:, 0:NT], [rs], [rs])
                if cts[ci] == "q":
                    self.stt(rr, rr, 128.0 ** -0.5, rs.ap[:, 0:NT], ALU.mult, ALU.mult, [raw[ci], rs], [raw[ci]])
                else:
                    self.tt(rr, rr, rs.ap[:, 0:NT], ALU.mult, [raw[ci], rs], [raw[ci]])
                if rope_off is not None:
                    ps2 = self.nps()
                    self.mm(ps2.ap[:, 0:NT], self.prot.ap, rr, True, True, [raw[ci], self.prot], [ps2])
                    t = tmp[ci % 2]
                    self.tt(t.ap[:, 0:NT], ps2.ap[:, 0:NT], rst.ap[:, 0:NT], ALU.mult, [ps2, rst], [t])
                    self.tt(rr, rr, rct.ap[:, 0:NT], ALU.mult, [raw[ci], rct], [raw[ci]], eng="pool")
                    self.tt(rr, rr, t.ap[:, 0:NT], ALU.add, [raw[ci], t], [raw[ci]])

    def gate_math(self, graw, nch, ng, Al_ap, dt_ap, nA_ap, pbufs, beta, g, wk):
        ngh = ng // 2
        g3 = graw.ap[:, 0:nch * ng].rearrange("p (c g) -> p c g", g=ng)
        b3 = g3[:, :, 0:ngh]
        a3 = g3[:, :, ngh:ng]
        def v3(buf):
            return buf.ap[:, 0:nch * ngh].rearrange("p (c g) -> p c g", g=ngh)
        be, gg, e, u = v3(beta), v3(g), v3(wk[0]), v3(wk[1])
        dt3 = dt_ap.rearrange("p (c g) -> p c g", g=ngh)
        nA3 = nA_ap.rearrange("p (c g) -> p c g", g=ngh)
        self.act(be, b3, AF.Exp, [graw], [beta], scale=-1.0)
        self.ts(be, be, 1.0, None, ALU.add, None, [beta], [beta])
        self.recip(be, be, [beta], [beta])
        self.tt(e, a3, dt3, ALU.add, [graw] + pbufs, [wk[0]])
        self.act(e, e, AF.Exp, [wk[0]], [wk[0]])
        self.ts(u, e, 1.0, None, ALU.add, None, [wk[0]], [wk[1]])
        self.act(gg, u, AF.Ln, [wk[1]], [g])
        self.ts(u, u, -1.0, 1e-30, ALU.add, ALU.max, [wk[1]], [wk[1]])
        self.recip(u, u, [wk[1]], [wk[1]])
        self.tt(gg, gg, e, ALU.mult, [g, wk[0]], [g])
        self.tt(gg, gg, u, ALU.mult, [g, wk[1]], [g])
        self.tt(gg, gg, nA3, ALU.mult, [g] + pbufs, [g])

    def new_chain(self, idx, name):
        cx = Chain()
        A = self.arena
        cx.ps = self.cps[idx]
        cx.idx = idx
        cx.psi = 0
        cx.S = A.alloc(128, name + "S")
        cx.w = {k: A.alloc(128, name + k) for k in
                ["gB", "X", "A0", "A1", "R", "kbg", "kdec", "vb", "u", "wT", "vn", "qd", "X2", "qkT"]}
        cx.w["BR0"] = A.alloc(256, name + "BR0")
        cx.w["BR1"] = A.alloc(256, name + "BR1")
        cx.c = A.alloc(8, name + "cols")
        return cx

    def cps_next(self, cx):
        cx.psi = (cx.psi + 1) % 4
        return cx.ps[cx.psi]

    def gdn_chunk(self, cx, kT, vT, qT, srcbufs, bcol, gcol, gbufs, fwd, out_ap=None, out_buf=None):
        W = cx.w
        cm = self.cm
        if fwd:
            Ud, Ms, MiT, last = self.Uinc, self.Lst, self.Uinc, 127
        else:
            Ud, Ms, MiT, last = self.Linc, self.Ust, self.Linc, 0
        gcc, glc, gll, ekd, egc, bg = [cx.c.ap[:, i:i + 1] for i in range(6)]
        C = cx.c
        P = cx.ps
        PB = P[0]
        bank = self.PS[4 + cx.idx]
        p01 = bank.ap[:, 0:256]
        BR = [W["BR0"], W["BR1"]]
        AA = [W["A0"], W["A1"]]
        self.ts(W["gB"].ap, self.Ones, gcol, None, ALU.mult, None, [cm] + gbufs, [W["gB"]], eng="pool")
        pgc = P[2]
        self.mm(pgc.ap, W["gB"].ap, Ud, True, True, [W["gB"], cm], [PB])
        yield
        self.tt(W["X2"].ap, pgc.ap, self.Id, ALU.mult, [PB, cm], [W["X2"]])
        self.S.add("dve", lambda E, o=gcc, i=W["X2"].ap: E.reduce_sum(out=o, in_=i, axis=mybir.AxisListType.X),
                   [W["X2"]], [C])
        self.act(gll, pgc.ap[:, last:last + 1], AF.Copy, [PB], [C])
        self.act(glc, pgc.ap[:, last:last + 1], AF.Exp, [PB], [C])
        self.act(ekd, gcc, AF.Exp, [C], [C], scale=-1.0, bias=gll)
        self.act(egc, gcc, AF.Exp, [C], [C])
        self.tt(bg, egc, bcol, ALU.mult, [C] + gbufs, [C])
        self.ts(W["X"].ap, pgc.ap, gcc, 0.0, ALU.subtract, ALU.max, [PB, C], [W["X"]])
        self.act(W["X"].ap, W["X"].ap, AF.Exp, [W["X"]], [W["X"]], scale=-1.0)
        if qT is not None:
            self.act(W["qd"].ap, pgc.ap, AF.Exp, [PB], [W["qd"]])
            self.tt(W["qd"].ap, W["qd"].ap, qT, ALU.mult, [W["qd"]] + srcbufs, [W["qd"]], eng="pool")
            self.ts(W["X2"].ap, pgc.ap, gcc, 0.0, ALU.subtract, ALU.min, [PB, C], [W["X2"]])
            self.act(W["X2"].ap, W["X2"].ap, AF.Exp, [W["X2"]], [W["X2"]])
            self.tt(W["X2"].ap, W["X2"].ap, MiT, ALU.mult, [W["X2"], cm], [W["X2"]], eng="pool")
            pqk = P[0]
            self.mm(pqk.ap, kT, qT, True, True, srcbufs, [PB])
            yield
            self.tt(W["qkT"].ap, pqk.ap, W["X2"].ap, ALU.mult, [PB, W["X2"]], [W["qkT"]])
        pkk = P[1]
        self.mm(pkk.ap, kT, kT, True, True, srcbufs, [PB])
        yield
        self.tt(AA[0].ap, pkk.ap, W["X"].ap, ALU.mult, [PB, W["X"]], [AA[0]])
        self.stt(AA[0].ap, AA[0].ap, bcol, Ms, ALU.mult, ALU.mult, [AA[0], cm] + gbufs, [AA[0]])
        pt = P[0]
        self.tr(pt.ap, AA[0].ap, [AA[0]], [PB])
        yield
        self.act(BR[0].ap[:, 0:128], pt.ap, AF.Copy, [PB], [BR[0]])
        self.tt(BR[1].ap[:, 128:256], self.Id, pt.ap, ALU.subtract, [cm, PB], [BR[1]])
        self.mm(P[0].ap, AA[0].ap, BR[0].ap[:, 0:128], True, True, [AA[0], BR[0]], [PB])
        yield
        self.act(BR[1].ap[:, 0:128], P[0].ap, AF.Copy, [PB], [BR[1]])
        self.mm(P[3].ap, BR[0].ap[:, 0:128], AA[0].ap, True, True, [AA[0], BR[0]], [PB])
        yield
        self.cp(AA[1].ap, P[3].ap, [PB], [AA[1]])
        for m in range(1, 6):
            cur, nxt = BR[m % 2], BR[(m + 1) % 2]
            Am, An = AA[m % 2], AA[(m + 1) % 2]
            if m < 5:
                self.mm(p01, Am.ap, cur.ap[:, 0:256], True, True, [Am, cur], [PB])
                yield
                self.act(nxt.ap[:, 0:128], bank.ap[:, 0:128], AF.Copy, [PB], [nxt])
                self.tt(nxt.ap[:, 128:256], cur.ap[:, 128:256], bank.ap[:, 128:256], ALU.add, [cur, PB], [nxt])
            else:
                self.mm(bank.ap[:, 128:256], Am.ap, cur.ap[:, 128:256], True, True, [Am, cur], [PB])
                yield
                self.tt(nxt.ap[:, 128:256], cur.ap[:, 128:256], bank.ap[:, 128:256], ALU.add, [cur, PB], [nxt])
            self.mm(P[3].ap, cur.ap[:, 0:128], Am.ap, True, True, [Am, cur], [PB])
            yield
            if m % 2 == 0:
                self.act(An.ap, P[3].ap, AF.Copy, [PB], [An])
            else:
                self.cp(An.ap, P[3].ap, [PB], [An])
        R5 = BR[0]
        self.mm(P[1].ap, AA[0].ap, R5.ap[:, 128:256], True, True, [AA[0], R5], [PB])
        yield
        R = W["R"]
        self.tt(R.ap, R5.ap[:, 128:256], P[1].ap, ALU.add, [R5, PB], [R])
        pk = P[2]
        self.tr(pk.ap, kT, srcbufs, [PB])
        yield
        self.act(W["kbg"].ap, pk.ap, AF.Copy, [PB, C], [W["kbg"]], scale=bg)
        self.ts(W["kdec"].ap, pk.ap, ekd, None, ALU.mult, None, [PB, C], [W["kdec"]])
        pv = P[3]
        self.tr(pv.ap, vT, srcbufs, [PB])
        yield
        self.act(W["vb"].ap, pv.ap, AF.Copy, [PB] + gbufs, [W["vb"]], scale=bcol)
        self.mm(P[0].ap, R.ap, W["vb"].ap, True, True, [R, W["vb"]], [PB])
        yield
        self.act(W["u"].ap, P[0].ap, AF.Copy, [PB], [W["u"]])
        self.mm(P[1].ap, W["kbg"].ap, R.ap, True, True, [R, W["kbg"]], [PB])
        yield
        self.cp(W["wT"].ap, P[1].ap, [PB], [W["wT"]])
        self.mm(P[2].ap, W["wT"].ap, cx.S.ap, True, True, [W["wT"], cx.S], [PB])
        yield
        self.tt(W["vn"].ap, W["u"].ap, P[2].ap, ALU.subtract, [W["u"], PB], [W["vn"]])
        if qT is not None:
            po = P[3]
            self.mm(po.ap, cx.S.ap, W["qd"].ap, True, False, [cx.S, W["qd"]], [PB])
            self.mm(po.ap, W["vn"].ap, W["qkT"].ap, False, True, [W["vn"], W["qkT"]], [PB])
            yield
            self.tt(out_ap, out_ap, po.ap, ALU.add, [out_buf, PB], [out_buf])
        self.mm(P[0].ap, W["kdec"].ap, W["vn"].ap, True, True, [W["kdec"], W["vn"]], [PB])
        yield
        self.stt(cx.S.ap, cx.S.ap, glc, P[0].ap, ALU.mult, ALU.add, [cx.S, C, PB], [cx.S])

    def run_rr(self, gens):
        gens = list(gens)
        while gens:
            for g in list(gens):
                try:
                    next(g)
                except StopIteration:
                    gens.remove(g)

    def load_w(self, buf, col0, ncol, r=False, src="w_in"):
        for kc in range(8):
            srcap = self.din[src][kc * 128:(kc + 1) * 128, col0:col0 + ncol]
            if not r:
                self.dma(buf.ap[:, kc * ncol:(kc + 1) * ncol], srcap, uw=[buf])
            else:
                self.stage_i = (self.stage_i + 1) % 2
                stg = self.stage[self.stage_i]
                self.dma(stg.ap[:, 0:ncol], srcap, w=[stg])
                if self.stage_i == 0:
                    self.act(buf.ap[:, kc * ncol:(kc + 1) * ncol], stg.ap[:, 0:ncol], AF.Copy, [stg], [buf])
                else:
                    self.cp(buf.ap[:, kc * ncol:(kc + 1) * ncol], stg.ap[:, 0:ncol], [stg], [buf])

    def wres(self, buf, ncol, ci):
        return (lambda kc: buf.ap[:, kc * ncol + ci * 128: kc * ncol + (ci + 1) * 128]), [buf]

    def phase_ctx(self):
        P = self.parena
        din = self.din
        self.kcT = [self.prarena.alloc(256, f"kcT{j}") for j in range(4)]
        self.vca = [self.prarena.alloc(8 * 128, f"vca{t}") for t in range(2)]
        self.Sfc = [P.alloc(128, f"Sfc{h}") for h in range(4)]
        self.Sbc = [P.alloc(128, f"Sbc{h}") for h in range(4)]
        self.Sf = [P.alloc(128, f"Sf{h}") for h in range(4)]
        self.Sb = [P.alloc(128, f"Sb{h}") for h in range(4)]
        self.omseg = P.alloc(8, "omseg")
        self.ts(self.omseg.ap, self.mseg.ap, -1.0, 1.0, ALU.mult, ALU.add, [self.mseg], [self.omseg])
        self.begin_phase(21000, 22000)
        A = self.arena
        RA = self.rarena
        self.chains = [self.new_chain(i, f"cx{i}") for i in range(4)]
        xsets, sq, rs = self.alloc_xt(512, nbuf=1, r=True)
        wkv = RA.alloc(8 * 1024, "wkv")
        self.load_w(wkv, 2560, 1024, r=True)
        wg = RA.alloc(8 * 16, "wg")
        self.load_w(wg, 4096, 16, r=True)
        wna = RA.alloc(8 * 1024, "wna")
        self.load_w(wna, 512, 1024, r=True)
        raw = [A.alloc(CTXL + 4, f"rawc{i}") for i in range(8)]
        graw = A.alloc(32, "graw")
        beta = A.alloc(16, "beta")
        g = A.alloc(16, "g")
        wk = [A.alloc(16, "gwk0"), A.alloc(16, "gwk1")]
        tmp = [A.alloc(512, "tmp0"), A.alloc(512, "tmp1")]

        def extra(j, xt, NT):
            for jj in range(4):
                def ev(ps, jj=jj):
                    self.act(self.kcT[jj].ap, ps.ap[:, 0:256], AF.Copy, [ps], [self.kcT[jj]])
                self.proj_fm(xt, 256, lambda kc, jj=jj: wna.ap[:, kc * 1024 + jj * 128: kc * 1024 + (jj + 1) * 128], 128,
                             [wna], ev)
            for t in range(2):
                ps = self.nps()
                for kc in range(8):
                    self.mm(ps.ap, xt[kc].ap[:, t * 128:(t + 1) * 128], wna.ap[:, kc * 1024 + 512: kc * 1024 + 1024],
                            kc == 0, kc == 7, [xt[kc], wna], [ps])
                v3 = self.vca[t].ap.rearrange("p (h c) -> p h c", c=128)
                p3 = ps.ap.rearrange("p (h c) -> p h c", c=64)
                self.act(v3[:, :, 0:64], p3, AF.Copy, [ps], [self.vca[t]])
                self.act(v3[:, :, 64:128], p3, AF.Identity, [ps], [self.vca[t]], scale=0.0, bias=1.0)

        self.prep_unit(lambda j, NT: [din["ctxT"][kc, :, :] for kc in range(8)], None, CTXL, 1,
                       ["k"] * 4 + ["v"] * 4, lambda ci: self.wres(wkv, 1024, ci), raw, self.convO, 20, None, None,
                       lambda kc: wg.ap[:, kc * 16:(kc + 1) * 16], [wg], 16, graw, None, xsets, sq, rs, tmp, None, None,
                       extra=extra)
        self.gate_math(graw, 2, 16, None, self.dtO.ap[:, 0:16], self.nAO.ap[:, 0:16], [self.dtO, self.nAO], beta, g, wk)
        self.tap("ctx_beta", beta.ap, [beta], 16)
        self.tap("ctx_g", g.ap, [g], 16)
        self.tap("ctx_k0", raw[0].ap[:, 0:256], [raw[0]], 256)
        self.tap("ctx_v0", raw[4].ap[:, 0:256], [raw[4]], 256)
        for dr in range(2):
            for h in range(4):
                self.memset(self.chains[h].S.ap, 0.0, [self.chains[h].S])
            for c in ([0, 1] if dr == 0 else [1, 0]):
                gens = []
                for h in range(4):
                    idx = c * 8 + dr * 4 + h
                    gens.append(self.gdn_chunk(self.chains[h], raw[h].ap[:, c * 128:(c + 1) * 128],
                                               raw[4 + h].ap[:, c * 128:(c + 1) * 128],
                                               None, [raw[h], raw[4 + h]], beta.ap[:, idx:idx + 1], g.ap[:, idx:idx + 1],
                                               [beta, g], dr == 0))
                self.run_rr(gens)
            dst = self.Sfc if dr == 0 else self.Sbc
            for h in range(4):
                self.cp(dst[h].ap, self.chains[h].S.ap, [self.chains[h].S], [dst[h]], eng="pool")
        self.tap("Sfc0", self.Sfc[0].ap, [self.Sfc[0]], 128)
        self.tap("Sbc0", self.Sbc[0].ap, [self.Sbc[0]], 128)
        self.end_phase()

    def switch(self, j):
        mj = self.mseg.ap[:, j:j + 1]
        oj = self.omseg.ap[:, j:j + 1]
        for h in range(4):
            cx = self.chains[h]
            self.stt(self.Sf[h].ap, cx.S.ap, mj, self.Sf[h].ap, ALU.mult, ALU.add, [cx.S, self.mseg, self.Sf[h]], [self.Sf[h]])
            t = cx.w["vn"]
            self.ts(t.ap, self.Sbc[h].ap, mj, None, ALU.mult, None, [self.Sbc[h], self.mseg], [t], eng="pool")
            self.stt(cx.S.ap, cx.S.ap, oj, t.ap, ALU.mult, ALU.add, [cx.S, self.omseg, t], [cx.S])

    def phase_stream(self):
        din = self.din
        self.begin_phase(25100, 17600)
        A = self.arena
        RA = self.rarena
        self.chains = [self.new_chain(i, f"cs{i}") for i in range(4)]
        xsets, sq, rs = self.alloc_xt(512, nbuf=2, r=True)
        wkv = RA.alloc(8 * 1024, "wkv")
        self.load_w(wkv, 2560, 1024, r=True)
        wgs_f = A.alloc(64, "wgs_f")
        wgs = RA.alloc(64, "wgs")
        convs = A.alloc(40, "convs")
        raw = [A.alloc(SU + 4, f"raws{i}") for i in range(8)]
        graw = A.alloc(64, "graw")
        beta = A.alloc(32, "beta")
        g = A.alloc(32, "g")
        wk = [A.alloc(32, "gwk0"), A.alloc(32, "gwk1")]
        tmp = [Buf(self.stage[0].ap[:, 0:512], "tmp0", key=self.stage[0].key),
               Buf(self.stage[0].ap[:, 512:1024], "tmp1", key=self.stage[0].key)]
        rct = Buf(self.stage[1].ap[:, 0:512], "rct", key=self.stage[1].key)
        rst = Buf(self.stage[1].ap[:, 512:1024], "rst", key=self.stage[1].key)
        for h in range(4):
            self.cp(self.chains[h].S.ap, self.Sfc[h].ap, [self.Sfc[h]], [self.chains[h].S], eng="pool")
            self.memset(self.Sf[h].ap, 0.0, [self.Sf[h]])
        def prologue(u):
            hal = [(din["xsH"][u, kc, :, 0:2], din["xsH"][u, kc, :, 2:4]) for kc in range(8)]
            self.gen_h(hal, 4, 0, xsets.x, sq, rs)
            self.gen_h([din["xsT"][u, kc, :, 0:512] for kc in range(8)], 512, 0, xsets[0], sq, rs)

        prologue(0)
        for u in range(NSU):
            if u % 2 == 0:
                self.switch(u // 2)
            for kc in range(8):
                self.dma(wgs_f.ap[:, kc * 8:(kc + 1) * 8], din["w_gs"][u, kc * 128:(kc + 1) * 128, :], uw=[wgs_f])
            self.act(wgs.ap, wgs_f.ap, AF.Copy, [wgs_f], [wgs])
            self.dma(convs.ap, din["convS"][u], w=[convs])
            halo = [(din["xsH"][u, kc, :, 0:2], din["xsH"][u, kc, :, 2:4]) for kc in range(8)]
            self.prep_unit(lambda j, NT, u=u: [din["xsT"][u, kc, :, j * NT:(j + 1) * NT] for kc in range(8)], halo, SU, 0,
                           ["k"] * 4 + ["v"] * 4, lambda ci: self.wres(wkv, 1024, ci), raw, convs, 0,
                           self.hmS.ap[:, u * 4:(u + 1) * 4], self.hmS,
                           lambda kc: wgs.ap[:, kc * 8:(kc + 1) * 8], [wgs], 8, graw, u * SU, xsets, sq, rs, tmp, rct, rst,
                           pro=True)
            self.gate_math(graw, 8, 8, None, self.dtS.ap[:, u * 32:(u + 1) * 32], self.nAS.ap[:, u * 32:(u + 1) * 32],
                           [self.dtS, self.nAS], beta, g, wk)
            if u + 1 < NSU:
                prologue(u + 1)
            for c in range(8):
                gens = []
                for h in range(4):
                    idx = c * 4 + h
                    gens.append(self.gdn_chunk(self.chains[h], raw[h].ap[:, c * 128:(c + 1) * 128],
                                               raw[4 + h].ap[:, c * 128:(c + 1) * 128], None, [raw[h], raw[4 + h]],
                                               beta.ap[:, idx:idx + 1], g.ap[:, idx:idx + 1], [beta, g], True))
                self.run_rr(gens)
            if u == 0:
                self.tap("s0_k0", raw[0].ap[:, 0:256], [raw[0]], 256)
                self.tap("s0_beta", beta.ap, [beta], 32)
                self.tap("s0_g", g.ap, [g], 32)
                self.tap("s0_S0", self.chains[0].S.ap, [self.chains[0].S], 128)
        self.switch(7)
        for h in range(4):
            self.cp(self.Sb[h].ap, self.chains[h].S.ap, [self.chains[h].S], [self.Sb[h]], eng="pool")
        self.tap("Sf0", self.Sf[0].ap, [self.Sf[0]], 128)
        self.tap("Sb0", self.Sb[0].ap, [self.Sb[0]], 128)
        self.end_phase()

    def phase_own_gdn(self):
        din = self.din
        ROPE_OWN = NSU * SU
        for p in range(2):
            self.begin_phase(43200, 0)
            A = self.arena
            self.chains = [self.new_chain(i, f"co{i}") for i in range(4)]
            xsets, sq, rs = self.alloc_xt(512, nbuf=1)
            wt = [A.alloc(1024, f"wt{i}") for i in range(2)]
            wz = A.alloc(8 * 256, "wz")
            self.load_w(wz, 3584 + 256 * p, 256)
            wg = A.alloc(8 * 16, "wg")
            self.load_w(wg, 4096, 16)
            raw = [A.alloc(SEGT + 4, f"rawo{i}") for i in range(6)]
            oT = [A.alloc(SEGT, f"oT{i}") for i in range(2)]
            gz = [A.alloc(SEGT, f"gz{i}") for i in range(2)]
            graw = A.alloc(256, "graw")
            beta = A.alloc(128, "beta")
            g = A.alloc(128, "g")
            wk = [A.alloc(128, "gwk0"), A.alloc(128, "gwk1")]
            tmp = [A.alloc(512, "tmp0"), A.alloc(512, "tmp1")]
            rct = A.alloc(512, "rct")
            rst = A.alloc(512, "rst")
            cols = [2048 + 128 * (2 * p), 2048 + 128 * (2 * p + 1), 2560 + 128 * (2 * p), 2560 + 128 * (2 * p + 1),
                    3072 + 128 * (2 * p), 3072 + 128 * (2 * p + 1)]
            self.wti = 0

            def wget(ci):
                self.wti = (self.wti + 1) % 2
                b = wt[self.wti]
                self.load_w(b, cols[ci], 128)
                return (lambda kc, b=b: b.ap[:, kc * 128:(kc + 1) * 128]), [b]

            def extra(j, xt, NT):
                for hh in range(2):
                    def ev(ps, hh=hh, j=j):
                        self.act(gz[hh].ap[:, j * NT:(j + 1) * NT], ps.ap[:, 0:NT], AF.Silu, [ps], [gz[hh]])
                    self.proj_fm(xt, NT, lambda kc, hh=hh: wz.ap[:, kc * 256 + hh * 128: kc * 256 + (hh + 1) * 128], 128,
                                 [wz], ev)

            halo = [(din["xwT"][kc, :, 254:256], din["xwT"][kc, :, 2304:2306]) for kc in range(8)]
            convp = A.alloc(30, "convp")
            for ci, base in enumerate([2 * p, 2 * p + 1, 4 + 2 * p, 5 + 2 * p, 8 + 2 * p, 9 + 2 * p]):
                self.cp(convp.ap[:, ci * 5:(ci + 1) * 5], self.convO.ap[:, base * 5:(base + 1) * 5], [self.convO], [convp],
                        eng="pool")
            self.prep_unit(lambda j, NT: [din["xwT"][kc, :, 256 + j * NT:256 + (j + 1) * NT] for kc in range(8)], halo, SEGT, 0,
                           ["q", "q", "k", "k", "v", "v"], wget, raw, convp, 0, self.hmO.ap[:, 0:4], self.hmO,
                           lambda kc: wg.ap[:, kc * 16:(kc + 1) * 16], [wg], 16, graw, ROPE_OWN, xsets, sq, rs, tmp, rct, rst,
                           extra=extra)
            self.gate_math(graw, 16, 16, None, self.dtO.ap, self.nAO.ap, [self.dtO, self.nAO], beta, g, wk)
            if p == 0:
                self.tap("own_q0", raw[0].ap[:, 0:256], [raw[0]], 256)
                self.tap("own_k0", raw[2].ap[:, 0:256], [raw[2]], 256)
            chains = self.chains
            for hh in range(2):
                h = 2 * p + hh
                self.cp(chains[2 * hh].S.ap, self.Sf[h].ap, [self.Sf[h]], [chains[2 * hh].S], eng="pool")
                self.cp(chains[2 * hh + 1].S.ap, self.Sb[h].ap, [self.Sb[h]], [chains[2 * hh + 1].S], eng="pool")
                self.memset(oT[hh].ap, 0.0, [oT[hh]])
            for s in range(16):
                gens = []
                for hh in range(2):
                    h = 2 * p + hh
                    for dr in range(2):
                        c = s if dr == 0 else 15 - s
                        idx = c * 8 + dr * 4 + h
                        sl = slice(c * 128, (c + 1) * 128)
                        gens.append(self.gdn_chunk(chains[2 * hh + dr], raw[2 + hh].ap[:, sl], raw[4 + hh].ap[:, sl],
                                                   raw[hh].ap[:, sl], [raw[hh], raw[2 + hh], raw[4 + hh]],
                                                   beta.ap[:, idx:idx + 1], g.ap[:, idx:idx + 1],
                                                   [beta, g], dr == 0, out_ap=oT[hh].ap[:, sl], out_buf=oT[hh]))
                self.run_rr(gens)
            for hh in range(2):
                h = 2 * p + hh
                for j in range(4):
                    o = oT[hh].ap[:, j * 512:(j + 1) * 512]
                    sqb = sq[j % 2]
                    self.act(sqb.ap, o, AF.Square, [oT[hh]], [sqb])
                    ps = self.PS[2]
                    self.mm(ps.ap, self.Ones, sqb.ap, True, True, [sqb, self.cm], [ps])
                    self.act(rs.ap, ps.ap, AF.Sqrt, [ps], [rs], scale=1.0 / 128, bias=EPS)
                    self.recip(rs.ap, rs.ap, [rs], [rs])
                    self.stt(o, o, self.gnw.ap[:, 0:1], rs.ap, ALU.mult, ALU.mult, [oT[hh], self.gnw, rs], [oT[hh]])
                    self.tt(o, o, gz[hh].ap[:, j * 512:(j + 1) * 512], ALU.mult, [oT[hh], gz[hh]], [oT[hh]], eng="pool")
                ob = Buf(None, "scrw")
                self.dma(self.scr[4 + h], oT[hh].ap, r=[oT[hh]], w=[ob])
                self.scr_bufs.append(ob)
                if h == 0:
                    self.tap("og0", oT[0].ap[:, 0:256], [oT[0]], 256)
            self.end_phase()

    def phase_natten(self):
        din = self.din
        psw = [(self.PS[2], self.PS[3], self.PSWt[1][:, :]), (self.PS[4], self.PS[5], self.PSWt[2][:, :])]
        for p in range(4):
            self.begin_phase(17000, 25300)
            A = self.arena
            RA = self.rarena
            xsets, sq, rs = self.alloc_xt(512, nbuf=2, r=True)
            wq = RA.alloc(1024, "wq"); wk_ = RA.alloc(1024, "wk"); wv = RA.alloc(1024, "wv"); wz = RA.alloc(1024, "wz")
            self.load_w(wq, 128 * p, 128, r=True)
            self.load_w(wk_, 512 + 128 * p, 128, r=True)
            self.load_w(wv, 1024 + 128 * p, 128, r=True)
            self.load_w(wz, 1536 + 128 * p, 128, r=True)
            kT = RA.alloc(WIN, "kT")
            qT = RA.alloc(SEGT, "qT")
            zT = A.alloc(SEGT, "zT")
            on = zT
            otmp = [A.alloc(128, f"otmp{i}") for i in range(2)]
            va = [RA.alloc(256, f"va{t}") for t in range(20)]
            bias = [A.alloc(3840, f"nab{hh}") for hh in range(2)]
            Eb = [RA.alloc(1024, f"Eb{i}") for i in range(2)]
            rc = [A.alloc(128, f"rc{i}") for i in range(2)]
            for hh in range(2):
                self.dma(bias[hh].ap, din["nab"][2 * p + hh], w=[bias[hh]])
            cur = xsets[0]
            self.gen_h([din["xwT"][kc, :, 0:512] for kc in range(8)], 512, 0, cur, sq, rs)
            for j in range(5):
                xt = cur
                if j + 1 < 5:
                    cur = xsets[(j + 1) % 2]
                    self.gen_h([din["xwT"][kc, :, (j + 1) * 512:(j + 2) * 512] for kc in range(8)], 512, 0, cur, sq, rs)
                def evk(ps, j=j):
                    self.act(kT.ap[:, j * 512:(j + 1) * 512], ps.ap, AF.Copy, [ps], [kT])
                self.proj_fm(xt, 512, lambda kc: wk_.ap[:, kc * 128:(kc + 1) * 128], 128, [wk_], evk)
                for t in range(4):
                    ps = self.nps()
                    for kc in range(8):
                        self.mm(ps.ap[:, 0:128], xt[kc].ap[:, t * 128:(t + 1) * 128], wv.ap[:, kc * 128:(kc + 1) * 128],
                                kc == 0, kc == 7, [xt[kc], wv], [ps])
                    vb = va[4 * j + t]
                    v3 = vb.ap.rearrange("p (h c) -> p h c", c=128)
                    p3 = ps.ap[:, 0:128].rearrange("p (h c) -> p h c", c=64)
                    self.act(v3[:, :, 0:64], p3, AF.Copy, [ps], [vb])
                    self.act(v3[:, :, 64:128], p3, AF.Identity, [ps], [vb], scale=0.0, bias=1.0)
                lo = max(j * 512, 256)
                hi = min((j + 1) * 512, 2304)
                n = hi - lo
                def evq(ps, lo=lo, n=n):
                    self.act(qT.ap[:, lo - 256:lo - 256 + n], ps.ap[:, 0:n], AF.Copy, [ps], [qT])
                self.proj_fm(xt, n, lambda kc: wq.ap[:, kc * 128:(kc + 1) * 128], 128, [wq], evq, c0=lo - j * 512)
                def evz(ps, lo=lo, n=n):
                    self.act(zT.ap[:, lo - 256:lo - 256 + n], ps.ap[:, 0:n], AF.Silu, [ps], [zT])
                self.proj_fm(xt, n, lambda kc: wz.ap[:, kc * 128:(kc + 1) * 128], 128, [wz], evz, c0=lo - j * 512)
            iters = [(gq, hh) for gq in range(16) for hh in range(2)]

            def emit_scores(it):
                gq, hh = iters[it]
                kt0 = min(gq, 14)
                hp = 64 * hh
                pa, pb, pw = psw[it % 2]
                qs = qT.ap[hp:hp + 64, gq * 128:(gq + 1) * 128]
                for t in range(6):
                    self.mm(pw[:, t * 128:(t + 1) * 128], kT.ap[hp:hp + 64, (kt0 + t) * 128:(kt0 + t + 1) * 128], qs,
                            True, True, [kT, qT], [pa, pb])
                for t in range(2):
                    self.mm(pw[:, (6 + t) * 128:(7 + t) * 128], self.kcT[p].ap[hp:hp + 64, t * 128:(t + 1) * 128], qs,
                            True, True, [self.kcT[p], qT], [pa, pb])

            def emit_rest(it):
                gq, hh = iters[it]
                kt0 = min(gq, 14)
                cls = {0: 0, 1: 1, 14: 3, 15: 4}.get(gq, 2)
                hp = 64 * hh
                pa, pb, pw = psw[it % 2]
                eb = Eb[it % 2]
                rcb = rc[it % 2]
                self.stt(eb.ap[:, 0:768], pw[:, 0:768], 0.125, bias[hh].ap[:, cls * 768:(cls + 1) * 768], ALU.mult, ALU.add,
                         [pa, pb, bias[hh]], [eb])
                self.act(eb.ap[:, 0:768], eb.ap[:, 0:768], AF.Exp, [eb], [eb])
                self.act(eb.ap[:, 768:1024], pw[:, 768:1024], AF.Exp, [pa, pb], [eb], scale=0.125)
                po = self.nps()
                for t in range(6):
                    self.mm(po.ap[:, 0:128], va[kt0 + t].ap[:, hh * 128:(hh + 1) * 128], eb.ap[:, t * 128:(t + 1) * 128],
                            t == 0, False, [va[kt0 + t], eb], [po])
                for t in range(2):
                    h = 2 * p + hh
                    self.mm(po.ap[:, 0:128], self.vca[t].ap[:, h * 128:(h + 1) * 128], eb.ap[:, (6 + t) * 128:(7 + t) * 128],
                            False, t == 1, [self.vca[t], eb], [po])
                self.recip(rcb.ap[0:64, :], po.ap[64:128, 0:128], [po], [rcb])
                ot = otmp[it % 2]
                self.tt(ot.ap[hp:hp + 64, :], po.ap[0:64, 0:128], rcb.ap[0:64, :], ALU.mult, [po, rcb], [ot])
                zs = zT.ap[hp:hp + 64, gq * 128:(gq + 1) * 128]
                self.tt(zs, zs, ot.ap[hp:hp + 64, :], ALU.mult, [zT, ot], [zT])

            emit_scores(0)
            for it in range(len(iters)):
                if it + 1 < len(iters):
                    emit_scores(it + 1)
                emit_rest(it)
            if p == 0:
                self.tap("na_o0", on.ap[:, 0:256], [on], 256)
            ob = Buf(None, "scrw")
            self.dma(self.scr[p], on.ap, r=[on], w=[ob])
            self.scr_bufs.append(ob)
            self.end_phase()

    def phase_final(self):
        din = self.din
        self.begin_phase(16500, 16600)
        A = self.arena
        RA = self.rarena
        wo = RA.alloc(8 * 1024, "wo")
        self.load_w(wo, 0, 1024, r=True, src="w_out")
        gtf = [[A.alloc(512, f"gtf{i}_{kc}") for kc in range(8)] for i in range(2)]
        gt = [[RA.alloc(512, f"gt{i}_{kc}") for kc in range(8)] for i in range(2)]
        ysb = [A.alloc(1024, f"ysb{i}") for i in range(2)]
        ysq = A.alloc(1024, "ysq")
        xo = [A.alloc(1024, f"xo{i}") for i in range(2)]
        ss = A.alloc(4, "ss")
        for jt in range(4):
            g8 = gt[jt % 2]
            g8f = gtf[jt % 2]
            for kc in range(8):
                self.dma(g8f[kc].ap, self.scr[kc, :, jt * 512:(jt + 1) * 512], r=self.scr_bufs, w=[g8f[kc]])
                if kc % 2 == 0:
                    self.act(g8[kc].ap, g8f[kc].ap, AF.Copy, [g8f[kc]], [g8[kc]])
                else:
                    self.cp(g8[kc].ap, g8f[kc].ap, [g8f[kc]], [g8[kc]])
            for s in range(4):
                t = jt * 4 + s
                y = ysb[t % 2]
                x_ = xo[t % 2]
                self.dma(x_.ap, din["xown"][t * 128:(t + 1) * 128, :], w=[x_])
                for n in range(2):
                    ps = self.nps()
                    for kc in range(8):
                        self.mm(ps.ap, g8[kc].ap[:, s * 128:(s + 1) * 128], wo.ap[:, kc * 1024 + n * 512: kc * 1024 + (n + 1) * 512],
                                kc == 0, kc == 7, [g8[kc], wo], [ps])
                    self.act(y.ap[:, n * 512:(n + 1) * 512], ps.ap, AF.Copy, [ps], [y])
                self.act(ysq.ap, y.ap, AF.Square, [y], [ysq])
                self.S.add("dve", lambda E, o=ss.ap[:, 0:1], i=ysq.ap: E.reduce_sum(out=o, in_=i, axis=mybir.AxisListType.X),
                           [ysq], [ss])
                self.act(ss.ap[:, 1:2], ss.ap[:, 0:1], AF.Sqrt, [ss], [ss], scale=1.0 / D, bias=EPS)
                self.recip(ss.ap[:, 2:3], ss.ap[:, 1:2], [ss], [ss])
                self.stt(y.ap, y.ap, ss.ap[:, 2:3], self.G2.ap, ALU.mult, ALU.mult, [y, ss, self.G2], [y])
                self.tt(y.ap, y.ap, x_.ap, ALU.add, [y, x_], [y], eng="pool")
                ob = Buf(None, "outw")
                self.dma(self.out[t * 128:(t + 1) * 128, :], y.ap, r=[y], w=[ob])
                self.outs.append(ob)
        self.phase_stack.close()


def build_program(stop="full", dbg=False):
    b = Builder(stop=stop, dbg=dbg)
    b.scr_bufs = []
    nc = b.build()
    return nc, b


_CACHE = {}


def kernel(**inputs):
    per_core = host_prep(inputs)
    nc, b = build_program()
    res = run_bass_kernel_spmd(nc, per_core, core_ids=list(range(NCORES)))
    out = np.concatenate([np.asarray(r["out"], np.float32) for r in res.results], 0)
    return out.reshape(1, SEQ, D).astype(np.float32)
```

```python
import os
import numpy as np
from contextlib import ExitStack
import concourse.bass as bass
import concourse.mybir as mybir
from concourse.bass_utils import run_bass_kernel_spmd

F32 = mybir.dt.float32
F32R = mybir.dt.float32r
AF = mybir.ActivationFunctionType
ALU = mybir.AluOpType

NCORES = 8
D = 1024
KC = 8
SEQ = 16384
SEGT = 2048
WIN = 2560
CTXL = 256
SU = 1024
NSU = 14
IN_DIM = 4112
EPS = 1e-6
NEG = -30000.0
PCOLS = 6400
PRCOLS = 3328
TOTCOLS = 53000

ENGS = ("pe", "act", "dve", "pool", "sp")


class Buf:
    _n = 0

    def __init__(self, ap, name="", key=None, excl=False):
        self.ap = ap
        if key is None:
            Buf._n += 1
            key = Buf._n
        self.key = key
        self.name = name
        self.excl = excl

    def v(self, a, b):
        return self.ap[:, a:b]

    def t(self, i, n):
        return self.ap[:, i * n:(i + 1) * n]


class Sched:
    def __init__(self, nc, n_dma_sems=24, sems_per_eng=4, blk=4096):
        self.nc = nc
        self.ops = []
        self.n_dma_sems = n_dma_sems
        self.sems_per_eng = sems_per_eng
        self.blk = blk
        self.BAR = Buf(None, "BAR")

    def add(self, eng, fn, reads=(), writes=(), dma=False, uwrites=()):
        r = [b.key for b in reads if not b.excl]
        r.append(self.BAR.key)
        wr = list(dict.fromkeys([b.key for b in writes] + [b.key for b in reads if b.excl]))
        self.ops.append(dict(eng=eng, fn=fn, reads=r, writes=wr, dma=dma,
                             uwrites=[b.key for b in uwrites]))

    def barrier(self):
        self.ops.append(dict(eng="pool", fn=None, reads=[], writes=[self.BAR.key], dma=False, uwrites=[]))

    def emit(self, stack):
        nc = self.nc
        ops = self.ops
        n = len(ops)
        last_w = {}
        readers = {}
        deps = [set() for _ in range(n)]
        unord = {}
        for i, op in enumerate(ops):
            for k in op["reads"]:
                deps[i].update(last_w.get(k, ()))
            for k in op["writes"]:
                deps[i].update(last_w.get(k, ()))
                deps[i].update(readers.get(k, ()))
            for k in op["uwrites"]:
                deps[i].update(readers.get(k, ()))
                if not (unord.get(k, False) and not readers.get(k)):
                    deps[i].update(last_w.get(k, ()))
            deps[i].discard(i)
            for k in op["reads"]:
                readers.setdefault(k, []).append(i)
            for k in op["writes"]:
                last_w[k] = [i]
                readers[k] = []
                unord[k] = False
            for k in op["uwrites"]:
                if unord.get(k, False) and not readers.get(k):
                    last_w[k].append(i)
                else:
                    last_w[k] = [i]
                    readers[k] = []
                    unord[k] = True
        need_sig = [False] * n
        for i, op in enumerate(ops):
            if op["dma"]:
                need_sig[i] = True
            for d in deps[i]:
                od = ops[d]
                if od["dma"] or op["dma"] or od["eng"] != op["eng"]:
                    need_sig[d] = True
                elif od["eng"] != "pe":
                    need_sig[d] = True
        for i, op in enumerate(ops):
            if op["fn"] is None and not op["dma"]:
                pass
        eng_sems = {e: [stack.enter_context(nc.semaphore(f"s_{e}{j}")) for j in range(self.sems_per_eng)]
                    for e in ENGS if e != "sp"}
        dma_sems = [stack.enter_context(nc.semaphore(f"s_dma{j}")) for j in range(self.n_dma_sems)]
        sig = [None] * n
        eng_cnt = {e: 0 for e in ENGS}
        dma_use = [0] * self.n_dma_sems
        dma_idx = 0
        dma_prev = [None] * n
        for i, op in enumerate(ops):
            if not need_sig[i]:
                continue
            if op["dma"]:
                s = dma_idx % self.n_dma_sems
                dma_idx += 1
                if dma_use[s] > 0:
                    dma_prev[i] = (dma_sems[s], 16 * dma_use[s])
                dma_use[s] += 1
                sig[i] = (dma_sems[s], 16 * dma_use[s], 16)
            else:
                e = op["eng"]
                c = eng_cnt[e]
                eng_cnt[e] += 1
                blk_i = c // self.blk
                s = blk_i % self.sems_per_eng
                val = (blk_i // self.sems_per_eng) * self.blk + (c % self.blk) + 1
                sig[i] = (eng_sems[e][s], val, 1)
        known = {e: {} for e in ENGS}
        waits = [[] for _ in range(n)]
        for i, op in enumerate(ops):
            e = op["eng"]
            wl = []
            if dma_prev[i] is not None:
                wl.append(dma_prev[i])
            for d in deps[i]:
                od = ops[d]
                if sig[d] is None:
                    continue
                if not od["dma"] and not op["dma"] and od["eng"] == e and e == "pe":
                    continue
                wl.append((sig[d][0], sig[d][1]))
            best = {}
            for (s, v) in wl:
                key = id(s)
                if v > known[e].get(key, 0):
                    if key not in best or best[key][1] < v:
                        best[key] = (s, v)
            for key, (s, v) in best.items():
                known[e][key] = v
                waits[i].append((s, v))
        self.n_waits = sum(len(w) for w in waits)
        self.n_sigs = sum(1 for s in sig if s is not None)
        with nc.Block() as block:
            def run(ename):
                def body(eng):
                    for i, op in enumerate(ops):
                        if op["eng"] != ename:
                            continue
                        for (s, v) in waits[i]:
                            eng.wait_ge(s, v)
                        if op["fn"] is None:
                            if sig[i] is not None:
                                eng.engine_nop().then_inc(sig[i][0], sig[i][2])
                            continue
                        ins = op["fn"](eng)
                        if sig[i] is not None:
                            ins.then_inc(sig[i][0], sig[i][2])
                return body
            block.sync(run("sp"))
            block.scalar(run("act"))
            block.vector(run("dve"))
            block.gpsimd(run("pool"))
            block.tensor(run("pe"))


def _rope_tables():
    t = np.arange(SEQ)
    row = (t // 64).astype(np.float32)
    col = (t % 64).astype(np.float32)
    inv = (np.float32(10000.0) ** (-np.arange(0, 64, 2, dtype=np.float32) / np.float32(64))).astype(np.float32)
    ar = (row[:, None] * inv[None, :]).astype(np.float32)
    ac = (col[:, None] * inv[None, :]).astype(np.float32)
    cr, sr, cc, sc = np.cos(ar), np.sin(ar), np.cos(ac), np.sin(ac)
    COS = np.concatenate([cr, cr, cc, cc], 1).T.astype(np.float32)
    SIN = np.concatenate([-sr, sr, -sc, sc], 1).T.astype(np.float32)
    return np.ascontiguousarray(COS), np.ascontiguousarray(SIN)


def _consts():
    i = np.arange(128)
    Linc = (i[:, None] >= i[None, :]).astype(np.float32)
    Lst = (i[:, None] > i[None, :]).astype(np.float32)
    Uinc = (i[:, None] <= i[None, :]).astype(np.float32)
    Ust = (i[:, None] < i[None, :]).astype(np.float32)
    I = np.eye(128, dtype=np.float32)
    ones = np.ones((128, 128), np.float32)
    cm = np.concatenate([Linc, Lst, Uinc, Ust, I, ones], 1)
    perm = np.concatenate([np.arange(32, 64), np.arange(0, 32), np.arange(96, 128), np.arange(64, 96)])
    Pm = np.zeros((128, 128), np.float32)
    Pm[np.arange(128), perm] = 1.0
    return np.ascontiguousarray(cm), np.ascontiguousarray(Pm.T)


def _na_bias_tables(rpb, core):
    base = 32 * core - 4
    out = np.full((8, 128, 5, 6, 128), NEG, np.float32)
    p = np.arange(128)
    q = np.arange(128)
    kcol = p % 64
    qcol = q % 64
    cs = np.clip(qcol - 8, 0, 48)
    colok = (kcol[:, None] >= cs[None, :]) & (kcol[:, None] < cs[None, :] + 16)
    coloff = kcol[:, None] - qcol[None, :] + 15
    for ci, g in enumerate([0, 1, 2, 14, 15]):
        kt0 = min(g, 14)
        for t in range(6):
            wrow = 2 * (kt0 + t) + p // 64
            krow = base + wrow
            j = 2 * g + q // 64
            r = 32 * core + j
            rs = np.clip(r - 4, 0, 248)
            rowok = (krow[:, None] >= rs[None, :]) & (krow[:, None] < rs[None, :] + 8)
            rowoff = krow[:, None] - r[None, :] + 7
            ok = rowok & colok
            ro = np.clip(rowoff, 0, 14)
            co = np.clip(coloff, 0, 30)
            vals = rpb[:, ro, co]
            out[:, :, ci, t, :] = np.where(ok[None], vals, NEG)
    return np.ascontiguousarray(out.reshape(8, 128, 5 * 6 * 128))


def host_prep(inp):
    x = np.asarray(inp["x"], np.float32)[0]
    ctx = np.asarray(inp["ctx"], np.float32)[0]
    c = np.asarray(inp["c"], np.float32)[0]
    c_ctx = np.asarray(inp["c_ctx"], np.float32)
    w_ada = np.ascontiguousarray(np.asarray(inp["w_ada"], np.float32)[0])
    b_ada = np.asarray(inp["b_ada"], np.float32)[0]
    g_pre = np.asarray(inp["g_pre"], np.float32)[0]
    g_post = np.asarray(inp["g_post"], np.float32)[0]
    w_in = np.ascontiguousarray(np.asarray(inp["w_in"], np.float32)[0])
    conv_w = np.asarray(inp["conv_w"], np.float32)[0]
    rpb = np.asarray(inp["rpb"], np.float32)[0]
    A_log = np.asarray(inp["A_log"], np.float32)[0]
    dt_bias = np.asarray(inp["dt_bias"], np.float32)[0]
    gnw = np.asarray(inp["gdn_norm_w"], np.float32)[0]
    w_out = np.ascontiguousarray(np.asarray(inp["w_out"], np.float32)[0])

    COS, SIN = _rope_tables()
    cmask, ProtT = _consts()
    xpad = np.zeros((SEQ + 8, D), np.float32)
    xpad[4:4 + SEQ] = x

    def colT(v, n):
        return np.ascontiguousarray(v.reshape(n, 128).T)

    cvec = np.ascontiguousarray(np.stack([c.reshape(8, 128).T, c_ctx.reshape(8, 128).T], -1).reshape(128, 16))
    shared = dict(
        ctxT=np.ascontiguousarray(ctx.T.reshape(8, 128, CTXL)),
        cvec=cvec, w_ada=w_ada, b_adaT=colT(b_ada, 24),
        b_gate_rep=np.ascontiguousarray(np.broadcast_to(b_ada[2048:3072], (128, 1024))),
        g_preT=colT(g_pre, 8),
        g_post_rep=np.ascontiguousarray(np.broadcast_to(g_post, (128, 1024))),
        w_in=w_in, w_out=w_out, cmask=cmask, ProtT=ProtT,
        gnw=np.ascontiguousarray(gnw.reshape(128, 1)),
        AlO=np.ascontiguousarray(np.broadcast_to(np.tile(A_log.reshape(8), 16), (128, 128))),
        dtO=np.ascontiguousarray(np.broadcast_to(np.tile(dt_bias.reshape(8), 16), (128, 128))),
    )
    shared["convO"] = np.ascontiguousarray(conv_w.T.reshape(12, 128, 5).transpose(1, 0, 2).reshape(128, 60))
    conv_kv = conv_w[:, 512:1536]
    conv_f = conv_kv.T.reshape(8, 128, 5).transpose(1, 0, 2).reshape(128, 40)
    conv_b = conv_kv[::-1].T.reshape(8, 128, 5).transpose(1, 0, 2).reshape(128, 40)

    per_core = []
    for i in range(NCORES):
        T0 = SEGT * i
        d = dict(shared)
        win = np.zeros((WIN, D), np.float32)
        lo, hi = T0 - 256, T0 + 2304
        a, b = max(lo, 0), min(hi, SEQ)
        win[a - lo:b - lo] = x[a:b]
        d["xwT"] = np.ascontiguousarray(win.T.reshape(8, 128, WIN))
        d["xown"] = np.ascontiguousarray(x[T0:T0 + SEGT])
        nf = 2 * i
        toks = np.empty((NSU, SU + 4), np.int64)
        isf = np.zeros(NSU, bool)
        for u in range(NSU):
            pidx = np.arange(-2, SU + 2)
            if u < nf:
                toks[u] = SU * u + pidx
                isf[u] = True
            else:
                v = u - nf
                toks[u] = (SEQ - 1) - (SU * v + pidx)
        valid = (toks >= 0) & (toks < SEQ)
        xs = xpad[np.clip(toks, -4, SEQ + 3) + 4]
        xs = xs * valid[..., None]
        xsT = xs.transpose(0, 2, 1).reshape(NSU, 8, 128, SU + 4)
        d["xsT"] = np.ascontiguousarray(xsT[..., 2:SU + 2])
        d["xsH"] = np.ascontiguousarray(np.concatenate([xsT[..., 0:2], xsT[..., SU + 2:SU + 4]], -1))
        hm = np.concatenate([valid[:, 0:2], valid[:, SU + 2:SU + 4]], 1).astype(np.float32)
        d["hmS"] = np.ascontiguousarray(np.broadcast_to(hm.reshape(1, NSU * 4), (128, NSU * 4)))
        hmo = np.ones((2, 4), np.float32)
        if i == 0:
            hmo[0, 0:2] = 0
        if i == NCORES - 1:
            hmo[0, 2:4] = 0
        d["hmO"] = np.ascontiguousarray(np.broadcast_to(hmo.reshape(1, 8), (128, 8)))
        wg = np.empty((NSU, D, 8), np.float32)
        al = np.empty((NSU, 4), np.float32)
        dtb = np.empty((NSU, 4), np.float32)
        cv = np.empty((NSU, 128, 40), np.float32)
        for u in range(NSU):
            dr = 0 if isf[u] else 1
            wg[u, :, 0:4] = w_in[:, 4096 + 4 * dr:4096 + 4 * dr + 4]
            wg[u, :, 4:8] = w_in[:, 4104 + 4 * dr:4104 + 4 * dr + 4]
            al[u] = A_log[dr]
            dtb[u] = dt_bias[dr]
            cv[u] = conv_f if isf[u] else conv_b
        d["w_gs"] = wg
        d["AlS"] = np.ascontiguousarray(np.broadcast_to(np.tile(al[:, None, :], (1, 8, 1)).reshape(1, NSU * 32), (128, NSU * 32)))
        d["dtS"] = np.ascontiguousarray(np.broadcast_to(np.tile(dtb[:, None, :], (1, 8, 1)).reshape(1, NSU * 32), (128, NSU * 32)))
        d["convS"] = cv
        tk = np.clip(toks[:, 2:SU + 2], 0, SEQ - 1).reshape(-1)
        tk = np.concatenate([tk, np.arange(T0, T0 + SEGT)])
        d["ropeC"] = np.ascontiguousarray(COS[:, tk])
        d["ropeS"] = np.ascontiguousarray(SIN[:, tk])
        ms = np.zeros((8,), np.float32)
        ms[i] = 1.0
        d["mseg"] = np.ascontiguousarray(np.broadcast_to(ms, (128, 8)))
        d["nab"] = _na_bias_tables(rpb, i)
        per_core.append(d)
    return per_core


IN_SHAPES = dict(
    xwT=[8, 128, WIN], xown=[SEGT, D], xsT=[NSU, 8, 128, SU], xsH=[NSU, 8, 128, 4], ctxT=[8, 128, CTXL],
    cvec=[128, 16], w_ada=[D, 3072], b_adaT=[128, 24], b_gate_rep=[128, 1024], g_preT=[128, 8],
    g_post_rep=[128, 1024], w_in=[D, IN_DIM], w_out=[D, D], cmask=[128, 768], ProtT=[128, 128],
    gnw=[128, 1], AlO=[128, 128], dtO=[128, 128], convO=[128, 60], hmS=[128, NSU * 4], hmO=[128, 8],
    w_gs=[NSU, D, 8], AlS=[128, NSU * 32], dtS=[128, NSU * 32], convS=[NSU, 128, 40],
    ropeC=[128, SEQ], ropeS=[128, SEQ], mseg=[128, 8], nab=[8, 128, 3840],
)


class Arena:
    def __init__(self, ap, size):
        self.ap = ap
        self.size = size
        self.off = 0

    def alloc(self, cols, name=""):
        o = self.off
        self.off += cols
        assert self.off <= self.size, f"arena overflow at {name}: {self.off} > {self.size}"
        return Buf(self.ap[:, o:o + cols], name)

    def mark(self):
        return self.off

    def release(self, m):
        self.off = m


class Chain:
    pass


class LR(list):
    r = False
    x = None


class Builder:
    def __init__(self, stop="full", dbg=False):
        self.stop = stop
        self.nc = bass.Bass("TRN2", target_bir_lowering=False)
        nc = self.nc
        self.din = {k: nc.dram_tensor(k, shp, F32, kind="ExternalInput").ap() for k, shp in IN_SHAPES.items()}
        self.out = nc.dram_tensor("out", [SEGT, D], F32, kind="ExternalOutput").ap()
        self.scr = nc.dram_tensor("scr_gated", [8, 128, SEGT], F32).ap()
        self.dbg = nc.dram_tensor("dbg", [128, 8192], F32, kind="ExternalOutput").ap() if dbg else None
        self.dbg_off = 0
        self.dbg_map = {}
        self.outs = []
        self.dmaq = 0

    def dma(self, out, in_, r=(), w=(), uw=(), eng=None):
        if eng is None:
            eng = "sp"
        self.S.add(eng, lambda E: E.dma_start(out=out, in_=in_), r, w, dma=True, uwrites=uw)

    def act(self, out, in_, func, r, w, scale=None, bias=None):
        kw = dict(out=out, in_=in_, func=func)
        if scale is not None:
            kw["scale"] = scale
        if bias is not None:
            kw["bias"] = bias
        self.S.add("act", lambda E: E.activation(**kw), r, w)

    def ts(self, out, in0, s1, s2, op0, op1, r, w, eng="dve"):
        kw = dict(out=out, in0=in0, scalar1=s1, scalar2=s2, op0=op0)
        if op1 is not None:
            kw["op1"] = op1
        self.S.add(eng, lambda E: E.tensor_scalar(**kw), r, w)

    def tt(self, out, in0, in1, op, r, w, eng="dve"):
        self.S.add(eng, lambda E: E.tensor_tensor(out=out, in0=in0, in1=in1, op=op), r, w)

    def stt(self, out, in0, scalar, in1, op0, op1, r, w, eng="dve"):
        eng = "dve"
        self.S.add(eng, lambda E: E.scalar_tensor_tensor(out=out, in0=in0, scalar=scalar, in1=in1,
                                                         op0=op0, op1=op1), r, w)

    def cp(self, out, in_, r, w, eng="dve"):
        self.S.add(eng, lambda E: E.tensor_copy(out=out, in_=in_), r, w)

    def recip(self, out, in_, r, w):
        self.S.add("dve", lambda E: E.reciprocal(out=out, in_=in_), r, w)

    def memset(self, ap, val, w, eng="pool"):
        self.S.add(eng, lambda E: E.memset(ap, val), (), w)

    def mm(self, out, lhsT, rhs, start, stop, r, w):
        self.S.add("pe", lambda E: E.matmul(out, lhsT=lhsT, rhs=rhs, start=start, stop=stop), r, w)

    def tr(self, out, in_, r, w):
        ident = self.Id
        self.S.add("pe", lambda E: E.transpose(out=out, in_=in_, identity=ident), list(r) + [self.cm], w)

    def const(self, name, cols):
        b = self.parena.alloc(cols, name)
        self.dma(b.ap, self.din[name], w=[b])
        return b

    def tap(self, name, ap, r, cols):
        if self.dbg is None:
            return
        o = self.dbg_off
        self.dbg_off += cols
        assert self.dbg_off <= 8192
        self.dbg_map[name] = (o, cols)
        ob = Buf(None, "dbg_" + name)
        self.dma(self.dbg[:, o:o + cols], ap, r=r, w=[ob])
        self.outs.append(ob)

    def nps(self):
        self.psi = (self.psi + 1) % 2
        return self.PS[self.psi]

    def build(self):
        nc = self.nc
        with ExitStack() as st:
            self.S = Sched(nc)
            pt_ = st.enter_context(nc.sbuf_tensor("parena", [128, PCOLS], F32))
            self.parena = Arena(pt_[:, :], PCOLS)
            prt_ = st.enter_context(nc.sbuf_tensor("prarena", [128, PRCOLS], F32R))
            self.prarena = Arena(prt_[:, :], PRCOLS)
            self.arena = None
            self.rarena = None
            self.phase_stack = None
            self.phase_id = 0
            self.PS = []
            for j in range(4):
                pt = st.enter_context(nc.psum_tensor(f"ps{j}", [128, 1024], F32))
                self.PS.append(Buf(pt[:, 0:512], f"ps{j}a", excl=True))
                self.PS.append(Buf(pt[:, 512:1024], f"ps{j}b", excl=True))
                self.PSWt = getattr(self, "PSWt", []) + [pt]
            self.psi = 0
            self.cps = [[Buf(self.PS[4 + c].ap[:, 128 * s:128 * (s + 1)], f"cps{c}_{s}", key=self.PS[4 + c].key, excl=True)
                         for s in range(4)] for c in range(4)]
            self.program()
            fin = list(self.outs)
            self.S.add("sp", None, fin, [])
            self.S.emit(st)
        return nc

    def begin_phase(self, cols, rcols=0):
        assert cols + rcols + PCOLS + PRCOLS <= TOTCOLS, (cols, rcols)
        self.phase_id += 1
        self.phase_stack = ExitStack()
        t = self.phase_stack.enter_context(self.nc.sbuf_tensor(f"ph{self.phase_id}", [128, cols], F32))
        self.arena = Arena(t[:, :], cols)
        self.rarena = None
        self.stage = None
        if rcols:
            tr_ = self.phase_stack.enter_context(self.nc.sbuf_tensor(f"phr{self.phase_id}", [128, rcols], F32R))
            self.rarena = Arena(tr_[:, :], rcols)
            self.stage = [self.arena.alloc(1024, "stage0"), self.arena.alloc(1024, "stage1")]
            self.stage_i = 0

    def end_phase(self):
        self.phase_stack.close()
        self.arena = None
        self.rarena = None
        self.S.barrier()

    def program(self):
        self.phase0()
        if self.stop == "p0":
            return
        self.phase_ctx()
        if self.stop == "ctx":
            return
        self.phase_stream()
        if self.stop == "stream":
            return
        self.phase_own_gdn()
        if self.stop == "gdn":
            return
        self.phase_natten()
        if self.stop == "na":
            return
        self.phase_final()

    def phase0(self):
        A = self.parena
        din = self.din
        self.cm = self.const("cmask", 768)
        self.Linc, self.Lst, self.Uinc, self.Ust, self.Id, self.Ones = [self.cm.t(i, 128) for i in range(6)]
        self.prot = self.const("ProtT", 128)
        self.gnw = self.const("gnw", 1)
        self.AlO = self.const("AlO", 128)
        self.dtO = self.const("dtO", 128)
        self.convO = self.const("convO", 60)
        self.hmO = self.const("hmO", 8)
        self.hmS = self.const("hmS", NSU * 4)
        self.AlS = self.const("AlS", NSU * 32)
        self.dtS = self.const("dtS", NSU * 32)
        self.mseg = self.const("mseg", 8)
        gpre = self.const("g_preT", 8)
        badaT = self.const("b_adaT", 24)
        cv = self.const("cvec", 16)
        self.mod = A.alloc(48, "mod")
        self.A1 = A.alloc(16, "A1")
        self.G2 = A.alloc(1024, "G2")
        self.nAO = A.alloc(128, "nAO")
        self.nAS = A.alloc(NSU * 32, "nAS")
        self.act(self.nAO.ap, self.AlO.ap, AF.Exp, [self.AlO], [self.nAO])
        self.ts(self.nAO.ap, self.nAO.ap, -1.0, None, ALU.mult, None, [self.nAO], [self.nAO])
        self.act(self.nAS.ap, self.AlS.ap, AF.Exp, [self.AlS], [self.nAS])
        self.ts(self.nAS.ap, self.nAS.ap, -1.0, None, ALU.mult, None, [self.nAS], [self.nAS])
        self.Ones_r = self.prarena.alloc(128, "Ones_r")
        self.cp(self.Ones_r.ap, self.Ones, [self.cm], [self.Ones_r])
        self.begin_phase(30000)
        A = self.arena
        csil = A.alloc(16, "csil")
        self.act(csil.ap, cv.ap, AF.Silu, [cv], [csil])
        wada = A.alloc(8 * 3072, "wada")
        for kc in range(8):
            self.dma(wada.v(kc * 3072, (kc + 1) * 3072), din["w_ada"][kc * 128:(kc + 1) * 128, :], uw=[wada])
        ps = self.PS[0]
        for ct in range(24):
            for kc in range(8):
                self.mm(ps.ap[:, ct * 2:ct * 2 + 2], wada.ap[:, kc * 3072 + ct * 128: kc * 3072 + (ct + 1) * 128],
                        csil.ap[:, kc * 2:kc * 2 + 2], kc == 0, kc == 7, [wada, csil], [ps])
        mod3 = self.mod.ap.rearrange("p (c w) -> p c w", w=2)
        ps3 = ps.ap[:, 0:48].rearrange("p (c w) -> p c w", w=2)
        for w in range(2):
            self.tt(mod3[:, :, w], ps3[:, :, w], badaT.ap, ALU.add, [ps, badaT], [self.mod])
        a13 = self.A1.ap.rearrange("p (c w) -> p c w", w=2)
        for w in range(2):
            self.stt(a13[:, :, w], mod3[:, 8:16, w], 1.0, gpre.ap, ALU.add, ALU.mult, [self.mod, gpre], [self.A1])
        rep = A.alloc(8 * 128, "rep")
        for kc in range(8):
            self.ts(rep.t(kc, 128), self.Ones, csil.ap[:, kc * 2:kc * 2 + 1], None, ALU.mult, None,
                    [self.cm, csil], [rep])
        bg = A.alloc(1024, "bgate")
        gp = A.alloc(1024, "gpost")
        self.dma(bg.ap, din["b_gate_rep"], w=[bg])
        self.dma(gp.ap, din["g_post_rep"], w=[gp])
        for n in range(2):
            pg = self.PS[2 + n]
            for kc in range(8):
                self.mm(pg.ap, rep.t(kc, 128), wada.ap[:, kc * 3072 + 2048 + n * 512: kc * 3072 + 2048 + (n + 1) * 512],
                        kc == 0, kc == 7, [rep, wada], [pg])
            self.tt(self.G2.t(n, 512), pg.ap, bg.t(n, 512), ALU.add, [pg, bg], [self.G2])
        self.tt(self.G2.ap, self.G2.ap, gp.ap, ALU.mult, [self.G2, gp], [self.G2])
        self.tap("mod", self.mod.ap, [self.mod], 48)
        self.tap("G2", self.G2.ap[:, 0:64], [self.G2], 64)
        self.end_phase()

    def gen_h(self, srcs, N, w, xt, sq, rs):
        xs = xt.x
        ones = self.Ones_r.ap if sq.r else self.Ones
        ones_b = self.Ones_r if sq.r else self.cm
        for kc in range(8):
            if isinstance(srcs[kc], tuple):
                self.dma(xs[kc].ap[:, 0:2], srcs[kc][0], uw=[xs[kc]])
                self.dma(xs[kc].ap[:, 2:4], srcs[kc][1], uw=[xs[kc]])
            else:
                self.dma(xs[kc].ap[:, 0:N], srcs[kc], w=[xs[kc]])
        ps = self.PS[2]
        for kc in range(8):
            sqb = sq[kc % 2]
            self.act(sqb.ap[:, 0:N], xs[kc].ap[:, 0:N], AF.Square, [xs[kc]], [sqb])
            self.mm(ps.ap[:, 0:N], ones, sqb.ap[:, 0:N], kc == 0, kc == 7, [sqb, ones_b], [ps])
        self.act(rs.ap[:, 0:N], ps.ap[:, 0:N], AF.Sqrt, [ps], [rs], scale=1.0 / D, bias=EPS)
        self.recip(rs.ap[:, 0:N], rs.ap[:, 0:N], [rs], [rs])
        for kc in range(8):
            self.tt(xs[kc].ap[:, 0:N], xs[kc].ap[:, 0:N], rs.ap[:, 0:N], ALU.mult, [xs[kc], rs], [xs[kc]], eng="pool")
            self.ts(xt[kc].ap[:, 0:N], xs[kc].ap[:, 0:N], self.A1.ap[:, kc * 2 + w:kc * 2 + w + 1],
                    self.mod.ap[:, kc * 2 + w:kc * 2 + w + 1], ALU.mult, ALU.add,
                    [xs[kc], self.A1, self.mod], [xt[kc]])

    def alloc_xt(self, N, nbuf=2, r=False):
        sets = LR()
        x_shared = None
        for j in range(nbuf):
            if r and x_shared is not None:
                x = x_shared
            else:
                x = [self.arena.alloc(N, f"xt{j}_{kc}") for kc in range(8)]
                x_shared = x
            if r:
                h = LR(self.rarena.alloc(N, f"ht{j}_{kc}") for kc in range(8))
            else:
                h = LR(x)
            h.x = x
            h.r = r
            sets.append(h)
        if r and nbuf >= 2:
            hh = LR(self.rarena.alloc(4, f"hth_{kc}") for kc in range(8))
            hh.x = [self.arena.alloc(4, f"xth_{kc}") for kc in range(8)]
            hh.r = True
            sets.x = hh
        src = self.rarena if r else self.arena
        sq = LR([src.alloc(N, "sq0"), src.alloc(N, "sq1")])
        sq.r = r
        rs = self.arena.alloc(N, "rs")
        return sets, sq, rs

    def proj_fm(self, xt, N, wfn, M, wbufs, evac, c0=0):
        ps = self.nps()
        for kc in range(8):
            self.mm(ps.ap[0:M, 0:N], wfn(kc), xt[kc].ap[:, c0:c0 + N], kc == 0, kc == 7, [xt[kc]] + list(wbufs), [ps])
        evac(ps)

    def prep_unit(self, main_fn, halo_srcs, n, w, cts, wget, raw, conv, conv_base, hm_ap, hm_buf,
                  gate_wfn, gate_wbufs, ng, graw, rope_off, xsets, sq, rs, tmp, rct, rst, extra=None):
        NT = min(512, n)
        ntile = n // NT
        nct = len(cts)
        prefetch = len(xsets) >= 2 and getattr(xsets, "x", None) is not None
        cur = None
        if halo_srcs is not None:
            xth = xsets.x if prefetch else xsets[0]
            self.gen_h(halo_srcs, 4, w, xth, sq, rs)
            if prefetch:
                cur = xsets[0]
                self.gen_h(main_fn(0, NT), NT, w, cur, sq, rs)
            for ci in range(nct):
                wfn_c, wbufs = wget(ci)
                def ev(ps, ci=ci):
                    self.tt(raw[ci].ap[:, 0:2], ps.ap[:, 0:2], hm_ap[:, 0:2], ALU.mult, [ps, hm_buf], [raw[ci]])
                    self.tt(raw[ci].ap[:, n + 2:n + 4], ps.ap[:, 2:4], hm_ap[:, 2:4], ALU.mult, [ps, hm_buf], [raw[ci]])
                self.proj_fm(xth, 4, wfn_c, 128, wbufs, ev)
        else:
            for ci in range(nct):
                self.memset(raw[ci].ap[:, 0:2], 0.0, [raw[ci]])
                self.memset(raw[ci].ap[:, n + 2:n + 4], 0.0, [raw[ci]])
        for j in range(ntile):
            if prefetch:
                if cur is None:
                    cur = xsets[j % 2]
                    self.gen_h(main_fn(j, NT), NT, w, cur, sq, rs)
                xt = cur
                cur = None
                if j + 1 < ntile:
                    cur = xsets[(j + 1) % 2]
                    self.gen_h(main_fn(j + 1, NT), NT, w, cur, sq, rs)
            else:
                xt = xsets[(j + 1) % len(xsets)]
                self.gen_h(main_fn(j, NT), NT, w, xt, sq, rs)
            for ci in range(nct):
                wfn_c, wbufs = wget(ci)
                def ev(ps, ci=ci, j=j):
                    self.act(raw[ci].ap[:, 2 + NT * j:2 + NT * (j + 1)], ps.ap[:, 0:NT], AF.Copy, [ps], [raw[ci]])
                self.proj_fm(xt, NT, wfn_c, 128, wbufs, ev)
            if extra is not None:
                extra(j, xt, NT)
            ps = self.nps()
            for kc in range(8):
                self.mm(ps.ap[0:ng, 0:NT], gate_wfn(kc), xt[kc].ap[:, 0:NT], kc == 0, kc == 7,
                        [xt[kc]] + list(gate_wbufs), [ps])
            gT = tmp[0]
            self.act(gT.ap[0:ng, 0:NT], ps.ap[0:ng, 0:NT], AF.Copy, [ps], [gT])
            for c in range(NT // 128):
                ps2 = self.nps()
                idn = self.Id[0:ng, 0:ng]
                self.S.add("pe", lambda E, o=ps2.ap[:, 0:ng], i=gT.ap[0:ng, c * 128:(c + 1) * 128], idn=idn:
                           E.transpose(out=o, in_=i, identity=idn), [gT, self.cm], [ps2])
                cc = j * (NT // 128) + c
                self.cp(graw.ap[:, cc * ng:(cc + 1) * ng], ps2.ap[:, 0:ng], [ps2], [graw])
        for j in range(ntile):
            a = NT * j
            for ci in range(nct):
                eng = "dve"
                cb = conv_base + ci * 5
                self.ts(tmp[ci % 2].ap[:, 0:NT], raw[ci].ap[:, a:a + NT], conv.ap[:, cb:cb + 1], None, ALU.mult, None,
                        [raw[ci], conv], [tmp[ci % 2]], eng=eng)
                for tp in range(1, 5):
                    self.stt(tmp[ci % 2].ap[:, 0:NT], raw[ci].ap[:, a + tp:a + tp + NT], conv.ap[:, cb + tp:cb + tp + 1],
                             tmp[ci % 2].ap[:, 0:NT], ALU.mult, ALU.add, [raw[ci], conv, tmp[ci % 2]], [tmp[ci % 2]], eng=eng)
                self.act(raw[ci].ap[:, a:a + NT], tmp[ci % 2].ap[:, 0:NT], AF.Silu, [tmp[ci % 2]], [raw[ci]])
        for j in range(ntile):
            a = NT * j
            if rope_off is not None:
                self.dma(rct.ap[:, 0:NT], self.din["ropeC"][:, rope_off + a:rope_off + a + NT], w=[rct])
                self.dma(rst.ap[:, 0:NT], self.din["ropeS"][:, rope_off + a:rope_off + a + NT], w=[rst])
            for ci in range(nct):
                if cts[ci] == "v":
                    continue
                rr = raw[ci].ap[:, a:a + NT]
                sqb = sq[ci % 2]
                self.act(sqb.ap[:, 0:NT], rr, AF.Square, [raw[ci]], [sqb])
                ps = self.PS[2]
                self.mm(ps.ap[:, 0:NT], self.Ones_r.ap if sq.r else self.Ones, sqb.ap[:, 0:NT], True, True,
                        [sqb, self.Ones_r if sq.r else self.cm], [ps])
                self.act(rs.ap[:, 0:NT], ps.ap[:, 0:NT], AF.Sqrt, [ps], [rs], scale=1.0, bias=EPS)
                self.recip(rs.ap[:, 0:NT], rs.ap[:, 0:NT], [rs], [rs])
                if cts[ci] == "q":
                    self.stt(rr, rr, 128.0 ** -0.5, rs.ap[:, 0:NT], ALU.mult, ALU.mult, [raw[ci], rs], [raw[ci]])
                else:
                    self.tt(rr, rr, rs.ap[:, 0:NT], ALU.mult, [raw[ci], rs], [raw[ci]])
                if rope_off is not None:
                    ps2 = self.nps()
                    self.mm(ps2.ap[:, 0:NT], self.prot.ap, rr, True, True, [raw[ci], self.prot], [ps2])
                    t = tmp[ci % 2]
                    self.tt(t.ap[:, 0:NT], ps2.ap[:, 0:NT], rst.ap[:, 0:NT], ALU.mult, [ps2, rst], [t])
                    self.tt(rr, rr, rct.ap[:, 0:NT], ALU.mult, [raw[ci], rct], [raw[ci]], eng="pool")
                    self.tt(rr, rr, t.ap[:, 0:NT], ALU.add, [raw[ci], t], [raw[ci]])

    def gate_math(self, graw, nch, ng, Al_ap, dt_ap, nA_ap, pbufs, beta, g, wk):
        ngh = ng // 2
        g3 = graw.ap[:, 0:nch * ng].rearrange("p (c g) -> p c g", g=ng)
        b3 = g3[:, :, 0:ngh]
        a3 = g3[:, :, ngh:ng]
        def v3(buf):
            return buf.ap[:, 0:nch * ngh].rearrange("p (c g) -> p c g", g=ngh)
        be, gg, e, u = v3(beta), v3(g), v3(wk[0]), v3(wk[1])
        dt3 = dt_ap.rearrange("p (c g) -> p c g", g=ngh)
        nA3 = nA_ap.rearrange("p (c g) -> p c g", g=ngh)
        self.act(be, b3, AF.Exp, [graw], [beta], scale=-1.0)
        self.ts(be, be, 1.0, None, ALU.add, None, [beta], [beta])
        self.recip(be, be, [beta], [beta])
        self.tt(e, a3, dt3, ALU.add, [graw] + pbufs, [wk[0]])
        self.act(e, e, AF.Exp, [wk[0]], [wk[0]])
        self.ts(u, e, 1.0, None, ALU.add, None, [wk[0]], [wk[1]])
        self.act(gg, u, AF.Ln, [wk[1]], [g])
        self.ts(u, u, -1.0, 1e-30, ALU.add, ALU.max, [wk[1]], [wk[1]])
        self.recip(u, u, [wk[1]], [wk[1]])
        self.tt(gg, gg, e, ALU.mult, [g, wk[0]], [g])
        self.tt(gg, gg, u, ALU.mult, [g, wk[1]], [g])
        self.tt(gg, gg, nA3, ALU.mult, [g] + pbufs, [g])

    def new_chain(self, idx, name):
        cx = Chain()
        A = self.arena
        cx.ps = self.cps[idx]
        cx.idx = idx
        cx.psi = 0
        cx.S = A.alloc(128, name + "S")
        cx.w = {k: A.alloc(128, name + k) for k in
                ["gB", "X", "A0", "A1", "R", "kbg", "kdec", "vb", "u", "wT", "vn", "qd", "X2", "qkT"]}
        cx.w["BR0"] = A.alloc(256, name + "BR0")
        cx.w["BR1"] = A.alloc(256, name + "BR1")
        cx.c = A.alloc(8, name + "cols")
        return cx

    def cps_next(self, cx):
        cx.psi = (cx.psi + 1) % 4
        return cx.ps[cx.psi]

    def gdn_chunk(self, cx, kT, vT, qT, srcbufs, bcol, gcol, gbufs, fwd, out_ap=None, out_buf=None):
        W = cx.w
        cm = self.cm
        if fwd:
            Ud, Ms, MiT, last = self.Uinc, self.Lst, self.Uinc, 127
        else:
            Ud, Ms, MiT, last = self.Linc, self.Ust, self.Linc, 0
        gcc, glc, gll, ekd, egc, bg = [cx.c.ap[:, i:i + 1] for i in range(6)]
        C = cx.c
        P = cx.ps
        PB = P[0]
        bank = self.PS[4 + cx.idx]
        p01 = bank.ap[:, 0:256]
        BR = [W["BR0"], W["BR1"]]
        AA = [W["A0"], W["A1"]]
        self.ts(W["gB"].ap, self.Ones, gcol, None, ALU.mult, None, [cm] + gbufs, [W["gB"]], eng="pool")
        pgc = P[2]
        self.mm(pgc.ap, W["gB"].ap, Ud, True, True, [W["gB"], cm], [PB])
        yield
        self.tt(W["X2"].ap, pgc.ap, self.Id, ALU.mult, [PB, cm], [W["X2"]])
        self.S.add("dve", lambda E, o=gcc, i=W["X2"].ap: E.reduce_sum(out=o, in_=i, axis=mybir.AxisListType.X),
                   [W["X2"]], [C])
        self.act(gll, pgc.ap[:, last:last + 1], AF.Copy, [PB], [C])
        self.act(glc, pgc.ap[:, last:last + 1], AF.Exp, [PB], [C])
        self.act(ekd, gcc, AF.Exp, [C], [C], scale=-1.0, bias=gll)
        self.act(egc, gcc, AF.Exp, [C], [C])
        self.tt(bg, egc, bcol, ALU.mult, [C] + gbufs, [C])
        self.ts(W["X"].ap, pgc.ap, gcc, 0.0, ALU.subtract, ALU.max, [PB, C], [W["X"]])
        self.act(W["X"].ap, W["X"].ap, AF.Exp, [W["X"]], [W["X"]], scale=-1.0)
        if qT is not None:
            self.act(W["qd"].ap, pgc.ap, AF.Exp, [PB], [W["qd"]])
            self.tt(W["qd"].ap, W["qd"].ap, qT, ALU.mult, [W["qd"]] + srcbufs, [W["qd"]], eng="pool")
            self.ts(W["X2"].ap, pgc.ap, gcc, 0.0, ALU.subtract, ALU.min, [PB, C], [W["X2"]])
            self.act(W["X2"].ap, W["X2"].ap, AF.Exp, [W["X2"]], [W["X2"]])
            self.tt(W["X2"].ap, W["X2"].ap, MiT, ALU.mult, [W["X2"], cm], [W["X2"]], eng="pool")
            pqk = P[0]
            self.mm(pqk.ap, kT, qT, True, True, srcbufs, [PB])
            yield
            self.tt(W["qkT"].ap, pqk.ap, W["X2"].ap, ALU.mult, [PB, W["X2"]], [W["qkT"]])
        pkk = P[1]
        self.mm(pkk.ap, kT, kT, True, True, srcbufs, [PB])
        yield
        self.tt(AA[0].ap, pkk.ap, W["X"].ap, ALU.mult, [PB, W["X"]], [AA[0]])
        self.stt(AA[0].ap, AA[0].ap, bcol, Ms, ALU.mult, ALU.mult, [AA[0], cm] + gbufs, [AA[0]])
        pt = P[0]
        self.tr(pt.ap, AA[0].ap, [AA[0]], [PB])
        yield
        self.act(BR[0].ap[:, 0:128], pt.ap, AF.Copy, [PB], [BR[0]])
        self.tt(BR[1].ap[:, 128:256], self.Id, pt.ap, ALU.subtract, [cm, PB], [BR[1]])
        self.mm(P[0].ap, AA[0].ap, BR[0].ap[:, 0:128], True, True, [AA[0], BR[0]], [PB])
        yield
        self.act(BR[1].ap[:, 0:128], P[0].ap, AF.Copy, [PB], [BR[1]])
        self.mm(P[3].ap, BR[0].ap[:, 0:128], AA[0].ap, True, True, [AA[0], BR[0]], [PB])
        yield
        self.cp(AA[1].ap, P[3].ap, [PB], [AA[1]])
        for m in range(1, 6):
            cur, nxt = BR[m % 2], BR[(m + 1) % 2]
            Am, An = AA[m % 2], AA[(m + 1) % 2]
            if m < 5:
                self.mm(p01, Am.ap, cur.ap[:, 0:256], True, True, [Am, cur], [PB])
                yield
                self.act(nxt.ap[:, 0:128], bank.ap[:, 0:128], AF.Copy, [PB], [nxt])
                self.tt(nxt.ap[:, 128:256], cur.ap[:, 128:256], bank.ap[:, 128:256], ALU.add, [cur, PB], [nxt])
            else:
                self.mm(bank.ap[:, 128:256], Am.ap, cur.ap[:, 128:256], True, True, [Am, cur], [PB])
                yield
                self.tt(nxt.ap[:, 128:256], cur.ap[:, 128:256], bank.ap[:, 128:256], ALU.add, [cur, PB], [nxt])
            self.mm(P[3].ap, cur.ap[:, 0:128], Am.ap, True, True, [Am, cur], [PB])
            yield
            if m % 2 == 0:
                self.act(An.ap, P[3].ap, AF.Copy, [PB], [An])
            else:
                self.cp(An.ap, P[3].ap, [PB], [An])
        R5 = BR[0]
        self.mm(P[1].ap, AA[0].ap, R5.ap[:, 128:256], True, True, [AA[0], R5], [PB])
        yield
        R = W["R"]
        self.tt(R.ap, R5.ap[:, 128:256], P[1].ap, ALU.add, [R5, PB], [R])
        pk = P[2]
        self.tr(pk.ap, kT, srcbufs, [PB])
        yield
        self.act(W["kbg"].ap, pk.ap, AF.Copy, [PB, C], [W["kbg"]], scale=bg)
        self.ts(W["kdec"].ap, pk.ap, ekd, None, ALU.mult, None, [PB, C], [W["kdec"]])
        pv = P[3]
        self.tr(pv.ap, vT, srcbufs, [PB])
        yield
        self.act(W["vb"].ap, pv.ap, AF.Copy, [PB] + gbufs, [W["vb"]], scale=bcol)
        self.mm(P[0].ap, R.ap, W["vb"].ap, True, True, [R, W["vb"]], [PB])
        yield
        self.act(W["u"].ap, P[0].ap, AF.Copy, [PB], [W["u"]])
        self.mm(P[1].ap, W["kbg"].ap, R.ap, True, True, [R, W["kbg"]], [PB])
        yield
        self.cp(W["wT"].ap, P[1].ap, [PB], [W["wT"]])
        self.mm(P[2].ap, W["wT"].ap, cx.S.ap, True, True, [W["wT"], cx.S], [PB])
        yield
        self.tt(W["vn"].ap, W["u"].ap, P[2].ap, ALU.subtract, [W["u"], PB], [W["vn"]])
        if qT is not None:
            po = P[3]
            self.mm(po.ap, cx.S.ap, W["qd"].ap, True, False, [cx.S, W["qd"]], [PB])
            self.mm(po.ap, W["vn"].ap, W["qkT"].ap, False, True, [W["vn"], W["qkT"]], [PB])
            yield
            self.tt(out_ap, out_ap, po.ap, ALU.add, [out_buf, PB], [out_buf])
        self.mm(P[0].ap, W["kdec"].ap, W["vn"].ap, True, True, [W["kdec"], W["vn"]], [PB])
        yield
        self.stt(cx.S.ap, cx.S.ap, glc, P[0].ap, ALU.mult, ALU.add, [cx.S, C, PB], [cx.S])

    def run_rr(self, gens):
        gens = list(gens)
        while gens:
            for g in list(gens):
                try:
                    next(g)
                except StopIteration:
                    gens.remove(g)

    def load_w(self, buf, col0, ncol, r=False, src="w_in"):
        for kc in range(8):
            srcap = self.din[src][kc * 128:(kc + 1) * 128, col0:col0 + ncol]
            if not r:
                self.dma(buf.ap[:, kc * ncol:(kc + 1) * ncol], srcap, uw=[buf])
            else:
                self.stage_i = (self.stage_i + 1) % 2
                stg = self.stage[self.stage_i]
                self.dma(stg.ap[:, 0:ncol], srcap, w=[stg])
                if self.stage_i == 0:
                    self.act(buf.ap[:, kc * ncol:(kc + 1) * ncol], stg.ap[:, 0:ncol], AF.Copy, [stg], [buf])
                else:
                    self.cp(buf.ap[:, kc * ncol:(kc + 1) * ncol], stg.ap[:, 0:ncol], [stg], [buf])

    def wres(self, buf, ncol, ci):
        return (lambda kc: buf.ap[:, kc * ncol + ci * 128: kc * ncol + (ci + 1) * 128]), [buf]

    def phase_ctx(self):
        P = self.parena
        din = self.din
        self.kcT = [self.prarena.alloc(256, f"kcT{j}") for j in range(4)]
        self.vca = [self.prarena.alloc(8 * 128, f"vca{t}") for t in range(2)]
        self.Sfc = [P.alloc(128, f"Sfc{h}") for h in range(4)]
        self.Sbc = [P.alloc(128, f"Sbc{h}") for h in range(4)]
        self.Sf = [P.alloc(128, f"Sf{h}") for h in range(4)]
        self.Sb = [P.alloc(128, f"Sb{h}") for h in range(4)]
        self.omseg = P.alloc(8, "omseg")
        self.ts(self.omseg.ap, self.mseg.ap, -1.0, 1.0, ALU.mult, ALU.add, [self.mseg], [self.omseg])
        self.begin_phase(21000, 22000)
        A = self.arena
        RA = self.rarena
        self.chains = [self.new_chain(i, f"cx{i}") for i in range(4)]
        xsets, sq, rs = self.alloc_xt(512, nbuf=1, r=True)
        wkv = RA.alloc(8 * 1024, "wkv")
        self.load_w(wkv, 2560, 1024, r=True)
        wg = RA.alloc(8 * 16, "wg")
        self.load_w(wg, 4096, 16, r=True)
        wna = RA.alloc(8 * 1024, "wna")
        self.load_w(wna, 512, 1024, r=True)
        raw = [A.alloc(CTXL + 4, f"rawc{i}") for i in range(8)]
        graw = A.alloc(32, "graw")
        beta = A.alloc(16, "beta")
        g = A.alloc(16, "g")
        wk = [A.alloc(16, "gwk0"), A.alloc(16, "gwk1")]
        tmp = [A.alloc(512, "tmp0"), A.alloc(512, "tmp1")]

        def extra(j, xt, NT):
            for jj in range(4):
                def ev(ps, jj=jj):
                    self.act(self.kcT[jj].ap, ps.ap[:, 0:256], AF.Copy, [ps], [self.kcT[jj]])
                self.proj_fm(xt, 256, lambda kc, jj=jj: wna.ap[:, kc * 1024 + jj * 128: kc * 1024 + (jj + 1) * 128], 128,
                             [wna], ev)
            for t in range(2):
                ps = self.nps()
                for kc in range(8):
                    self.mm(ps.ap, xt[kc].ap[:, t * 128:(t + 1) * 128], wna.ap[:, kc * 1024 + 512: kc * 1024 + 1024],
                            kc == 0, kc == 7, [xt[kc], wna], [ps])
                v3 = self.vca[t].ap.rearrange("p (h c) -> p h c", c=128)
                p3 = ps.ap.rearrange("p (h c) -> p h c", c=64)
                self.act(v3[:, :, 0:64], p3, AF.Copy, [ps], [self.vca[t]])
                self.act(v3[:, :, 64:128], p3, AF.Identity, [ps], [self.vca[t]], scale=0.0, bias=1.0)

        self.prep_unit(lambda j, NT: [din["ctxT"][kc, :, :] for kc in range(8)], None, CTXL, 1,
                       ["k"] * 4 + ["v"] * 4, lambda ci: self.wres(wkv, 1024, ci), raw, self.convO, 20, None, None,
                       lambda kc: wg.ap[:, kc * 16:(kc + 1) * 16], [wg], 16, graw, None, xsets, sq, rs, tmp, None, None,
                       extra=extra)
        self.gate_math(graw, 2, 16, None, self.dtO.ap[:, 0:16], self.nAO.ap[:, 0:16], [self.dtO, self.nAO], beta, g, wk)
        self.tap("ctx_beta", beta.ap, [beta], 16)
        self.tap("ctx_g", g.ap, [g], 16)
        self.tap("ctx_k0", raw[0].ap[:, 0:256], [raw[0]], 256)
        self.tap("ctx_v0", raw[4].ap[:, 0:256], [raw[4]], 256)
        for dr in range(2):
            for h in range(4):
                self.memset(self.chains[h].S.ap, 0.0, [self.chains[h].S])
            for c in ([0, 1] if dr == 0 else [1, 0]):
                gens = []
                for h in range(4):
                    idx = c * 8 + dr * 4 + h
                    gens.append(self.gdn_chunk(self.chains[h], raw[h].ap[:, c * 128:(c + 1) * 128],
                                               raw[4 + h].ap[:, c * 128:(c + 1) * 128],
                                               None, [raw[h], raw[4 + h]], beta.ap[:, idx:idx + 1], g.ap[:, idx:idx + 1],
                                               [beta, g], dr == 0))
                self.run_rr(gens)
            dst = self.Sfc if dr == 0 else self.Sbc
            for h in range(4):
                self.cp(dst[h].ap, self.chains[h].S.ap, [self.chains[h].S], [dst[h]], eng="pool")
        self.tap("Sfc0", self.Sfc[0].ap, [self.Sfc[0]], 128)
        self.tap("Sbc0", self.Sbc[0].ap, [self.Sbc[0]], 128)
        self.end_phase()

    def switch(self, j):
        mj = self.mseg.ap[:, j:j + 1]
        oj = self.omseg.ap[:, j:j + 1]
        for h in range(4):
            cx = self.chains[h]
            self.stt(self.Sf[h].ap, cx.S.ap, mj, self.Sf[h].ap, ALU.mult, ALU.add, [cx.S, self.mseg, self.Sf[h]], [self.Sf[h]])
            t = cx.w["vn"]
            self.ts(t.ap, self.Sbc[h].ap, mj, None, ALU.mult, None, [self.Sbc[h], self.mseg], [t], eng="pool")
            self.stt(cx.S.ap, cx.S.ap, oj, t.ap, ALU.mult, ALU.add, [cx.S, self.omseg, t], [cx.S])

    def phase_stream(self):
        din = self.din
        self.begin_phase(25100, 17600)
        A = self.arena
        RA = self.rarena
        self.chains = [self.new_chain(i, f"cs{i}") for i in range(4)]
        xsets, sq, rs = self.alloc_xt(512, nbuf=2, r=True)
        wkv = RA.alloc(8 * 1024, "wkv")
        self.load_w(wkv, 2560, 1024, r=True)
        wgs_f = A.alloc(64, "wgs_f")
        wgs = RA.alloc(64, "wgs")
        convs = A.alloc(40, "convs")
        raw = [A.alloc(SU + 4, f"raws{i}") for i in range(8)]
        graw = A.alloc(64, "graw")
        beta = A.alloc(32, "beta")
        g = A.alloc(32, "g")
        wk = [A.alloc(32, "gwk0"), A.alloc(32, "gwk1")]
        tmp = [Buf(self.stage[0].ap[:, 0:512], "tmp0", key=self.stage[0].key),
               Buf(self.stage[0].ap[:, 512:1024], "tmp1", key=self.stage[0].key)]
        rct = Buf(self.stage[1].ap[:, 0:512], "rct", key=self.stage[1].key)
        rst = Buf(self.stage[1].ap[:, 512:1024], "rst", key=self.stage[1].key)
        for h in range(4):
            self.cp(self.chains[h].S.ap, self.Sfc[h].ap, [self.Sfc[h]], [self.chains[h].S], eng="pool")
            self.memset(self.Sf[h].ap, 0.0, [self.Sf[h]])
        for u in range(NSU):
            if u % 2 == 0:
                self.switch(u // 2)
            for kc in range(8):
                self.dma(wgs_f.ap[:, kc * 8:(kc + 1) * 8], din["w_gs"][u, kc * 128:(kc + 1) * 128, :], uw=[wgs_f])
            self.act(wgs.ap, wgs_f.ap, AF.Copy, [wgs_f], [wgs])
            self.dma(convs.ap, din["convS"][u], w=[convs])
            halo = [(din["xsH"][u, kc, :, 0:2], din["xsH"][u, kc, :, 2:4]) for kc in range(8)]
            self.prep_unit(lambda j, NT, u=u: [din["xsT"][u, kc, :, j * NT:(j + 1) * NT] for kc in range(8)], halo, SU, 0,
                           ["k"] * 4 + ["v"] * 4, lambda ci: self.wres(wkv, 1024, ci), raw, convs, 0,
                           self.hmS.ap[:, u * 4:(u + 1) * 4], self.hmS,
                           lambda kc: wgs.ap[:, kc * 8:(kc + 1) * 8], [wgs], 8, graw, u * SU, xsets, sq, rs, tmp, rct, rst)
            self.gate_math(graw, 8, 8, None, self.dtS.ap[:, u * 32:(u + 1) * 32], self.nAS.ap[:, u * 32:(u + 1) * 32],
                           [self.dtS, self.nAS], beta, g, wk)
            for c in range(8):
                gens = []
                for h in range(4):
                    idx = c * 4 + h
                    gens.append(self.gdn_chunk(self.chains[h], raw[h].ap[:, c * 128:(c + 1) * 128],
                                               raw[4 + h].ap[:, c * 128:(c + 1) * 128], None, [raw[h], raw[4 + h]],
                                               beta.ap[:, idx:idx + 1], g.ap[:, idx:idx + 1], [beta, g], True))
                self.run_rr(gens)
            if u == 0:
                self.tap("s0_k0", raw[0].ap[:, 0:256], [raw[0]], 256)
                self.tap("s0_beta", beta.ap, [beta], 32)
                self.tap("s0_g", g.ap, [g], 32)
                self.tap("s0_S0", self.chains[0].S.ap, [self.chains[0].S], 128)
        self.switch(7)
        for h in range(4):
            self.cp(self.Sb[h].ap, self.chains[h].S.ap, [self.chains[h].S], [self.Sb[h]], eng="pool")
        self.tap("Sf0", self.Sf[0].ap, [self.Sf[0]], 128)
        self.tap("Sb0", self.Sb[0].ap, [self.Sb[0]], 128)
        self.end_phase()

    def phase_own_gdn(self):
        din = self.din
        ROPE_OWN = NSU * SU
        for p in range(2):
            self.begin_phase(43200, 0)
            A = self.arena
            self.chains = [self.new_chain(i, f"co{i}") for i in range(4)]
            xsets, sq, rs = self.alloc_xt(512, nbuf=1)
            wt = [A.alloc(1024, f"wt{i}") for i in range(2)]
            wz = A.alloc(8 * 256, "wz")
            self.load_w(wz, 3584 + 256 * p, 256)
            wg = A.alloc(8 * 16, "wg")
            self.load_w(wg, 4096, 16)
            raw = [A.alloc(SEGT + 4, f"rawo{i}") for i in range(6)]
            oT = [A.alloc(SEGT, f"oT{i}") for i in range(2)]
            gz = [A.alloc(SEGT, f"gz{i}") for i in range(2)]
            graw = A.alloc(256, "graw")
            beta = A.alloc(128, "beta")
            g = A.alloc(128, "g")
            wk = [A.alloc(128, "gwk0"), A.alloc(128, "gwk1")]
            tmp = [A.alloc(512, "tmp0"), A.alloc(512, "tmp1")]
            rct = A.alloc(512, "rct")
            rst = A.alloc(512, "rst")
            cols = [2048 + 128 * (2 * p), 2048 + 128 * (2 * p + 1), 2560 + 128 * (2 * p), 2560 + 128 * (2 * p + 1),
                    3072 + 128 * (2 * p), 3072 + 128 * (2 * p + 1)]
            self.wti = 0

            def wget(ci):
                self.wti = (self.wti + 1) % 2
                b = wt[self.wti]
                self.load_w(b, cols[ci], 128)
                return (lambda kc, b=b: b.ap[:, kc * 128:(kc + 1) * 128]), [b]

            def extra(j, xt, NT):
                for hh in range(2):
                    def ev(ps, hh=hh, j=j):
                        self.act(gz[hh].ap[:, j * NT:(j + 1) * NT], ps.ap[:, 0:NT], AF.Silu, [ps], [gz[hh]])
                    self.proj_fm(xt, NT, lambda kc, hh=hh: wz.ap[:, kc * 256 + hh * 128: kc * 256 + (hh + 1) * 128], 128,
                                 [wz], ev)

            halo = [(din["xwT"][kc, :, 254:256], din["xwT"][kc, :, 2304:2306]) for kc in range(8)]
            convp = A.alloc(30, "convp")
            for ci, base in enumerate([2 * p, 2 * p + 1, 4 + 2 * p, 5 + 2 * p, 8 + 2 * p, 9 + 2 * p]):
                self.cp(convp.ap[:, ci * 5:(ci + 1) * 5], self.convO.ap[:, base * 5:(base + 1) * 5], [self.convO], [convp],
                        eng="pool")
            self.prep_unit(lambda j, NT: [din["xwT"][kc, :, 256 + j * NT:256 + (j + 1) * NT] for kc in range(8)], halo, SEGT, 0,
                           ["q", "q", "k", "k", "v", "v"], wget, raw, convp, 0, self.hmO.ap[:, 0:4], self.hmO,
                           lambda kc: wg.ap[:, kc * 16:(kc + 1) * 16], [wg], 16, graw, ROPE_OWN, xsets, sq, rs, tmp, rct, rst,
                           extra=extra)
            self.gate_math(graw, 16, 16, None, self.dtO.ap, self.nAO.ap, [self.dtO, self.nAO], beta, g, wk)
            if p == 0:
                self.tap("own_q0", raw[0].ap[:, 0:256], [raw[0]], 256)
                self.tap("own_k0", raw[2].ap[:, 0:256], [raw[2]], 256)
            chains = self.chains
            for hh in range(2):
                h = 2 * p + hh
                self.cp(chains[2 * hh].S.ap, self.Sf[h].ap, [self.Sf[h]], [chains[2 * hh].S], eng="pool")
                self.cp(chains[2 * hh + 1].S.ap, self.Sb[h].ap, [self.Sb[h]], [chains[2 * hh + 1].S], eng="pool")
                self.memset(oT[hh].ap, 0.0, [oT[hh]])
            for s in range(16):
                gens = []
                for hh in range(2):
                    h = 2 * p + hh
                    for dr in range(2):
                        c = s if dr == 0 else 15 - s
                        idx = c * 8 + dr * 4 + h
                        sl = slice(c * 128, (c + 1) * 128)
                        gens.append(self.gdn_chunk(chains[2 * hh + dr], raw[2 + hh].ap[:, sl], raw[4 + hh].ap[:, sl],
                                                   raw[hh].ap[:, sl], [raw[hh], raw[2 + hh], raw[4 + hh]],
                                                   beta.ap[:, idx:idx + 1], g.ap[:, idx:idx + 1],
                                                   [beta, g], dr == 0, out_ap=oT[hh].ap[:, sl], out_buf=oT[hh]))
                self.run_rr(gens)
            for hh in range(2):
                h = 2 * p + hh
                for j in range(4):
                    o = oT[hh].ap[:, j * 512:(j + 1) * 512]
                    sqb = sq[j % 2]
                    self.act(sqb.ap, o, AF.Square, [oT[hh]], [sqb])
                    ps = self.PS[2]
                    self.mm(ps.ap, self.Ones, sqb.ap, True, True, [sqb, self.cm], [ps])
                    self.act(rs.ap, ps.ap, AF.Sqrt, [ps], [rs], scale=1.0 / 128, bias=EPS)
                    self.recip(rs.ap, rs.ap, [rs], [rs])
                    self.stt(o, o, self.gnw.ap[:, 0:1], rs.ap, ALU.mult, ALU.mult, [oT[hh], self.gnw, rs], [oT[hh]])
                    self.tt(o, o, gz[hh].ap[:, j * 512:(j + 1) * 512], ALU.mult, [oT[hh], gz[hh]], [oT[hh]], eng="pool")
                ob = Buf(None, "scrw")
                self.dma(self.scr[4 + h], oT[hh].ap, r=[oT[hh]], w=[ob])
                self.scr_bufs.append(ob)
                if h == 0:
                    self.tap("og0", oT[0].ap[:, 0:256], [oT[0]], 256)
            self.end_phase()

    def phase_natten(self):
        din = self.din
        psw = [(self.PS[2], self.PS[3], self.PSWt[1][:, :]), (self.PS[4], self.PS[5], self.PSWt[2][:, :])]
        for p in range(4):
            self.begin_phase(17000, 25300)
            A = self.arena
            RA = self.rarena
            xsets, sq, rs = self.alloc_xt(512, nbuf=2, r=True)
            wq = RA.alloc(1024, "wq"); wk_ = RA.alloc(1024, "wk"); wv = RA.alloc(1024, "wv"); wz = RA.alloc(1024, "wz")
            self.load_w(wq, 128 * p, 128, r=True)
            self.load_w(wk_, 512 + 128 * p, 128, r=True)
            self.load_w(wv, 1024 + 128 * p, 128, r=True)
            self.load_w(wz, 1536 + 128 * p, 128, r=True)
            kT = RA.alloc(WIN, "kT")
            qT = RA.alloc(SEGT, "qT")
            zT = A.alloc(SEGT, "zT")
            on = zT
            otmp = [A.alloc(128, f"otmp{i}") for i in range(2)]
            va = [RA.alloc(256, f"va{t}") for t in range(20)]
            bias = [A.alloc(3840, f"nab{hh}") for hh in range(2)]
            Eb = [RA.alloc(1024, f"Eb{i}") for i in range(2)]
            rc = [A.alloc(128, f"rc{i}") for i in range(2)]
            for hh in range(2):
                self.dma(bias[hh].ap, din["nab"][2 * p + hh], w=[bias[hh]])
            cur = xsets[0]
            self.gen_h([din["xwT"][kc, :, 0:512] for kc in range(8)], 512, 0, cur, sq, rs)
            for j in range(5):
                xt = cur
                if j + 1 < 5:
                    cur = xsets[(j + 1) % 2]
                    self.gen_h([din["xwT"][kc, :, (j + 1) * 512:(j + 2) * 512] for kc in range(8)], 512, 0, cur, sq, rs)
                def evk(ps, j=j):
                    self.act(kT.ap[:, j * 512:(j + 1) * 512], ps.ap, AF.Copy, [ps], [kT])
                self.proj_fm(xt, 512, lambda kc: wk_.ap[:, kc * 128:(kc + 1) * 128], 128, [wk_], evk)
                for t in range(4):
                    ps = self.nps()
                    for kc in range(8):
                        self.mm(ps.ap[:, 0:128], xt[kc].ap[:, t * 128:(t + 1) * 128], wv.ap[:, kc * 128:(kc + 1) * 128],
                                kc == 0, kc == 7, [xt[kc], wv], [ps])
                    vb = va[4 * j + t]
                    v3 = vb.ap.rearrange("p (h c) -> p h c", c=128)
                    p3 = ps.ap[:, 0:128].rearrange("p (h c) -> p h c", c=64)
                    self.act(v3[:, :, 0:64], p3, AF.Copy, [ps], [vb])
                    self.act(v3[:, :, 64:128], p3, AF.Identity, [ps], [vb], scale=0.0, bias=1.0)
                lo = max(j * 512, 256)
                hi = min((j + 1) * 512, 2304)
                n = hi - lo
                def evq(ps, lo=lo, n=n):
                    self.act(qT.ap[:, lo - 256:lo - 256 + n], ps.ap[:, 0:n], AF.Copy, [ps], [qT])
                self.proj_fm(xt, n, lambda kc: wq.ap[:, kc * 128:(kc + 1) * 128], 128, [wq], evq, c0=lo - j * 512)
                def evz(ps, lo=lo, n=n):
                    self.act(zT.ap[:, lo - 256:lo - 256 + n], ps.ap[:, 0:n], AF.Silu, [ps], [zT])
                self.proj_fm(xt, n, lambda kc: wz.ap[:, kc * 128:(kc + 1) * 128], 128, [wz], evz, c0=lo - j * 512)
            iters = [(gq, hh) for gq in range(16) for hh in range(2)]

            def emit_scores(it):
                gq, hh = iters[it]
                kt0 = min(gq, 14)
                hp = 64 * hh
                pa, pb, pw = psw[it % 2]
                qs = qT.ap[hp:hp + 64, gq * 128:(gq + 1) * 128]
                for t in range(6):
                    self.mm(pw[:, t * 128:(t + 1) * 128], kT.ap[hp:hp + 64, (kt0 + t) * 128:(kt0 + t + 1) * 128], qs,
                            True, True, [kT, qT], [pa, pb])
                for t in range(2):
                    self.mm(pw[:, (6 + t) * 128:(7 + t) * 128], self.kcT[p].ap[hp:hp + 64, t * 128:(t + 1) * 128], qs,
                            True, True, [self.kcT[p], qT], [pa, pb])

            pos = {}

            def emit_mid(it):
                gq, hh = iters[it]
                kt0 = min(gq, 14)
                cls = {0: 0, 1: 1, 14: 3, 15: 4}.get(gq, 2)
                pa, pb, pw = psw[it % 2]
                eb = Eb[it % 2]
                self.act(eb.ap[:, 768:1024], pw[:, 768:1024], AF.Exp, [pa, pb], [eb], scale=0.125)
                self.stt(eb.ap[:, 0:768], pw[:, 0:768], 0.125, bias[hh].ap[:, cls * 768:(cls + 1) * 768], ALU.mult, ALU.add,
                         [pa, pb, bias[hh]], [eb])
                self.act(eb.ap[:, 0:768], eb.ap[:, 0:768], AF.Exp, [eb], [eb])
                po = self.nps()
                for t in range(6):
                    self.mm(po.ap[:, 0:128], va[kt0 + t].ap[:, hh * 128:(hh + 1) * 128], eb.ap[:, t * 128:(t + 1) * 128],
                            t == 0, False, [va[kt0 + t], eb], [po])
                for t in range(2):
                    h = 2 * p + hh
                    self.mm(po.ap[:, 0:128], self.vca[t].ap[:, h * 128:(h + 1) * 128], eb.ap[:, (6 + t) * 128:(7 + t) * 128],
                            False, t == 1, [self.vca[t], eb], [po])
                pos[it] = po

            def emit_epi(it):
                gq, hh = iters[it]
                hp = 64 * hh
                po = pos.pop(it)
                rcb = rc[it % 2]
                ot = otmp[it % 2]
                self.recip(rcb.ap[0:64, :], po.ap[64:128, 0:128], [po], [rcb])
                self.tt(ot.ap[hp:hp + 64, :], po.ap[0:64, 0:128], rcb.ap[0:64, :], ALU.mult, [po, rcb], [ot])
                zs = zT.ap[hp:hp + 64, gq * 128:(gq + 1) * 128]
                self.tt(zs, zs, ot.ap[hp:hp + 64, :], ALU.mult, [zT, ot], [zT])

            emit_scores(0)
            for it in range(len(iters)):
                if it + 1 < len(iters):
                    emit_scores(it + 1)
                emit_mid(it)
                if it >= 1:
                    emit_epi(it - 1)
            emit_epi(len(iters) - 1)
            if p == 0:
                self.tap("na_o0", on.ap[:, 0:256], [on], 256)
            ob = Buf(None, "scrw")
            self.dma(self.scr[p], on.ap, r=[on], w=[ob])
            self.scr_bufs.append(ob)
            self.end_phase()

    def phase_final(self):
        din = self.din
        self.begin_phase(16500, 16600)
        A = self.arena
        RA = self.rarena
        wo = RA.alloc(8 * 1024, "wo")
        self.load_w(wo, 0, 1024, r=True, src="w_out")
        gtf = [[A.alloc(512, f"gtf{i}_{kc}") for kc in range(8)] for i in range(2)]
        gt = [[RA.alloc(512, f"gt{i}_{kc}") for kc in range(8)] for i in range(2)]
        ysb = [A.alloc(1024, f"ysb{i}") for i in range(2)]
        ysq = A.alloc(1024, "ysq")
        xo = [A.alloc(1024, f"xo{i}") for i in range(2)]
        ss = A.alloc(4, "ss")
        for jt in range(4):
            g8 = gt[jt % 2]
            g8f = gtf[jt % 2]
            for kc in range(8):
                self.dma(g8f[kc].ap, self.scr[kc, :, jt * 512:(jt + 1) * 512], r=self.scr_bufs, w=[g8f[kc]])
                if kc % 2 == 0:
                    self.act(g8[kc].ap, g8f[kc].ap, AF.Copy, [g8f[kc]], [g8[kc]])
                else:
                    self.cp(g8[kc].ap, g8f[kc].ap, [g8f[kc]], [g8[kc]])
            for s in range(4):
                t = jt * 4 + s
                y = ysb[t % 2]
                x_ = xo[t % 2]
                self.dma(x_.ap, din["xown"][t * 128:(t + 1) * 128, :], w=[x_])
                for n in range(2):
                    ps = self.nps()
                    for kc in range(8):
                        self.mm(ps.ap, g8[kc].ap[:, s * 128:(s + 1) * 128], wo.ap[:, kc * 1024 + n * 512: kc * 1024 + (n + 1) * 512],
                                kc == 0, kc == 7, [g8[kc], wo], [ps])
                    self.act(y.ap[:, n * 512:(n + 1) * 512], ps.ap, AF.Copy, [ps], [y])
                self.act(ysq.ap, y.ap, AF.Square, [y], [ysq])
                self.S.add("dve", lambda E, o=ss.ap[:, 0:1], i=ysq.ap: E.reduce_sum(out=o, in_=i, axis=mybir.AxisListType.X),
                           [ysq], [ss])
                self.act(ss.ap[:, 1:2], ss.ap[:, 0:1], AF.Sqrt, [ss], [ss], scale=1.0 / D, bias=EPS)
                self.recip(ss.ap[:, 2:3], ss.ap[:, 1:2], [ss], [ss])
                self.stt(y.ap, y.ap, ss.ap[:, 2:3], self.G2.ap, ALU.mult, ALU.mult, [y, ss, self.G2], [y])
                self.tt(y.ap, y.ap, x_.ap, ALU.add, [y, x_], [y], eng="pool")
                ob = Buf(None, "outw")
                self.dma(self.out[t * 128:(t + 1) * 128, :], y.ap, r=[y], w=[ob])
                self.outs.append(ob)
        self.phase_stack.close()


def build_program(stop="full", dbg=False):
    b = Builder(stop=stop, dbg=dbg)
    b.scr_bufs = []
    nc = b.build()
    return nc, b


_CACHE = {}


def kernel(**inputs):
    per_core = host_prep(inputs)
    nc, b = build_program()
    res = run_bass_kernel_spmd(nc, per_core, core_ids=list(range(NCORES)))
    out = np.concatenate([np.asarray(r["out"], np.float32) for r in res.results], 0)
    return out.reshape(1, SEQ, D).astype(np.float32)
```

```python
import os
import numpy as np
from contextlib import ExitStack
import concourse.bass as bass
import concourse.mybir as mybir
from concourse.bass_utils import run_bass_kernel_spmd

F32 = mybir.dt.float32
F32R = mybir.dt.float32r
AF = mybir.ActivationFunctionType
ALU = mybir.AluOpType

NCORES = 8
D = 1024
KC = 8
SEQ = 16384
SEGT = 2048
WIN = 2560
CTXL = 256
SU = 1024
NSU = 14
IN_DIM = 4112
EPS = 1e-6
NEG = -30000.0
PCOLS = 6400
PRCOLS = 3328
TOTCOLS = 53000

ENGS = ("pe", "act", "dve", "pool", "sp")


class Buf:
    _n = 0

    def __init__(self, ap, name="", key=None, excl=False):
        self.ap = ap
        if key is None:
            Buf._n += 1
            key = Buf._n
        self.key = key
        self.name = name
        self.excl = excl

    def v(self, a, b):
        return self.ap[:, a:b]

    def t(self, i, n):
        return self.ap[:, i * n:(i + 1) * n]


class Sched:
    def __init__(self, nc, n_dma_sems=24, sems_per_eng=4, blk=4096):
        self.nc = nc
        self.ops = []
        self.n_dma_sems = n_dma_sems
        self.sems_per_eng = sems_per_eng
        self.blk = blk
        self.BAR = Buf(None, "BAR")

    def add(self, eng, fn, reads=(), writes=(), dma=False, uwrites=()):
        r = [b.key for b in reads if not b.excl]
        r.append(self.BAR.key)
        wr = list(dict.fromkeys([b.key for b in writes] + [b.key for b in reads if b.excl]))
        self.ops.append(dict(eng=eng, fn=fn, reads=r, writes=wr, dma=dma,
                             uwrites=[b.key for b in uwrites]))

    def barrier(self):
        self.ops.append(dict(eng="pool", fn=None, reads=[], writes=[self.BAR.key], dma=False, uwrites=[]))

    def emit(self, stack):
        nc = self.nc
        ops = self.ops
        n = len(ops)
        last_w = {}
        readers = {}
        deps = [set() for _ in range(n)]
        unord = {}
        for i, op in enumerate(ops):
            for k in op["reads"]:
                deps[i].update(last_w.get(k, ()))
            for k in op["writes"]:
                deps[i].update(last_w.get(k, ()))
                deps[i].update(readers.get(k, ()))
            for k in op["uwrites"]:
                deps[i].update(readers.get(k, ()))
                if not (unord.get(k, False) and not readers.get(k)):
                    deps[i].update(last_w.get(k, ()))
            deps[i].discard(i)
            for k in op["reads"]:
                readers.setdefault(k, []).append(i)
            for k in op["writes"]:
                last_w[k] = [i]
                readers[k] = []
                unord[k] = False
            for k in op["uwrites"]:
                if unord.get(k, False) and not readers.get(k):
                    last_w[k].append(i)
                else:
                    last_w[k] = [i]
                    readers[k] = []
                    unord[k] = True
        need_sig = [False] * n
        for i, op in enumerate(ops):
            if op["dma"]:
                need_sig[i] = True
            for d in deps[i]:
                od = ops[d]
                if od["dma"] or op["dma"] or od["eng"] != op["eng"]:
                    need_sig[d] = True
                elif od["eng"] != "pe":
                    need_sig[d] = True
        for i, op in enumerate(ops):
            if op["fn"] is None and not op["dma"]:
                pass
        eng_sems = {e: [stack.enter_context(nc.semaphore(f"s_{e}{j}")) for j in range(self.sems_per_eng)]
                    for e in ENGS if e != "sp"}
        dma_sems = [stack.enter_context(nc.semaphore(f"s_dma{j}")) for j in range(self.n_dma_sems)]
        sig = [None] * n
        eng_cnt = {e: 0 for e in ENGS}
        dma_use = [0] * self.n_dma_sems
        dma_idx = 0
        dma_prev = [None] * n
        for i, op in enumerate(ops):
            if not need_sig[i]:
                continue
            if op["dma"]:
                s = dma_idx % self.n_dma_sems
                dma_idx += 1
                if dma_use[s] > 0:
                    dma_prev[i] = (dma_sems[s], 16 * dma_use[s])
                dma_use[s] += 1
                sig[i] = (dma_sems[s], 16 * dma_use[s], 16)
            else:
                e = op["eng"]
                c = eng_cnt[e]
                eng_cnt[e] += 1
                blk_i = c // self.blk
                s = blk_i % self.sems_per_eng
                val = (blk_i // self.sems_per_eng) * self.blk + (c % self.blk) + 1
                sig[i] = (eng_sems[e][s], val, 1)
        known = {e: {} for e in ENGS}
        waits = [[] for _ in range(n)]
        for i, op in enumerate(ops):
            e = op["eng"]
            wl = []
            if dma_prev[i] is not None:
                wl.append(dma_prev[i])
            for d in deps[i]:
                od = ops[d]
                if sig[d] is None:
                    continue
                if not od["dma"] and not op["dma"] and od["eng"] == e and e == "pe":
                    continue
                wl.append((sig[d][0], sig[d][1]))
            best = {}
            for (s, v) in wl:
                key = id(s)
                if v > known[e].get(key, 0):
                    if key not in best or best[key][1] < v:
                        best[key] = (s, v)
            for key, (s, v) in best.items():
                known[e][key] = v
                waits[i].append((s, v))
        self.n_waits = sum(len(w) for w in waits)
        self.n_sigs = sum(1 for s in sig if s is not None)
        with nc.Block() as block:
            def run(ename):
                def body(eng):
                    for i, op in enumerate(ops):
                        if op["eng"] != ename:
                            continue
                        for (s, v) in waits[i]:
                            eng.wait_ge(s, v)
                        if op["fn"] is None:
                            if sig[i] is not None:
                                eng.engine_nop().then_inc(sig[i][0], sig[i][2])
                            continue
                        ins = op["fn"](eng)
                        if sig[i] is not None:
                            ins.then_inc(sig[i][0], sig[i][2])
                return body
            block.sync(run("sp"))
            block.scalar(run("act"))
            block.vector(run("dve"))
            block.gpsimd(run("pool"))
            block.tensor(run("pe"))


def _rope_tables():
    t = np.arange(SEQ)
    row = (t // 64).astype(np.float32)
    col = (t % 64).astype(np.float32)
    inv = (np.float32(10000.0) ** (-np.arange(0, 64, 2, dtype=np.float32) / np.float32(64))).astype(np.float32)
    ar = (row[:, None] * inv[None, :]).astype(np.float32)
    ac = (col[:, None] * inv[None, :]).astype(np.float32)
    cr, sr, cc, sc = np.cos(ar), np.sin(ar), np.cos(ac), np.sin(ac)
    COS = np.concatenate([cr, cr, cc, cc], 1).T.astype(np.float32)
    SIN = np.concatenate([-sr, sr, -sc, sc], 1).T.astype(np.float32)
    return np.ascontiguousarray(COS), np.ascontiguousarray(SIN)


def _consts():
    i = np.arange(128)
    Linc = (i[:, None] >= i[None, :]).astype(np.float32)
    Lst = (i[:, None] > i[None, :]).astype(np.float32)
    Uinc = (i[:, None] <= i[None, :]).astype(np.float32)
    Ust = (i[:, None] < i[None, :]).astype(np.float32)
    I = np.eye(128, dtype=np.float32)
    ones = np.ones((128, 128), np.float32)
    cm = np.concatenate([Linc, Lst, Uinc, Ust, I, ones], 1)
    perm = np.concatenate([np.arange(32, 64), np.arange(0, 32), np.arange(96, 128), np.arange(64, 96)])
    Pm = np.zeros((128, 128), np.float32)
    Pm[np.arange(128), perm] = 1.0
    return np.ascontiguousarray(cm), np.ascontiguousarray(Pm.T)


def _na_bias_tables(rpb, core):
    base = 32 * core - 4
    out = np.full((8, 128, 5, 6, 128), NEG, np.float32)
    p = np.arange(128)
    q = np.arange(128)
    kcol = p % 64
    qcol = q % 64
    cs = np.clip(qcol - 8, 0, 48)
    colok = (kcol[:, None] >= cs[None, :]) & (kcol[:, None] < cs[None, :] + 16)
    coloff = kcol[:, None] - qcol[None, :] + 15
    for ci, g in enumerate([0, 1, 2, 14, 15]):
        kt0 = min(g, 14)
        for t in range(6):
            wrow = 2 * (kt0 + t) + p // 64
            krow = base + wrow
            j = 2 * g + q // 64
            r = 32 * core + j
            rs = np.clip(r - 4, 0, 248)
            rowok = (krow[:, None] >= rs[None, :]) & (krow[:, None] < rs[None, :] + 8)
            rowoff = krow[:, None] - r[None, :] + 7
            ok = rowok & colok
            ro = np.clip(rowoff, 0, 14)
            co = np.clip(coloff, 0, 30)
            vals = rpb[:, ro, co]
            out[:, :, ci, t, :] = np.where(ok[None], vals, NEG)
    return np.ascontiguousarray(out.reshape(8, 128, 5 * 6 * 128))


def host_prep(inp):
    x = np.asarray(inp["x"], np.float32)[0]
    ctx = np.asarray(inp["ctx"], np.float32)[0]
    c = np.asarray(inp["c"], np.float32)[0]
    c_ctx = np.asarray(inp["c_ctx"], np.float32)
    w_ada = np.ascontiguousarray(np.asarray(inp["w_ada"], np.float32)[0])
    b_ada = np.asarray(inp["b_ada"], np.float32)[0]
    g_pre = np.asarray(inp["g_pre"], np.float32)[0]
    g_post = np.asarray(inp["g_post"], np.float32)[0]
    w_in = np.ascontiguousarray(np.asarray(inp["w_in"], np.float32)[0])
    conv_w = np.asarray(inp["conv_w"], np.float32)[0]
    rpb = np.asarray(inp["rpb"], np.float32)[0]
    A_log = np.asarray(inp["A_log"], np.float32)[0]
    dt_bias = np.asarray(inp["dt_bias"], np.float32)[0]
    gnw = np.asarray(inp["gdn_norm_w"], np.float32)[0]
    w_out = np.ascontiguousarray(np.asarray(inp["w_out"], np.float32)[0])

    COS, SIN = _rope_tables()
    cmask, ProtT = _consts()
    xpad = np.zeros((SEQ + 8, D), np.float32)
    xpad[4:4 + SEQ] = x

    def colT(v, n):
        return np.ascontiguousarray(v.reshape(n, 128).T)

    cvec = np.ascontiguousarray(np.stack([c.reshape(8, 128).T, c_ctx.reshape(8, 128).T], -1).reshape(128, 16))
    shared = dict(
        ctxT=np.ascontiguousarray(ctx.T.reshape(8, 128, CTXL)),
        cvec=cvec, w_ada=w_ada, b_adaT=colT(b_ada, 24),
        b_gate_rep=np.ascontiguousarray(np.broadcast_to(b_ada[2048:3072], (128, 1024))),
        g_preT=colT(g_pre, 8),
        g_post_rep=np.ascontiguousarray(np.broadcast_to(g_post, (128, 1024))),
        w_in=w_in, w_out=w_out, cmask=cmask, ProtT=ProtT,
        gnw=np.ascontiguousarray(gnw.reshape(128, 1)),
        AlO=np.ascontiguousarray(np.broadcast_to(np.tile(A_log.reshape(8), 16), (128, 128))),
        dtO=np.ascontiguousarray(np.broadcast_to(np.tile(dt_bias.reshape(8), 16), (128, 128))),
    )
    shared["convO"] = np.ascontiguousarray(conv_w.T.reshape(12, 128, 5).transpose(1, 0, 2).reshape(128, 60))
    conv_kv = conv_w[:, 512:1536]
    conv_f = conv_kv.T.reshape(8, 128, 5).transpose(1, 0, 2).reshape(128, 40)
    conv_b = conv_kv[::-1].T.reshape(8, 128, 5).transpose(1, 0, 2).reshape(128, 40)

    per_core = []
    for i in range(NCORES):
        T0 = SEGT * i
        d = dict(shared)
        win = np.zeros((WIN, D), np.float32)
        lo, hi = T0 - 256, T0 + 2304
        a, b = max(lo, 0), min(hi, SEQ)
        win[a - lo:b - lo] = x[a:b]
        d["xwT"] = np.ascontiguousarray(win.T.reshape(8, 128, WIN))
        d["xown"] = np.ascontiguousarray(x[T0:T0 + SEGT])
        nf = 2 * i
        toks = np.empty((NSU, SU + 4), np.int64)
        isf = np.zeros(NSU, bool)
        for u in range(NSU):
            pidx = np.arange(-2, SU + 2)
            if u < nf:
                toks[u] = SU * u + pidx
                isf[u] = True
            else:
                v = u - nf
                toks[u] = (SEQ - 1) - (SU * v + pidx)
        valid = (toks >= 0) & (toks < SEQ)
        xs = xpad[np.clip(toks, -4, SEQ + 3) + 4]
        xs = xs * valid[..., None]
        xsT = xs.transpose(0, 2, 1).reshape(NSU, 8, 128, SU + 4)
        d["xsT"] = np.ascontiguousarray(xsT[..., 2:SU + 2])
        d["xsH"] = np.ascontiguousarray(np.concatenate([xsT[..., 0:2], xsT[..., SU + 2:SU + 4]], -1))
        hm = np.concatenate([valid[:, 0:2], valid[:, SU + 2:SU + 4]], 1).astype(np.float32)
        d["hmS"] = np.ascontiguousarray(np.broadcast_to(hm.reshape(1, NSU * 4), (128, NSU * 4)))
        hmo = np.ones((2, 4), np.float32)
        if i == 0:
            hmo[0, 0:2] = 0
        if i == NCORES - 1:
            hmo[0, 2:4] = 0
        d["hmO"] = np.ascontiguousarray(np.broadcast_to(hmo.reshape(1, 8), (128, 8)))
        wg = np.empty((NSU, D, 8), np.float32)
        al = np.empty((NSU, 4), np.float32)
        dtb = np.empty((NSU, 4), np.float32)
        cv = np.empty((NSU, 128, 40), np.float32)
        for u in range(NSU):
            dr = 0 if isf[u] else 1
            wg[u, :, 0:4] = w_in[:, 4096 + 4 * dr:4096 + 4 * dr + 4]
            wg[u, :, 4:8] = w_in[:, 4104 + 4 * dr:4104 + 4 * dr + 4]
            al[u] = A_log[dr]
            dtb[u] = dt_bias[dr]
            cv[u] = conv_f if isf[u] else conv_b
        d["w_gs"] = wg
        d["AlS"] = np.ascontiguousarray(np.broadcast_to(np.tile(al[:, None, :], (1, 8, 1)).reshape(1, NSU * 32), (128, NSU * 32)))
        d["dtS"] = np.ascontiguousarray(np.broadcast_to(np.tile(dtb[:, None, :], (1, 8, 1)).reshape(1, NSU * 32), (128, NSU * 32)))
        d["convS"] = cv
        tk = np.clip(toks[:, 2:SU + 2], 0, SEQ - 1).reshape(-1)
        tk = np.concatenate([tk, np.arange(T0, T0 + SEGT)])
        d["ropeC"] = np.ascontiguousarray(COS[:, tk])
        d["ropeS"] = np.ascontiguousarray(SIN[:, tk])
        ms = np.zeros((8,), np.float32)
        ms[i] = 1.0
        d["mseg"] = np.ascontiguousarray(np.broadcast_to(ms, (128, 8)))
        d["nab"] = _na_bias_tables(rpb, i)
        per_core.append(d)
    return per_core


IN_SHAPES = dict(
    xwT=[8, 128, WIN], xown=[SEGT, D], xsT=[NSU, 8, 128, SU], xsH=[NSU, 8, 128, 4], ctxT=[8, 128, CTXL],
    cvec=[128, 16], w_ada=[D, 3072], b_adaT=[128, 24], b_gate_rep=[128, 1024], g_preT=[128, 8],
    g_post_rep=[128, 1024], w_in=[D, IN_DIM], w_out=[D, D], cmask=[128, 768], ProtT=[128, 128],
    gnw=[128, 1], AlO=[128, 128], dtO=[128, 128], convO=[128, 60], hmS=[128, NSU * 4], hmO=[128, 8],
    w_gs=[NSU, D, 8], AlS=[128, NSU * 32], dtS=[128, NSU * 32], convS=[NSU, 128, 40],
    ropeC=[128, SEQ], ropeS=[128, SEQ], mseg=[128, 8], nab=[8, 128, 3840],
)


class Arena:
    def __init__(self, ap, size):
        self.ap = ap
        self.size = size
        self.off = 0

    def alloc(self, cols, name=""):
        o = self.off
        self.off += cols
        assert self.off <= self.size, f"arena overflow at {name}: {self.off} > {self.size}"
        return Buf(self.ap[:, o:o + cols], name)

    def mark(self):
        return self.off

    def release(self, m):
        self.off = m


class Chain:
    pass


class LR(list):
    r = False
    x = None


class Builder:
    def __init__(self, stop="full", dbg=False):
        self.stop = stop
        self.nc = bass.Bass("TRN2", target_bir_lowering=False)
        nc = self.nc
        self.din = {k: nc.dram_tensor(k, shp, F32, kind="ExternalInput").ap() for k, shp in IN_SHAPES.items()}
        self.out = nc.dram_tensor("out", [SEGT, D], F32, kind="ExternalOutput").ap()
        self.scr = nc.dram_tensor("scr_gated", [8, 128, SEGT], F32).ap()
        self.dbg = nc.dram_tensor("dbg", [128, 8192], F32, kind="ExternalOutput").ap() if dbg else None
        self.dbg_off = 0
        self.dbg_map = {}
        self.outs = []
        self.dmaq = 0

    def dma(self, out, in_, r=(), w=(), uw=(), eng=None):
        if eng is None:
            eng = "sp"
        self.S.add(eng, lambda E: E.dma_start(out=out, in_=in_), r, w, dma=True, uwrites=uw)

    def act(self, out, in_, func, r, w, scale=None, bias=None):
        kw = dict(out=out, in_=in_, func=func)
        if scale is not None:
            kw["scale"] = scale
        if bias is not None:
            kw["bias"] = bias
        self.S.add("act", lambda E: E.activation(**kw), r, w)

    def ts(self, out, in0, s1, s2, op0, op1, r, w, eng="dve"):
        kw = dict(out=out, in0=in0, scalar1=s1, scalar2=s2, op0=op0)
        if op1 is not None:
            kw["op1"] = op1
        self.S.add(eng, lambda E: E.tensor_scalar(**kw), r, w)

    def tt(self, out, in0, in1, op, r, w, eng="dve"):
        self.S.add(eng, lambda E: E.tensor_tensor(out=out, in0=in0, in1=in1, op=op), r, w)

    def stt(self, out, in0, scalar, in1, op0, op1, r, w, eng="dve"):
        eng = "dve"
        self.S.add(eng, lambda E: E.scalar_tensor_tensor(out=out, in0=in0, scalar=scalar, in1=in1,
                                                         op0=op0, op1=op1), r, w)

    def cp(self, out, in_, r, w, eng="dve"):
        self.S.add(eng, lambda E: E.tensor_copy(out=out, in_=in_), r, w)

    def recip(self, out, in_, r, w):
        self.S.add("dve", lambda E: E.reciprocal(out=out, in_=in_), r, w)

    def memset(self, ap, val, w, eng="pool"):
        self.S.add(eng, lambda E: E.memset(ap, val), (), w)

    def mm(self, out, lhsT, rhs, start, stop, r, w):
        self.S.add("pe", lambda E: E.matmul(out, lhsT=lhsT, rhs=rhs, start=start, stop=stop), r, w)

    def tr(self, out, in_, r, w):
        ident = self.Id
        self.S.add("pe", lambda E: E.transpose(out=out, in_=in_, identity=ident), list(r) + [self.cm], w)

    def const(self, name, cols):
        b = self.parena.alloc(cols, name)
        self.dma(b.ap, self.din[name], w=[b])
        return b

    def tap(self, name, ap, r, cols):
        if self.dbg is None:
            return
        o = self.dbg_off
        self.dbg_off += cols
        assert self.dbg_off <= 8192
        self.dbg_map[name] = (o, cols)
        ob = Buf(None, "dbg_" + name)
        self.dma(self.dbg[:, o:o + cols], ap, r=r, w=[ob])
        self.outs.append(ob)

    def nps(self):
        self.psi = (self.psi + 1) % 2
        return self.PS[self.psi]

    def build(self):
        nc = self.nc
        with ExitStack() as st:
            self.S = Sched(nc)
            pt_ = st.enter_context(nc.sbuf_tensor("parena", [128, PCOLS], F32))
            self.parena = Arena(pt_[:, :], PCOLS)
            prt_ = st.enter_context(nc.sbuf_tensor("prarena", [128, PRCOLS], F32R))
            self.prarena = Arena(prt_[:, :], PRCOLS)
            self.arena = None
            self.rarena = None
            self.phase_stack = None
            self.phase_id = 0
            self.PS = []
            for j in range(4):
                pt = st.enter_context(nc.psum_tensor(f"ps{j}", [128, 1024], F32))
                self.PS.append(Buf(pt[:, 0:512], f"ps{j}a", excl=True))
                self.PS.append(Buf(pt[:, 512:1024], f"ps{j}b", excl=True))
                self.PSWt = getattr(self, "PSWt", []) + [pt]
            self.psi = 0
            self.cps = [[Buf(self.PS[4 + c].ap[:, 128 * s:128 * (s + 1)], f"cps{c}_{s}", key=self.PS[4 + c].key, excl=True)
                         for s in range(4)] for c in range(4)]
            self.program()
            fin = list(self.outs)
            self.S.add("sp", None, fin, [])
            self.S.emit(st)
        return nc

    def begin_phase(self, cols, rcols=0):
        assert cols + rcols + PCOLS + PRCOLS <= TOTCOLS, (cols, rcols)
        self.phase_id += 1
        self.phase_stack = ExitStack()
        t = self.phase_stack.enter_context(self.nc.sbuf_tensor(f"ph{self.phase_id}", [128, cols], F32))
        self.arena = Arena(t[:, :], cols)
        self.rarena = None
        self.stage = None
        if rcols:
            tr_ = self.phase_stack.enter_context(self.nc.sbuf_tensor(f"phr{self.phase_id}", [128, rcols], F32R))
            self.rarena = Arena(tr_[:, :], rcols)
            self.stage = [self.arena.alloc(1024, "stage0"), self.arena.alloc(1024, "stage1")]
            self.stage_i = 0

    def end_phase(self):
        self.phase_stack.close()
        self.arena = None
        self.rarena = None
        self.S.barrier()

    def program(self):
        self.phase0()
        if self.stop == "p0":
            return
        self.phase_ctx()
        if self.stop == "ctx":
            return
        self.phase_stream()
        if self.stop == "stream":
            return
        self.phase_own_gdn()
        if self.stop == "gdn":
            return
        self.phase_natten()
        if self.stop == "na":
            return
        self.phase_final()

    def phase0(self):
        A = self.parena
        din = self.din
        self.cm = self.const("cmask", 768)
        self.Linc, self.Lst, self.Uinc, self.Ust, self.Id, self.Ones = [self.cm.t(i, 128) for i in range(6)]
        self.prot = self.const("ProtT", 128)
        self.gnw = self.const("gnw", 1)
        self.AlO = self.const("AlO", 128)
        self.dtO = self.const("dtO", 128)
        self.convO = self.const("convO", 60)
        self.hmO = self.const("hmO", 8)
        self.hmS = self.const("hmS", NSU * 4)
        self.AlS = self.const("AlS", NSU * 32)
        self.dtS = self.const("dtS", NSU * 32)
        self.mseg = self.const("mseg", 8)
        gpre = self.const("g_preT", 8)
        badaT = self.const("b_adaT", 24)
        cv = self.const("cvec", 16)
        self.mod = A.alloc(48, "mod")
        self.A1 = A.alloc(16, "A1")
        self.G2 = A.alloc(1024, "G2")
        self.nAO = A.alloc(128, "nAO")
        self.nAS = A.alloc(NSU * 32, "nAS")
        self.act(self.nAO.ap, self.AlO.ap, AF.Exp, [self.AlO], [self.nAO])
        self.ts(self.nAO.ap, self.nAO.ap, -1.0, None, ALU.mult, None, [self.nAO], [self.nAO])
        self.act(self.nAS.ap, self.AlS.ap, AF.Exp, [self.AlS], [self.nAS])
        self.ts(self.nAS.ap, self.nAS.ap, -1.0, None, ALU.mult, None, [self.nAS], [self.nAS])
        self.Ones_r = self.prarena.alloc(128, "Ones_r")
        self.cp(self.Ones_r.ap, self.Ones, [self.cm], [self.Ones_r])
        self.begin_phase(30000)
        A = self.arena
        csil = A.alloc(16, "csil")
        self.act(csil.ap, cv.ap, AF.Silu, [cv], [csil])
        wada = A.alloc(8 * 3072, "wada")
        for kc in range(8):
            self.dma(wada.v(kc * 3072, (kc + 1) * 3072), din["w_ada"][kc * 128:(kc + 1) * 128, :], uw=[wada])
        ps = self.PS[0]
        for ct in range(24):
            for kc in range(8):
                self.mm(ps.ap[:, ct * 2:ct * 2 + 2], wada.ap[:, kc * 3072 + ct * 128: kc * 3072 + (ct + 1) * 128],
                        csil.ap[:, kc * 2:kc * 2 + 2], kc == 0, kc == 7, [wada, csil], [ps])
        mod3 = self.mod.ap.rearrange("p (c w) -> p c w", w=2)
        ps3 = ps.ap[:, 0:48].rearrange("p (c w) -> p c w", w=2)
        for w in range(2):
            self.tt(mod3[:, :, w], ps3[:, :, w], badaT.ap, ALU.add, [ps, badaT], [self.mod])
        a13 = self.A1.ap.rearrange("p (c w) -> p c w", w=2)
        for w in range(2):
            self.stt(a13[:, :, w], mod3[:, 8:16, w], 1.0, gpre.ap, ALU.add, ALU.mult, [self.mod, gpre], [self.A1])
        rep = A.alloc(8 * 128, "rep")
        for kc in range(8):
            self.ts(rep.t(kc, 128), self.Ones, csil.ap[:, kc * 2:kc * 2 + 1], None, ALU.mult, None,
                    [self.cm, csil], [rep])
        bg = A.alloc(1024, "bgate")
        gp = A.alloc(1024, "gpost")
        self.dma(bg.ap, din["b_gate_rep"], w=[bg])
        self.dma(gp.ap, din["g_post_rep"], w=[gp])
        for n in range(2):
            pg = self.PS[2 + n]
            for kc in range(8):
                self.mm(pg.ap, rep.t(kc, 128), wada.ap[:, kc * 3072 + 2048 + n * 512: kc * 3072 + 2048 + (n + 1) * 512],
                        kc == 0, kc == 7, [rep, wada], [pg])
            self.tt(self.G2.t(n, 512), pg.ap, bg.t(n, 512), ALU.add, [pg, bg], [self.G2])
        self.tt(self.G2.ap, self.G2.ap, gp.ap, ALU.mult, [self.G2, gp], [self.G2])
        self.tap("mod", self.mod.ap, [self.mod], 48)
        self.tap("G2", self.G2.ap[:, 0:64], [self.G2], 64)
        self.end_phase()

    def gen_h(self, srcs, N, w, xt, sq, rs):
        xs = xt.x
        ones = self.Ones_r.ap if sq.r else self.Ones
        ones_b = self.Ones_r if sq.r else self.cm
        for kc in range(8):
            if isinstance(srcs[kc], tuple):
                self.dma(xs[kc].ap[:, 0:2], srcs[kc][0], uw=[xs[kc]])
                self.dma(xs[kc].ap[:, 2:4], srcs[kc][1], uw=[xs[kc]])
            else:
                self.dma(xs[kc].ap[:, 0:N], srcs[kc], w=[xs[kc]])
        ps = self.PS[2]
        for kc in range(8):
            sqb = sq[kc % 2]
            self.act(sqb.ap[:, 0:N], xs[kc].ap[:, 0:N], AF.Square, [xs[kc]], [sqb])
            self.mm(ps.ap[:, 0:N], ones, sqb.ap[:, 0:N], kc == 0, kc == 7, [sqb, ones_b], [ps])
        self.act(rs.ap[:, 0:N], ps.ap[:, 0:N], AF.Sqrt, [ps], [rs], scale=1.0 / D, bias=EPS)
        self.recip(rs.ap[:, 0:N], rs.ap[:, 0:N], [rs], [rs])
        for kc in range(8):
            self.tt(xs[kc].ap[:, 0:N], xs[kc].ap[:, 0:N], rs.ap[:, 0:N], ALU.mult, [xs[kc], rs], [xs[kc]], eng="pool")
            self.ts(xt[kc].ap[:, 0:N], xs[kc].ap[:, 0:N], self.A1.ap[:, kc * 2 + w:kc * 2 + w + 1],
                    self.mod.ap[:, kc * 2 + w:kc * 2 + w + 1], ALU.mult, ALU.add,
                    [xs[kc], self.A1, self.mod], [xt[kc]])

    def alloc_xt(self, N, nbuf=2, r=False):
        sets = LR()
        x_shared = None
        for j in range(nbuf):
            if r and x_shared is not None:
                x = x_shared
            else:
                x = [self.arena.alloc(N, f"xt{j}_{kc}") for kc in range(8)]
                x_shared = x
            if r:
                h = LR(self.rarena.alloc(N, f"ht{j}_{kc}") for kc in range(8))
            else:
                h = LR(x)
            h.x = x
            h.r = r
            sets.append(h)
        if r and nbuf >= 2:
            hh = LR(self.rarena.alloc(4, f"hth_{kc}") for kc in range(8))
            hh.x = [self.arena.alloc(4, f"xth_{kc}") for kc in range(8)]
            hh.r = True
            sets.x = hh
        src = self.rarena if r else self.arena
        sq = LR([src.alloc(N, "sq0"), src.alloc(N, "sq1")])
        sq.r = r
        rs = self.arena.alloc(N, "rs")
        return sets, sq, rs

    def proj_fm(self, xt, N, wfn, M, wbufs, evac, c0=0):
        ps = self.nps()
        for kc in range(8):
            self.mm(ps.ap[0:M, 0:N], wfn(kc), xt[kc].ap[:, c0:c0 + N], kc == 0, kc == 7, [xt[kc]] + list(wbufs), [ps])
        evac(ps)

    def prep_unit(self, main_fn, halo_srcs, n, w, cts, wget, raw, conv, conv_base, hm_ap, hm_buf,
                  gate_wfn, gate_wbufs, ng, graw, rope_off, xsets, sq, rs, tmp, rct, rst, extra=None):
        NT = min(512, n)
        ntile = n // NT
        rs_all = rs
        rs = rs[0] if isinstance(rs, list) else rs
        nct = len(cts)
        prefetch = len(xsets) >= 2 and getattr(xsets, "x", None) is not None
        cur = None
        if halo_srcs is not None:
            xth = xsets.x if prefetch else xsets[0]
            self.gen_h(halo_srcs, 4, w, xth, sq, rs)
            if prefetch:
                cur = xsets[0]
                self.gen_h(main_fn(0, NT), NT, w, cur, sq, rs)
            for ci in range(nct):
                wfn_c, wbufs = wget(ci)
                def ev(ps, ci=ci):
                    self.tt(raw[ci].ap[:, 0:2], ps.ap[:, 0:2], hm_ap[:, 0:2], ALU.mult, [ps, hm_buf], [raw[ci]])
                    self.tt(raw[ci].ap[:, n + 2:n + 4], ps.ap[:, 2:4], hm_ap[:, 2:4], ALU.mult, [ps, hm_buf], [raw[ci]])
                self.proj_fm(xth, 4, wfn_c, 128, wbufs, ev)
        else:
            for ci in range(nct):
                self.memset(raw[ci].ap[:, 0:2], 0.0, [raw[ci]])
                self.memset(raw[ci].ap[:, n + 2:n + 4], 0.0, [raw[ci]])
        for j in range(ntile):
            if prefetch:
                if cur is None:
                    cur = xsets[j % 2]
                    self.gen_h(main_fn(j, NT), NT, w, cur, sq, rs)
                xt = cur
                cur = None
                if j + 1 < ntile:
                    cur = xsets[(j + 1) % 2]
                    self.gen_h(main_fn(j + 1, NT), NT, w, cur, sq, rs)
            else:
                xt = xsets[(j + 1) % len(xsets)]
                self.gen_h(main_fn(j, NT), NT, w, xt, sq, rs)
            for ci in range(nct):
                wfn_c, wbufs = wget(ci)
                def ev(ps, ci=ci, j=j):
                    self.act(raw[ci].ap[:, 2 + NT * j:2 + NT * (j + 1)], ps.ap[:, 0:NT], AF.Copy, [ps], [raw[ci]])
                self.proj_fm(xt, NT, wfn_c, 128, wbufs, ev)
            if extra is not None:
                extra(j, xt, NT)
            ps = self.nps()
            for kc in range(8):
                self.mm(ps.ap[0:ng, 0:NT], gate_wfn(kc), xt[kc].ap[:, 0:NT], kc == 0, kc == 7,
                        [xt[kc]] + list(gate_wbufs), [ps])
            gT = tmp[0]
            self.act(gT.ap[0:ng, 0:NT], ps.ap[0:ng, 0:NT], AF.Copy, [ps], [gT])
            for c in range(NT // 128):
                ps2 = self.nps()
                idn = self.Id[0:ng, 0:ng]
                self.S.add("pe", lambda E, o=ps2.ap[:, 0:ng], i=gT.ap[0:ng, c * 128:(c + 1) * 128], idn=idn:
                           E.transpose(out=o, in_=i, identity=idn), [gT, self.cm], [ps2])
                cc = j * (NT // 128) + c
                self.cp(graw.ap[:, cc * ng:(cc + 1) * ng], ps2.ap[:, 0:ng], [ps2], [graw])
        for j in range(ntile):
            a = NT * j
            for ci in range(nct):
                eng = "dve"
                cb = conv_base + ci * 5
                self.ts(tmp[ci % 2].ap[:, 0:NT], raw[ci].ap[:, a:a + NT], conv.ap[:, cb:cb + 1], None, ALU.mult, None,
                        [raw[ci], conv], [tmp[ci % 2]], eng=eng)
                for tp in range(1, 5):
                    self.stt(tmp[ci % 2].ap[:, 0:NT], raw[ci].ap[:, a + tp:a + tp + NT], conv.ap[:, cb + tp:cb + tp + 1],
                             tmp[ci % 2].ap[:, 0:NT], ALU.mult, ALU.add, [raw[ci], conv, tmp[ci % 2]], [tmp[ci % 2]], eng=eng)
                self.act(raw[ci].ap[:, a:a + NT], tmp[ci % 2].ap[:, 0:NT], AF.Silu, [tmp[ci % 2]], [raw[ci]])
        rs_l = rs_all if isinstance(rs_all, list) else [rs_all]
        groups = [(j, ci) for j in range(ntile) for ci in range(nct) if cts[ci] != "v"]

        def stats(gi):
            j, ci = groups[gi]
            a_ = NT * j
            rr = raw[ci].ap[:, a_:a_ + NT]
            rsb = rs_l[gi % len(rs_l)]
            sqb = sq[gi % 2]
            self.act(sqb.ap[:, 0:NT], rr, AF.Square, [raw[ci]], [sqb])
            ps = self.PS[2]
            self.mm(ps.ap[:, 0:NT], self.Ones_r.ap if sq.r else self.Ones, sqb.ap[:, 0:NT], True, True,
                    [sqb, self.Ones_r if sq.r else self.cm], [ps])
            self.act(rsb.ap[:, 0:NT], ps.ap[:, 0:NT], AF.Sqrt, [ps], [rsb], scale=1.0, bias=EPS)
            self.recip(rsb.ap[:, 0:NT], rsb.ap[:, 0:NT], [rsb], [rsb])

        def apply(gi):
            j, ci = groups[gi]
            a_ = NT * j
            rr = raw[ci].ap[:, a_:a_ + NT]
            rsb = rs_l[gi % len(rs_l)]
            if rope_off is not None and (gi == 0 or groups[gi - 1][0] != j):
                self.dma(rct.ap[:, 0:NT], self.din["ropeC"][:, rope_off + a_:rope_off + a_ + NT], w=[rct])
                self.dma(rst.ap[:, 0:NT], self.din["ropeS"][:, rope_off + a_:rope_off + a_ + NT], w=[rst])
            if cts[ci] == "q":
                self.stt(rr, rr, 128.0 ** -0.5, rsb.ap[:, 0:NT], ALU.mult, ALU.mult, [raw[ci], rsb], [raw[ci]])
            else:
                self.tt(rr, rr, rsb.ap[:, 0:NT], ALU.mult, [raw[ci], rsb], [raw[ci]])
            if rope_off is not None:
                ps2 = self.nps()
                self.mm(ps2.ap[:, 0:NT], self.prot.ap, rr, True, True, [raw[ci], self.prot], [ps2])
                t = tmp[ci % 2]
                self.tt(t.ap[:, 0:NT], ps2.ap[:, 0:NT], rst.ap[:, 0:NT], ALU.mult, [ps2, rst], [t])
                self.tt(rr, rr, rct.ap[:, 0:NT], ALU.mult, [raw[ci], rct], [raw[ci]], eng="pool")
                self.tt(rr, rr, t.ap[:, 0:NT], ALU.add, [raw[ci], t], [raw[ci]])

        if len(rs_l) >= 2 and groups:
            stats(0)
            for gi in range(len(groups)):
                if gi + 1 < len(groups):
                    stats(gi + 1)
                apply(gi)
        else:
            for gi in range(len(groups)):
                stats(gi)
                apply(gi)

    def gate_math(self, graw, nch, ng, Al_ap, dt_ap, nA_ap, pbufs, beta, g, wk):
        ngh = ng // 2
        g3 = graw.ap[:, 0:nch * ng].rearrange("p (c g) -> p c g", g=ng)
        b3 = g3[:, :, 0:ngh]
        a3 = g3[:, :, ngh:ng]
        def v3(buf):
            return buf.ap[:, 0:nch * ngh].rearrange("p (c g) -> p c g", g=ngh)
        be, gg, e, u = v3(beta), v3(g), v3(wk[0]), v3(wk[1])
        dt3 = dt_ap.rearrange("p (c g) -> p c g", g=ngh)
        nA3 = nA_ap.rearrange("p (c g) -> p c g", g=ngh)
        self.act(be, b3, AF.Exp, [graw], [beta], scale=-1.0)
        self.ts(be, be, 1.0, None, ALU.add, None, [beta], [beta])
        self.recip(be, be, [beta], [beta])
        self.tt(e, a3, dt3, ALU.add, [graw] + pbufs, [wk[0]])
        self.act(e, e, AF.Exp, [wk[0]], [wk[0]])
        self.ts(u, e, 1.0, None, ALU.add, None, [wk[0]], [wk[1]])
        self.act(gg, u, AF.Ln, [wk[1]], [g])
        self.ts(u, u, -1.0, 1e-30, ALU.add, ALU.max, [wk[1]], [wk[1]])
        self.recip(u, u, [wk[1]], [wk[1]])
        self.tt(gg, gg, e, ALU.mult, [g, wk[0]], [g])
        self.tt(gg, gg, u, ALU.mult, [g, wk[1]], [g])
        self.tt(gg, gg, nA3, ALU.mult, [g] + pbufs, [g])

    def new_chain(self, idx, name):
        cx = Chain()
        A = self.arena
        cx.ps = self.cps[idx]
        cx.idx = idx
        cx.psi = 0
        cx.S = A.alloc(128, name + "S")
        cx.w = {k: A.alloc(128, name + k) for k in
                ["gB", "X", "A0", "A1", "R", "kbg", "kdec", "vb", "u", "wT", "vn", "qd", "X2", "qkT"]}
        cx.w["BR0"] = A.alloc(256, name + "BR0")
        cx.w["BR1"] = A.alloc(256, name + "BR1")
        cx.c = A.alloc(8, name + "cols")
        return cx

    def cps_next(self, cx):
        cx.psi = (cx.psi + 1) % 4
        return cx.ps[cx.psi]

    def gdn_chunk(self, cx, kT, vT, qT, srcbufs, bcol, gcol, gbufs, fwd, out_ap=None, out_buf=None):
        W = cx.w
        cm = self.cm
        if fwd:
            Ud, Ms, MiT, last = self.Uinc, self.Lst, self.Uinc, 127
        else:
            Ud, Ms, MiT, last = self.Linc, self.Ust, self.Linc, 0
        gcc, glc, gll, ekd, egc, bg = [cx.c.ap[:, i:i + 1] for i in range(6)]
        C = cx.c
        P = cx.ps
        PB = P[0]
        bank = self.PS[4 + cx.idx]
        p01 = bank.ap[:, 0:256]
        BR = [W["BR0"], W["BR1"]]
        AA = [W["A0"], W["A1"]]
        self.ts(W["gB"].ap, self.Ones, gcol, None, ALU.mult, None, [cm] + gbufs, [W["gB"]], eng="pool")
        pgc = P[2]
        self.mm(pgc.ap, W["gB"].ap, Ud, True, True, [W["gB"], cm], [PB])
        yield
        self.tt(W["X2"].ap, pgc.ap, self.Id, ALU.mult, [PB, cm], [W["X2"]])
        self.S.add("dve", lambda E, o=gcc, i=W["X2"].ap: E.reduce_sum(out=o, in_=i, axis=mybir.AxisListType.X),
                   [W["X2"]], [C])
        self.act(gll, pgc.ap[:, last:last + 1], AF.Copy, [PB], [C])
        self.act(glc, pgc.ap[:, last:last + 1], AF.Exp, [PB], [C])
        self.act(ekd, gcc, AF.Exp, [C], [C], scale=-1.0, bias=gll)
        self.act(egc, gcc, AF.Exp, [C], [C])
        self.tt(bg, egc, bcol, ALU.mult, [C] + gbufs, [C])
        self.ts(W["X"].ap, pgc.ap, gcc, 0.0, ALU.subtract, ALU.max, [PB, C], [W["X"]])
        self.act(W["X"].ap, W["X"].ap, AF.Exp, [W["X"]], [W["X"]], scale=-1.0)
        if qT is not None:
            self.act(W["qd"].ap, pgc.ap, AF.Exp, [PB], [W["qd"]])
            self.tt(W["qd"].ap, W["qd"].ap, qT, ALU.mult, [W["qd"]] + srcbufs, [W["qd"]], eng="pool")
            self.ts(W["X2"].ap, pgc.ap, gcc, 0.0, ALU.subtract, ALU.min, [PB, C], [W["X2"]])
            self.act(W["X2"].ap, W["X2"].ap, AF.Exp, [W["X2"]], [W["X2"]])
            self.tt(W["X2"].ap, W["X2"].ap, MiT, ALU.mult, [W["X2"], cm], [W["X2"]], eng="pool")
            pqk = P[0]
            self.mm(pqk.ap, kT, qT, True, True, srcbufs, [PB])
            yield
            self.tt(W["qkT"].ap, pqk.ap, W["X2"].ap, ALU.mult, [PB, W["X2"]], [W["qkT"]])
        pkk = P[1]
        self.mm(pkk.ap, kT, kT, True, True, srcbufs, [PB])
        yield
        self.tt(AA[0].ap, pkk.ap, W["X"].ap, ALU.mult, [PB, W["X"]], [AA[0]])
        self.stt(AA[0].ap, AA[0].ap, bcol, Ms, ALU.mult, ALU.mult, [AA[0], cm] + gbufs, [AA[0]])
        pt = P[0]
        self.tr(pt.ap, AA[0].ap, [AA[0]], [PB])
        yield
        self.act(BR[0].ap[:, 0:128], pt.ap, AF.Copy, [PB], [BR[0]])
        self.tt(BR[1].ap[:, 128:256], self.Id, pt.ap, ALU.subtract, [cm, PB], [BR[1]])
        self.mm(P[0].ap, AA[0].ap, BR[0].ap[:, 0:128], True, True, [AA[0], BR[0]], [PB])
        yield
        self.act(BR[1].ap[:, 0:128], P[0].ap, AF.Copy, [PB], [BR[1]])
        self.mm(P[3].ap, BR[0].ap[:, 0:128], AA[0].ap, True, True, [AA[0], BR[0]], [PB])
        yield
        self.cp(AA[1].ap, P[3].ap, [PB], [AA[1]])
        for m in range(1, 6):
            cur, nxt = BR[m % 2], BR[(m + 1) % 2]
            Am, An = AA[m % 2], AA[(m + 1) % 2]
            if m < 5:
                self.mm(p01, Am.ap, cur.ap[:, 0:256], True, True, [Am, cur], [PB])
                yield
                self.act(nxt.ap[:, 0:128], bank.ap[:, 0:128], AF.Copy, [PB], [nxt])
                self.tt(nxt.ap[:, 128:256], cur.ap[:, 128:256], bank.ap[:, 128:256], ALU.add, [cur, PB], [nxt])
            else:
                self.mm(bank.ap[:, 128:256], Am.ap, cur.ap[:, 128:256], True, True, [Am, cur], [PB])
                yield
                self.tt(nxt.ap[:, 128:256], cur.ap[:, 128:256], bank.ap[:, 128:256], ALU.add, [cur, PB], [nxt])
            self.mm(P[3].ap, cur.ap[:, 0:128], Am.ap, True, True, [Am, cur], [PB])
            yield
            if m % 2 == 0:
                self.act(An.ap, P[3].ap, AF.Copy, [PB], [An])
            else:
                self.cp(An.ap, P[3].ap, [PB], [An])
        R5 = BR[0]
        self.mm(P[1].ap, AA[0].ap, R5.ap[:, 128:256], True, True, [AA[0], R5], [PB])
        yield
        R = W["R"]
        self.tt(R.ap, R5.ap[:, 128:256], P[1].ap, ALU.add, [R5, PB], [R])
        pk = P[2]
        self.tr(pk.ap, kT, srcbufs, [PB])
        yield
        self.act(W["kbg"].ap, pk.ap, AF.Copy, [PB, C], [W["kbg"]], scale=bg)
        self.ts(W["kdec"].ap, pk.ap, ekd, None, ALU.mult, None, [PB, C], [W["kdec"]])
        pv = P[3]
        self.tr(pv.ap, vT, srcbufs, [PB])
        yield
        self.act(W["vb"].ap, pv.ap, AF.Copy, [PB] + gbufs, [W["vb"]], scale=bcol)
        self.mm(P[0].ap, R.ap, W["vb"].ap, True, True, [R, W["vb"]], [PB])
        yield
        self.act(W["u"].ap, P[0].ap, AF.Copy, [PB], [W["u"]])
        self.mm(P[1].ap, W["kbg"].ap, R.ap, True, True, [R, W["kbg"]], [PB])
        yield
        self.cp(W["wT"].ap, P[1].ap, [PB], [W["wT"]])
        self.mm(P[2].ap, W["wT"].ap, cx.S.ap, True, True, [W["wT"], cx.S], [PB])
        yield
        self.tt(W["vn"].ap, W["u"].ap, P[2].ap, ALU.subtract, [W["u"], PB], [W["vn"]])
        if qT is not None:
            po = P[3]
            self.mm(po.ap, cx.S.ap, W["qd"].ap, True, False, [cx.S, W["qd"]], [PB])
            self.mm(po.ap, W["vn"].ap, W["qkT"].ap, False, True, [W["vn"], W["qkT"]], [PB])
            yield
            self.tt(out_ap, out_ap, po.ap, ALU.add, [out_buf, PB], [out_buf])
        self.mm(P[0].ap, W["kdec"].ap, W["vn"].ap, True, True, [W["kdec"], W["vn"]], [PB])
        yield
        self.stt(cx.S.ap, cx.S.ap, glc, P[0].ap, ALU.mult, ALU.add, [cx.S, C, PB], [cx.S])

    def run_rr(self, gens):
        gens = list(gens)
        while gens:
            for g in list(gens):
                try:
                    next(g)
                except StopIteration:
                    gens.remove(g)

    def load_w(self, buf, col0, ncol, r=False, src="w_in"):
        for kc in range(8):
            srcap = self.din[src][kc * 128:(kc + 1) * 128, col0:col0 + ncol]
            if not r:
                self.dma(buf.ap[:, kc * ncol:(kc + 1) * ncol], srcap, uw=[buf])
            else:
                self.stage_i = (self.stage_i + 1) % 2
                stg = self.stage[self.stage_i]
                self.dma(stg.ap[:, 0:ncol], srcap, w=[stg])
                if self.stage_i == 0:
                    self.act(buf.ap[:, kc * ncol:(kc + 1) * ncol], stg.ap[:, 0:ncol], AF.Copy, [stg], [buf])
                else:
                    self.cp(buf.ap[:, kc * ncol:(kc + 1) * ncol], stg.ap[:, 0:ncol], [stg], [buf])

    def wres(self, buf, ncol, ci):
        return (lambda kc: buf.ap[:, kc * ncol + ci * 128: kc * ncol + (ci + 1) * 128]), [buf]

    def phase_ctx(self):
        P = self.parena
        din = self.din
        self.kcT = [self.prarena.alloc(256, f"kcT{j}") for j in range(4)]
        self.vca = [self.prarena.alloc(8 * 128, f"vca{t}") for t in range(2)]
        self.Sfc = [P.alloc(128, f"Sfc{h}") for h in range(4)]
        self.Sbc = [P.alloc(128, f"Sbc{h}") for h in range(4)]
        self.Sf = [P.alloc(128, f"Sf{h}") for h in range(4)]
        self.Sb = [P.alloc(128, f"Sb{h}") for h in range(4)]
        self.omseg = P.alloc(8, "omseg")
        self.ts(self.omseg.ap, self.mseg.ap, -1.0, 1.0, ALU.mult, ALU.add, [self.mseg], [self.omseg])
        self.begin_phase(21000, 22000)
        A = self.arena
        RA = self.rarena
        self.chains = [self.new_chain(i, f"cx{i}") for i in range(4)]
        xsets, sq, rs = self.alloc_xt(512, nbuf=1, r=True)
        wkv = RA.alloc(8 * 1024, "wkv")
        self.load_w(wkv, 2560, 1024, r=True)
        wg = RA.alloc(8 * 16, "wg")
        self.load_w(wg, 4096, 16, r=True)
        wna = RA.alloc(8 * 1024, "wna")
        self.load_w(wna, 512, 1024, r=True)
        raw = [A.alloc(CTXL + 4, f"rawc{i}") for i in range(8)]
        graw = A.alloc(32, "graw")
        beta = A.alloc(16, "beta")
        g = A.alloc(16, "g")
        wk = [A.alloc(16, "gwk0"), A.alloc(16, "gwk1")]
        tmp = [A.alloc(512, "tmp0"), A.alloc(512, "tmp1")]

        def extra(j, xt, NT):
            for jj in range(4):
                def ev(ps, jj=jj):
                    self.act(self.kcT[jj].ap, ps.ap[:, 0:256], AF.Copy, [ps], [self.kcT[jj]])
                self.proj_fm(xt, 256, lambda kc, jj=jj: wna.ap[:, kc * 1024 + jj * 128: kc * 1024 + (jj + 1) * 128], 128,
                             [wna], ev)
            for t in range(2):
                ps = self.nps()
                for kc in range(8):
                    self.mm(ps.ap, xt[kc].ap[:, t * 128:(t + 1) * 128], wna.ap[:, kc * 1024 + 512: kc * 1024 + 1024],
                            kc == 0, kc == 7, [xt[kc], wna], [ps])
                v3 = self.vca[t].ap.rearrange("p (h c) -> p h c", c=128)
                p3 = ps.ap.rearrange("p (h c) -> p h c", c=64)
                self.act(v3[:, :, 0:64], p3, AF.Copy, [ps], [self.vca[t]])
                self.act(v3[:, :, 64:128], p3, AF.Identity, [ps], [self.vca[t]], scale=0.0, bias=1.0)

        self.prep_unit(lambda j, NT: [din["ctxT"][kc, :, :] for kc in range(8)], None, CTXL, 1,
                       ["k"] * 4 + ["v"] * 4, lambda ci: self.wres(wkv, 1024, ci), raw, self.convO, 20, None, None,
                       lambda kc: wg.ap[:, kc * 16:(kc + 1) * 16], [wg], 16, graw, None, xsets, sq, rs, tmp, None, None,
                       extra=extra)
        self.gate_math(graw, 2, 16, None, self.dtO.ap[:, 0:16], self.nAO.ap[:, 0:16], [self.dtO, self.nAO], beta, g, wk)
        self.tap("ctx_beta", beta.ap, [beta], 16)
        self.tap("ctx_g", g.ap, [g], 16)
        self.tap("ctx_k0", raw[0].ap[:, 0:256], [raw[0]], 256)
        self.tap("ctx_v0", raw[4].ap[:, 0:256], [raw[4]], 256)
        for dr in range(2):
            for h in range(4):
                self.memset(self.chains[h].S.ap, 0.0, [self.chains[h].S])
            for c in ([0, 1] if dr == 0 else [1, 0]):
                gens = []
                for h in range(4):
                    idx = c * 8 + dr * 4 + h
                    gens.append(self.gdn_chunk(self.chains[h], raw[h].ap[:, c * 128:(c + 1) * 128],
                                               raw[4 + h].ap[:, c * 128:(c + 1) * 128],
                                               None, [raw[h], raw[4 + h]], beta.ap[:, idx:idx + 1], g.ap[:, idx:idx + 1],
                                               [beta, g], dr == 0))
                self.run_rr(gens)
            dst = self.Sfc if dr == 0 else self.Sbc
            for h in range(4):
                self.cp(dst[h].ap, self.chains[h].S.ap, [self.chains[h].S], [dst[h]], eng="pool")
        self.tap("Sfc0", self.Sfc[0].ap, [self.Sfc[0]], 128)
        self.tap("Sbc0", self.Sbc[0].ap, [self.Sbc[0]], 128)
        self.end_phase()

    def switch(self, j):
        mj = self.mseg.ap[:, j:j + 1]
        oj = self.omseg.ap[:, j:j + 1]
        for h in range(4):
            cx = self.chains[h]
            self.stt(self.Sf[h].ap, cx.S.ap, mj, self.Sf[h].ap, ALU.mult, ALU.add, [cx.S, self.mseg, self.Sf[h]], [self.Sf[h]])
            t = cx.w["vn"]
            self.ts(t.ap, self.Sbc[h].ap, mj, None, ALU.mult, None, [self.Sbc[h], self.mseg], [t], eng="pool")
            self.stt(cx.S.ap, cx.S.ap, oj, t.ap, ALU.mult, ALU.add, [cx.S, self.omseg, t], [cx.S])

    def phase_stream(self):
        din = self.din
        self.begin_phase(25650, 17600)
        A = self.arena
        RA = self.rarena
        self.chains = [self.new_chain(i, f"cs{i}") for i in range(4)]
        xsets, sq, rs = self.alloc_xt(512, nbuf=2, r=True)
        rs = [rs, A.alloc(512, "rs_b")]
        wkv = RA.alloc(8 * 1024, "wkv")
        self.load_w(wkv, 2560, 1024, r=True)
        wgs_f = A.alloc(64, "wgs_f")
        wgs = RA.alloc(64, "wgs")
        convs = A.alloc(40, "convs")
        raw = [A.alloc(SU + 4, f"raws{i}") for i in range(8)]
        graw = A.alloc(64, "graw")
        beta = A.alloc(32, "beta")
        g = A.alloc(32, "g")
        wk = [A.alloc(32, "gwk0"), A.alloc(32, "gwk1")]
        tmp = [Buf(self.stage[0].ap[:, 0:512], "tmp0", key=self.stage[0].key),
               Buf(self.stage[0].ap[:, 512:1024], "tmp1", key=self.stage[0].key)]
        rct = Buf(self.stage[1].ap[:, 0:512], "rct", key=self.stage[1].key)
        rst = Buf(self.stage[1].ap[:, 512:1024], "rst", key=self.stage[1].key)
        for h in range(4):
            self.cp(self.chains[h].S.ap, self.Sfc[h].ap, [self.Sfc[h]], [self.chains[h].S], eng="pool")
            self.memset(self.Sf[h].ap, 0.0, [self.Sf[h]])
        for u in range(NSU):
            if u % 2 == 0:
                self.switch(u // 2)
            for kc in range(8):
                self.dma(wgs_f.ap[:, kc * 8:(kc + 1) * 8], din["w_gs"][u, kc * 128:(kc + 1) * 128, :], uw=[wgs_f])
            self.act(wgs.ap, wgs_f.ap, AF.Copy, [wgs_f], [wgs])
            self.dma(convs.ap, din["convS"][u], w=[convs])
            halo = [(din["xsH"][u, kc, :, 0:2], din["xsH"][u, kc, :, 2:4]) for kc in range(8)]
            self.prep_unit(lambda j, NT, u=u: [din["xsT"][u, kc, :, j * NT:(j + 1) * NT] for kc in range(8)], halo, SU, 0,
                           ["k"] * 4 + ["v"] * 4, lambda ci: self.wres(wkv, 1024, ci), raw, convs, 0,
                           self.hmS.ap[:, u * 4:(u + 1) * 4], self.hmS,
                           lambda kc: wgs.ap[:, kc * 8:(kc + 1) * 8], [wgs], 8, graw, u * SU, xsets, sq, rs, tmp, rct, rst)
            self.gate_math(graw, 8, 8, None, self.dtS.ap[:, u * 32:(u + 1) * 32], self.nAS.ap[:, u * 32:(u + 1) * 32],
                           [self.dtS, self.nAS], beta, g, wk)
            for c in range(8):
                gens = []
                for h in range(4):
                    idx = c * 4 + h
                    gens.append(self.gdn_chunk(self.chains[h], raw[h].ap[:, c * 128:(c + 1) * 128],
                                               raw[4 + h].ap[:, c * 128:(c + 1) * 128], None, [raw[h], raw[4 + h]],
                                               beta.ap[:, idx:idx + 1], g.ap[:, idx:idx + 1], [beta, g], True))
                self.run_rr(gens)
            if u == 0:
                self.tap("s0_k0", raw[0].ap[:, 0:256], [raw[0]], 256)
                self.tap("s0_beta", beta.ap, [beta], 32)
                self.tap("s0_g", g.ap, [g], 32)
                self.tap("s0_S0", self.chains[0].S.ap, [self.chains[0].S], 128)
        self.switch(7)
        for h in range(4):
            self.cp(self.Sb[h].ap, self.chains[h].S.ap, [self.chains[h].S], [self.Sb[h]], eng="pool")
        self.tap("Sf0", self.Sf[0].ap, [self.Sf[0]], 128)
        self.tap("Sb0", self.Sb[0].ap, [self.Sb[0]], 128)
        self.end_phase()

    def phase_own_gdn(self):
        din = self.din
        ROPE_OWN = NSU * SU
        for p in range(2):
            self.begin_phase(43200, 0)
            A = self.arena
            self.chains = [self.new_chain(i, f"co{i}") for i in range(4)]
            xsets, sq, rs = self.alloc_xt(512, nbuf=1)
            wt = [A.alloc(1024, f"wt{i}") for i in range(2)]
            wz = A.alloc(8 * 256, "wz")
            self.load_w(wz, 3584 + 256 * p, 256)
            wg = A.alloc(8 * 16, "wg")
            self.load_w(wg, 4096, 16)
            raw = [A.alloc(SEGT + 4, f"rawo{i}") for i in range(6)]
            oT = [A.alloc(SEGT, f"oT{i}") for i in range(2)]
            gz = [A.alloc(SEGT, f"gz{i}") for i in range(2)]
            graw = A.alloc(256, "graw")
            beta = A.alloc(128, "beta")
            g = A.alloc(128, "g")
            wk = [A.alloc(128, "gwk0"), A.alloc(128, "gwk1")]
            tmp = [A.alloc(512, "tmp0"), A.alloc(512, "tmp1")]
            rct = A.alloc(512, "rct")
            rst = A.alloc(512, "rst")
            cols = [2048 + 128 * (2 * p), 2048 + 128 * (2 * p + 1), 2560 + 128 * (2 * p), 2560 + 128 * (2 * p + 1),
                    3072 + 128 * (2 * p), 3072 + 128 * (2 * p + 1)]
            self.wti = 0

            def wget(ci):
                self.wti = (self.wti + 1) % 2
                b = wt[self.wti]
                self.load_w(b, cols[ci], 128)
                return (lambda kc, b=b: b.ap[:, kc * 128:(kc + 1) * 128]), [b]

            def extra(j, xt, NT):
                for hh in range(2):
                    def ev(ps, hh=hh, j=j):
                        self.act(gz[hh].ap[:, j * NT:(j + 1) * NT], ps.ap[:, 0:NT], AF.Silu, [ps], [gz[hh]])
                    self.proj_fm(xt, NT, lambda kc, hh=hh: wz.ap[:, kc * 256 + hh * 128: kc * 256 + (hh + 1) * 128], 128,
                                 [wz], ev)

            halo = [(din["xwT"][kc, :, 254:256], din["xwT"][kc, :, 2304:2306]) for kc in range(8)]
            convp = A.alloc(30, "convp")
            for ci, base in enumerate([2 * p, 2 * p + 1, 4 + 2 * p, 5 + 2 * p, 8 + 2 * p, 9 + 2 * p]):
                self.cp(convp.ap[:, ci * 5:(ci + 1) * 5], self.convO.ap[:, base * 5:(base + 1) * 5], [self.convO], [convp],
                        eng="pool")
            self.prep_unit(lambda j, NT: [din["xwT"][kc, :, 256 + j * NT:256 + (j + 1) * NT] for kc in range(8)], halo, SEGT, 0,
                           ["q", "q", "k", "k", "v", "v"], wget, raw, convp, 0, self.hmO.ap[:, 0:4], self.hmO,
                           lambda kc: wg.ap[:, kc * 16:(kc + 1) * 16], [wg], 16, graw, ROPE_OWN, xsets, sq, rs, tmp, rct, rst,
                           extra=extra)
            self.gate_math(graw, 16, 16, None, self.dtO.ap, self.nAO.ap, [self.dtO, self.nAO], beta, g, wk)
            if p == 0:
                self.tap("own_q0", raw[0].ap[:, 0:256], [raw[0]], 256)
                self.tap("own_k0", raw[2].ap[:, 0:256], [raw[2]], 256)
            chains = self.chains
            for hh in range(2):
                h = 2 * p + hh
                self.cp(chains[2 * hh].S.ap, self.Sf[h].ap, [self.Sf[h]], [chains[2 * hh].S], eng="pool")
                self.cp(chains[2 * hh + 1].S.ap, self.Sb[h].ap, [self.Sb[h]], [chains[2 * hh + 1].S], eng="pool")
                self.memset(oT[hh].ap, 0.0, [oT[hh]])
            for s in range(16):
                gens = []
                for hh in range(2):
                    h = 2 * p + hh
                    for dr in range(2):
                        c = s if dr == 0 else 15 - s
                        idx = c * 8 + dr * 4 + h
                        sl = slice(c * 128, (c + 1) * 128)
                        gens.append(self.gdn_chunk(chains[2 * hh + dr], raw[2 + hh].ap[:, sl], raw[4 + hh].ap[:, sl],
                                                   raw[hh].ap[:, sl], [raw[hh], raw[2 + hh], raw[4 + hh]],
                                                   beta.ap[:, idx:idx + 1], g.ap[:, idx:idx + 1],
                                                   [beta, g], dr == 0, out_ap=oT[hh].ap[:, sl], out_buf=oT[hh]))
                self.run_rr(gens)
            for hh in range(2):
                h = 2 * p + hh
                for j in range(4):
                    o = oT[hh].ap[:, j * 512:(j + 1) * 512]
                    sqb = sq[j % 2]
                    self.act(sqb.ap, o, AF.Square, [oT[hh]], [sqb])
                    ps = self.PS[2]
                    self.mm(ps.ap, self.Ones, sqb.ap, True, True, [sqb, self.cm], [ps])
                    self.act(rs.ap, ps.ap, AF.Sqrt, [ps], [rs], scale=1.0 / 128, bias=EPS)
                    self.recip(rs.ap, rs.ap, [rs], [rs])
                    self.stt(o, o, self.gnw.ap[:, 0:1], rs.ap, ALU.mult, ALU.mult, [oT[hh], self.gnw, rs], [oT[hh]])
                    self.tt(o, o, gz[hh].ap[:, j * 512:(j + 1) * 512], ALU.mult, [oT[hh], gz[hh]], [oT[hh]], eng="pool")
                ob = Buf(None, "scrw")
                self.dma(self.scr[4 + h], oT[hh].ap, r=[oT[hh]], w=[ob])
                self.scr_bufs.append(ob)
                if h == 0:
                    self.tap("og0", oT[0].ap[:, 0:256], [oT[0]], 256)
            self.end_phase()

    def phase_natten(self):
        din = self.din
        psw = [(self.PS[2], self.PS[3], self.PSWt[1][:, :]), (self.PS[4], self.PS[5], self.PSWt[2][:, :])]
        for p in range(4):
            self.begin_phase(17000, 25300)
            A = self.arena
            RA = self.rarena
            xsets, sq, rs = self.alloc_xt(512, nbuf=2, r=True)
            wq = RA.alloc(1024, "wq"); wk_ = RA.alloc(1024, "wk"); wv = RA.alloc(1024, "wv"); wz = RA.alloc(1024, "wz")
            self.load_w(wq, 128 * p, 128, r=True)
            self.load_w(wk_, 512 + 128 * p, 128, r=True)
            self.load_w(wv, 1024 + 128 * p, 128, r=True)
            self.load_w(wz, 1536 + 128 * p, 128, r=True)
            kT = RA.alloc(WIN, "kT")
            qT = RA.alloc(SEGT, "qT")
            zT = A.alloc(SEGT, "zT")
            on = zT
            otmp = [A.alloc(128, f"otmp{i}") for i in range(2)]
            va = [RA.alloc(256, f"va{t}") for t in range(20)]
            bias = [A.alloc(3840, f"nab{hh}") for hh in range(2)]
            Eb = [RA.alloc(1024, f"Eb{i}") for i in range(2)]
            rc = [A.alloc(128, f"rc{i}") for i in range(2)]
            for hh in range(2):
                self.dma(bias[hh].ap, din["nab"][2 * p + hh], w=[bias[hh]])
            cur = xsets[0]
            self.gen_h([din["xwT"][kc, :, 0:512] for kc in range(8)], 512, 0, cur, sq, rs)
            for j in range(5):
                xt = cur
                if j + 1 < 5:
                    cur = xsets[(j + 1) % 2]
                    self.gen_h([din["xwT"][kc, :, (j + 1) * 512:(j + 2) * 512] for kc in range(8)], 512, 0, cur, sq, rs)
                def evk(ps, j=j):
                    self.act(kT.ap[:, j * 512:(j + 1) * 512], ps.ap, AF.Copy, [ps], [kT])
                self.proj_fm(xt, 512, lambda kc: wk_.ap[:, kc * 128:(kc + 1) * 128], 128, [wk_], evk)
                for t in range(4):
                    ps = self.nps()
                    for kc in range(8):
                        self.mm(ps.ap[:, 0:128], xt[kc].ap[:, t * 128:(t + 1) * 128], wv.ap[:, kc * 128:(kc + 1) * 128],
                                kc == 0, kc == 7, [xt[kc], wv], [ps])
                    vb = va[4 * j + t]
                    v3 = vb.ap.rearrange("p (h c) -> p h c", c=128)
                    p3 = ps.ap[:, 0:128].rearrange("p (h c) -> p h c", c=64)
                    self.act(v3[:, :, 0:64], p3, AF.Copy, [ps], [vb])
                    self.act(v3[:, :, 64:128], p3, AF.Identity, [ps], [vb], scale=0.0, bias=1.0)
                lo = max(j * 512, 256)
                hi = min((j + 1) * 512, 2304)
                n = hi - lo
                def evq(ps, lo=lo, n=n):
                    self.act(qT.ap[:, lo - 256:lo - 256 + n], ps.ap[:, 0:n], AF.Copy, [ps], [qT])
                self.proj_fm(xt, n, lambda kc: wq.ap[:, kc * 128:(kc + 1) * 128], 128, [wq], evq, c0=lo - j * 512)
                def evz(ps, lo=lo, n=n):
                    self.act(zT.ap[:, lo - 256:lo - 256 + n], ps.ap[:, 0:n], AF.Silu, [ps], [zT])
                self.proj_fm(xt, n, lambda kc: wz.ap[:, kc * 128:(kc + 1) * 128], 128, [wz], evz, c0=lo - j * 512)
            iters = [(gq, hh) for gq in range(16) for hh in range(2)]

            def emit_scores(it):
                gq, hh = iters[it]
                kt0 = min(gq, 14)
                hp = 64 * hh
                pa, pb, pw = psw[it % 2]
                qs = qT.ap[hp:hp + 64, gq * 128:(gq + 1) * 128]
                for t in range(6):
                    self.mm(pw[:, t * 128:(t + 1) * 128], kT.ap[hp:hp + 64, (kt0 + t) * 128:(kt0 + t + 1) * 128], qs,
                            True, True, [kT, qT], [pa, pb])
                for t in range(2):
                    self.mm(pw[:, (6 + t) * 128:(7 + t) * 128], self.kcT[p].ap[hp:hp + 64, t * 128:(t + 1) * 128], qs,
                            True, True, [self.kcT[p], qT], [pa, pb])

            pos = {}

            def emit_mid(it):
                gq, hh = iters[it]
                kt0 = min(gq, 14)
                cls = {0: 0, 1: 1, 14: 3, 15: 4}.get(gq, 2)
                pa, pb, pw = psw[it % 2]
                eb = Eb[it % 2]
                self.act(eb.ap[:, 768:1024], pw[:, 768:1024], AF.Exp, [pa, pb], [eb], scale=0.125)
                self.stt(eb.ap[:, 0:768], pw[:, 0:768], 0.125, bias[hh].ap[:, cls * 768:(cls + 1) * 768], ALU.mult, ALU.add,
                         [pa, pb, bias[hh]], [eb])
                self.act(eb.ap[:, 0:768], eb.ap[:, 0:768], AF.Exp, [eb], [eb])
                po = self.nps()
                for t in range(6):
                    self.mm(po.ap[:, 0:128], va[kt0 + t].ap[:, hh * 128:(hh + 1) * 128], eb.ap[:, t * 128:(t + 1) * 128],
                            t == 0, False, [va[kt0 + t], eb], [po])
                for t in range(2):
                    h = 2 * p + hh
                    self.mm(po.ap[:, 0:128], self.vca[t].ap[:, h * 128:(h + 1) * 128], eb.ap[:, (6 + t) * 128:(7 + t) * 128],
                            False, t == 1, [self.vca[t], eb], [po])
                pos[it] = po

            def emit_epi(it):
                gq, hh = iters[it]
                hp = 64 * hh
                po = pos.pop(it)
                rcb = rc[it % 2]
                ot = otmp[it % 2]
                self.recip(rcb.ap[0:64, :], po.ap[64:128, 0:128], [po], [rcb])
                self.tt(ot.ap[hp:hp + 64, :], po.ap[0:64, 0:128], rcb.ap[0:64, :], ALU.mult, [po, rcb], [ot])
                zs = zT.ap[hp:hp + 64, gq * 128:(gq + 1) * 128]
                self.tt(zs, zs, ot.ap[hp:hp + 64, :], ALU.mult, [zT, ot], [zT])

            emit_scores(0)
            for it in range(len(iters)):
                if it + 1 < len(iters):
                    emit_scores(it + 1)
                emit_mid(it)
                if it >= 1:
                    emit_epi(it - 1)
            emit_epi(len(iters) - 1)
            if p == 0:
                self.tap("na_o0", on.ap[:, 0:256], [on], 256)
            ob = Buf(None, "scrw")
            self.dma(self.scr[p], on.ap, r=[on], w=[ob])
            self.scr_bufs.append(ob)
            self.end_phase()

    def phase_final(self):
        din = self.din
        self.begin_phase(16500, 16600)
        A = self.arena
        RA = self.rarena
        wo = RA.alloc(8 * 1024, "wo")
        self.load_w(wo, 0, 1024, r=True, src="w_out")
        gtf = [[A.alloc(512, f"gtf{i}_{kc}") for kc in range(8)] for i in range(2)]
        gt = [[RA.alloc(512, f"gt{i}_{kc}") for kc in range(8)] for i in range(2)]
        ysb = [A.alloc(1024, f"ysb{i}") for i in range(2)]
        ysq = A.alloc(1024, "ysq")
        xo = [A.alloc(1024, f"xo{i}") for i in range(2)]
        ss = A.alloc(4, "ss")
        for jt in range(4):
            g8 = gt[jt % 2]
            g8f = gtf[jt % 2]
            for kc in range(8):
                self.dma(g8f[kc].ap, self.scr[kc, :, jt * 512:(jt + 1) * 512], r=self.scr_bufs, w=[g8f[kc]])
                if kc % 2 == 0:
                    self.act(g8[kc].ap, g8f[kc].ap, AF.Copy, [g8f[kc]], [g8[kc]])
                else:
                    self.cp(g8[kc].ap, g8f[kc].ap, [g8f[kc]], [g8[kc]])
            for s in range(4):
                t = jt * 4 + s
                y = ysb[t % 2]
                x_ = xo[t % 2]
                self.dma(x_.ap, din["xown"][t * 128:(t + 1) * 128, :], w=[x_])
                for n in range(2):
                    ps = self.nps()
                    for kc in range(8):
                        self.mm(ps.ap, g8[kc].ap[:, s * 128:(s + 1) * 128], wo.ap[:, kc * 1024 + n * 512: kc * 1024 + (n + 1) * 512],
                                kc == 0, kc == 7, [g8[kc], wo], [ps])
                    self.act(y.ap[:, n * 512:(n + 1) * 512], ps.ap, AF.Copy, [ps], [y])
                self.act(ysq.ap, y.ap, AF.Square, [y], [ysq])
                self.S.add("dve", lambda E, o=ss.ap[:, 0:1], i=ysq.ap: E.reduce_sum(out=o, in_=i, axis=mybir.AxisListType.X),
                           [ysq], [ss])
                self.act(ss.ap[:, 1:2], ss.ap[:, 0:1], AF.Sqrt, [ss], [ss], scale=1.0 / D, bias=EPS)
                self.recip(ss.ap[:, 2:3], ss.ap[:, 1:2], [ss], [ss])
                self.stt(y.ap, y.ap, ss.ap[:, 2:3], self.G2.ap, ALU.mult, ALU.mult, [y, ss, self.G2], [y])
                self.tt(y.ap, y.ap, x_.ap, ALU.add, [y, x_], [y], eng="pool")
                ob = Buf(None, "outw")
                self.dma(self.out[t * 128:(t + 1) * 128, :], y.ap, r=[y], w=[ob])
                self.outs.append(ob)
        self.phase_stack.close()


def build_program(stop="full", dbg=False):
    b = Builder(stop=stop, dbg=dbg)
    b.scr_bufs = []
    nc = b.build()
    return nc, b


_CACHE = {}


def kernel(**inputs):
    per_core = host_prep(inputs)
    nc, b = build_program()
    res = run_bass_kernel_spmd(nc, per_core, core_ids=list(range(NCORES)))
    out = np.concatenate([np.asarray(r["out"], np.float32) for r in res.results], 0)
    return out.reshape(1, SEQ, D).astype(np.float32)
```
